# Optimizing a Trainium2 kernel written in Bass

```python
import jax, jax.numpy as jnp
from jax import lax
import numpy as np

D_MODEL = 1024
BATCH = 8
SEQ = 2048
DEPTH = 4
DEC_BATCH = 128
DEC_SEQ = 8
PAST_LEN = 16384
PAGE_SIZE = 128

D_A = D_MODEL // 2
D_B = D_MODEL // 2
CONV_A = 31
CONV_B = 3
D_IN_EVEN = 2 * D_A + 3 * D_B
D_C = D_MODEL
C_HEADS = 8
C_HEAD_DIM = D_C // C_HEADS
CHUNK = 128
D_FF = 4 * D_MODEL
N_EVEN = (DEPTH + 1) // 2
N_ODD = DEPTH // 2
N_MOD = 6
EPS = 1e-6

kernel_name = 'hybrid_conformerconv_shortconv_gmlp_decoder_step'


def rmsnorm(x, g):
    xf = x.astype(jnp.float32)
    y = xf * lax.rsqrt(jnp.mean(xf * xf, axis=-1, keepdims=True) + EPS)
    return (y * g.astype(jnp.float32)).astype(x.dtype)


def layernorm(x, g, b):
    xf = x.astype(jnp.float32)
    mu = jnp.mean(xf, axis=-1, keepdims=True)
    var = jnp.mean(jnp.square(xf - mu), axis=-1, keepdims=True)
    y = (xf - mu) * lax.rsqrt(var + EPS) * g.astype(jnp.float32) + b.astype(jnp.float32)
    return y.astype(x.dtype)


def modulate(x, shift, scale):
    return x * (1 + scale[:, None, :]) + shift[:, None, :]


def causal_dwconv(x_ext, w):
    ch = x_ext.shape[-1]
    return lax.conv_general_dilated(
        x_ext, w[:, None, :].astype(x_ext.dtype), window_strides=(1,), padding='VALID',
        dimension_numbers=('NWC', 'WIO', 'NWC'), feature_group_count=ch)


def even_mixer(h, hist_a, hist_b, w_in, conv_a_w, conv_a_b, ln_a_g, ln_a_b, conv_b_w, w_out):
    z = h @ w_in
    a_val, a_gate, b_x, b_b, b_c = jnp.split(
        z, [D_A, 2 * D_A, 2 * D_A + D_B, 2 * D_A + 2 * D_B], axis=-1)
    a = a_val * jax.nn.sigmoid(a_gate)
    a_ext = jnp.concatenate([hist_a, a], axis=1)
    a = causal_dwconv(a_ext, conv_a_w) + conv_a_b
    a = jax.nn.silu(layernorm(a, ln_a_g, ln_a_b))
    bx = b_c * b_x
    b_ext = jnp.concatenate([hist_b, bx], axis=1)
    b = b_b * causal_dwconv(b_ext, conv_b_w)
    y = jnp.concatenate([a, b], axis=-1) @ w_out
    return y, a_ext[:, -(CONV_A - 1):], b_ext[:, -(CONV_B - 1):]


def odd_mixer(h, w_in, b_in, ln_v_g, ln_v_b, w_s, b_s, w_out):
    n, length, _ = h.shape
    z = jax.nn.gelu(h @ w_in + b_in)
    u, v = jnp.split(z, 2, axis=-1)
    v = layernorm(v, ln_v_g, ln_v_b)
    t = min(length, CHUNK)
    n_chunks = length // t
    mask = jnp.tril(jnp.ones((t, t), dtype=bool))
    ws = jnp.where(mask, w_s[:, :t, :t], 0).astype(v.dtype)
    vc = v.reshape(n, n_chunks, t, C_HEADS, C_HEAD_DIM)
    s = jnp.einsum('hts,bnshd->bnthd', ws, vc) + b_s[:, :t].T[None, None, :, :, None]
    s = s.reshape(n, length, D_C)
    y = (u * s) @ w_out
    return y, v[:, -t:]


def trunk(x, c, hist_a, hist_b, w_in_ab, conv_a_w, conv_a_b, ln_a_g, ln_a_b, conv_b_w, w_out_ab,
          w_in_c, b_in_c, ln_v_g, ln_v_b, w_s, b_s, w_out_c, w_ada, b_ada, norm_g, w_ff1, w_ff2, final_g):
    n = x.shape[0]
    c_act = jax.nn.silu(c)
    new_a, new_b, new_v = [], [], []
    for l in range(DEPTH):
        mod = (c_act @ w_ada[l] + b_ada[l]).reshape(n, N_MOD, D_MODEL)
        sh1, sc1, g1, sh2, sc2, g2 = (mod[:, i] for i in range(N_MOD))
        h = modulate(rmsnorm(x, norm_g[l, 0]), sh1, sc1)
        if l % 2 == 0:
            e = l // 2
            y, sa, sb = even_mixer(h, hist_a[e], hist_b[e], w_in_ab[e], conv_a_w[e], conv_a_b[e],
                                   ln_a_g[e], ln_a_b[e], conv_b_w[e], w_out_ab[e])
            new_a.append(sa)
            new_b.append(sb)
        else:
            o = l // 2
            y, sv = odd_mixer(h, w_in_c[o], b_in_c[o], ln_v_g[o], ln_v_b[o], w_s[o], b_s[o], w_out_c[o])
            new_v.append(sv)
        x = x + g1[:, None, :] * y
        h = modulate(rmsnorm(x, norm_g[l, 1]), sh2, sc2)
        x = x + g2[:, None, :] * (jnp.square(jax.nn.relu(h @ w_ff1[l])) @ w_ff2[l])
    x = rmsnorm(x, final_g)
    return x, jnp.stack(new_a), jnp.stack(new_b), jnp.stack(new_v)


def setup_inputs(seed: int = 0) -> dict:
    key = jax.random.key(seed)
    ks = iter(list(jax.random.split(key, 32)))

    def nrm(shape, scale):
        return jax.random.normal(next(ks), shape, jnp.float32) * scale

    d = D_MODEL
    return {
        'x_prompt': nrm((BATCH, SEQ, d), 1.0),
        'x_sample': nrm((DEC_BATCH, DEC_SEQ, d), 1.0),
        'state_conv_a': nrm((N_EVEN, DEC_BATCH, CONV_A - 1, D_A), 0.5),
        'state_conv_b': nrm((N_EVEN, DEC_BATCH, CONV_B - 1, D_B), 0.5),
        'c_prompt': nrm((BATCH, d), 1.0),
        'c_sample': nrm((DEC_BATCH, d), 1.0),
        'w_in_ab': nrm((N_EVEN, d, D_IN_EVEN), d ** -0.5),
        'conv_a_w': nrm((N_EVEN, CONV_A, D_A), CONV_A ** -0.5),
        'conv_a_b': nrm((N_EVEN, D_A), 0.02),
        'ln_a_g': 1.0 + nrm((N_EVEN, D_A), 0.02),
        'ln_a_b': nrm((N_EVEN, D_A), 0.02),
        'conv_b_w': nrm((N_EVEN, CONV_B, D_B), CONV_B ** -0.5),
        'w_out_ab': nrm((N_EVEN, D_A + D_B, d), (D_A + D_B) ** -0.5),
        'w_in_c': nrm((N_ODD, d, 2 * D_C), d ** -0.5),
        'b_in_c': nrm((N_ODD, 2 * D_C), 0.02),
        'ln_v_g': 1.0 + nrm((N_ODD, D_C), 0.02),
        'ln_v_b': nrm((N_ODD, D_C), 0.02),
        'w_s': nrm((N_ODD, C_HEADS, CHUNK, CHUNK), CHUNK ** -0.5),
        'b_s': 1.0 + nrm((N_ODD, C_HEADS, CHUNK), 0.02),
        'w_out_c': nrm((N_ODD, D_C, d), D_C ** -0.5),
        'w_ada': nrm((DEPTH, d, N_MOD * d), 0.3 * d ** -0.5),
        'b_ada': nrm((DEPTH, N_MOD * d), 0.05),
        'norm_g': 1.0 + nrm((DEPTH, 2, d), 0.02),
        'w_ff1': nrm((DEPTH, d, D_FF), d ** -0.5),
        'w_ff2': nrm((DEPTH, D_FF, d), D_FF ** -0.5),
        'final_g': 1.0 + nrm((d,), 0.02),
    }


def reference(x_prompt, x_sample, state_conv_a, state_conv_b, c_prompt, c_sample,
              w_in_ab, conv_a_w, conv_a_b, ln_a_g, ln_a_b, conv_b_w, w_out_ab,
              w_in_c, b_in_c, ln_v_g, ln_v_b, w_s, b_s, w_out_c,
              w_ada, b_ada, norm_g, w_ff1, w_ff2, final_g):
    weights = (w_in_ab, conv_a_w, conv_a_b, ln_a_g, ln_a_b, conv_b_w, w_out_ab,
               w_in_c, b_in_c, ln_v_g, ln_v_b, w_s, b_s, w_out_c,
               w_ada, b_ada, norm_g, w_ff1, w_ff2, final_g)
    hist_a0 = jnp.zeros((N_EVEN, x_prompt.shape[0], CONV_A - 1, D_A), x_prompt.dtype)
    hist_b0 = jnp.zeros((N_EVEN, x_prompt.shape[0], CONV_B - 1, D_B), x_prompt.dtype)
    y_prompt, conv_a_p, conv_b_p, chunk_v_p = trunk(x_prompt, c_prompt, hist_a0, hist_b0, *weights)
    y_sample, conv_a_s, conv_b_s, chunk_v_s = trunk(
        x_sample, c_sample, state_conv_a.astype(x_sample.dtype), state_conv_b.astype(x_sample.dtype), *weights)
    return (y_prompt, y_sample, conv_a_p, conv_a_s, conv_b_p, conv_b_s, chunk_v_p, chunk_v_s)
```

```python
import numpy as np
from contextlib import ExitStack
import concourse.bass as bass
import concourse.mybir as mybir
from concourse.bass_utils import run_bass_kernel_spmd

F32 = mybir.dt.float32
BF16 = mybir.dt.bfloat16
U8 = mybir.dt.uint8
AF = mybir.ActivationFunctionType
ALU = mybir.AluOpType

D = 1024
KC = 8
SEQ = 2048
NS = 16
DSQ = 8
DEPTH = 4
EPS = 1e-6
GRAN = 256
WARM_N = 14
NCORES = 8

R_NG = 0
R_FG = 64
R_BADA = 72
R_CAW = 264
R_CAB = 512
R_LAG = 520
R_LAB = 528
R_CBW = 536
R_BU = 560
R_TOT = 576
NRT = 5


class View:
    __slots__ = ("ap", "keys")

    def __init__(self, ap, keys):
        self.ap = ap
        self.keys = keys

    def r(self, pat, **kw):
        return View(self.ap.rearrange(pat, **kw), self.keys)

    def bc(self, shape):
        return View(self.ap.to_broadcast(list(shape)), self.keys)

    def un(self, ax):
        return View(self.ap.unsqueeze(ax), self.keys)


class Buf:
    def __init__(self, arena, off, shape, dt):
        self.off = off
        self.shape = tuple(shape)
        self.dt = dt
        self.es = 4 if dt == F32 else 2
        n = 1
        for s in shape:
            n *= s
        self.nbytes = n * self.es
        flat = arena[:, off:off + self.nbytes].bitcast(dt)
        if len(shape) == 1:
            self.ap = flat
        elif len(shape) == 2:
            self.ap = flat.rearrange("p (a b) -> p a b", a=shape[0])
        elif len(shape) == 3:
            self.ap = flat.rearrange("p (a b c) -> p a b c", a=shape[0], b=shape[1])
        else:
            raise ValueError
        st = []
        acc = 1
        for s in reversed(self.shape):
            st.append(acc)
            acc *= s
        self.strides = tuple(reversed(st))

    def __getitem__(self, idx):
        if not isinstance(idx, tuple):
            idx = (idx,)
        idx = list(idx) + [slice(None)] * (1 + len(self.shape) - len(idx))
        ap = self.ap[tuple(idx)]
        rng = []
        for d, ix in enumerate(idx[1:]):
            if isinstance(ix, int):
                rng.append((ix, ix + 1))
            else:
                a = 0 if ix.start is None else ix.start
                b = self.shape[d] if ix.stop is None else ix.stop
                assert ix.step in (None, 1)
                rng.append((a, b))
        outer = rng[:-1]
        n_outer = 1
        for a, b in outer:
            n_outer *= (b - a)
        keys = set()
        la, lb = rng[-1]
        if n_outer <= 128:
            def rec(d, base):
                if d == len(outer):
                    lo = self.off + (base + la) * self.es
                    hi = self.off + (base + lb) * self.es
                    for g in range(lo // GRAN, (hi - 1) // GRAN + 1):
                        keys.add(g)
                    return
                for i in range(outer[d][0], outer[d][1]):
                    rec(d + 1, base + i * self.strides[d])
            rec(0, 0)
        else:
            lo = self.off + sum(r[0] * s for r, s in zip(rng, self.strides)) * self.es
            hi = self.off + (sum((r[1] - 1) * s for r, s in zip(rng, self.strides)) + 1) * self.es
            for g in range(lo // GRAN, (hi - 1) // GRAN + 1):
                keys.add(g)
        return View(ap, frozenset(keys))


class PBank:
    def __init__(self, t, i):
        self.t = t
        self.i = i
        self.keys = frozenset([("ps", i)])

    def __getitem__(self, idx):
        return View(self.t[idx], self.keys)


class Prog:
    def __init__(self):
        self.ops = []
        self.last_w = {}
        self.readers = {}
        self.slots = {}

    def add(self, eng, fn, reads=(), writes=(), dma=None, ninc=1):
        idx = len(self.ops)
        deps = {}
        rk = set()
        for r in reads:
            rk |= set(r) if isinstance(r, (set, frozenset)) else {r}
        wk = set()
        for w in writes:
            wk |= set(w) if isinstance(w, (set, frozenset)) else {w}
        wk |= {k for k in rk if isinstance(k, tuple) and k[0] == "ps"}
        lw = self.last_w
        rd = self.readers
        for k in rk:
            w = lw.get(k)
            if w is not None:
                deps[w] = "raw"
        for k in wk:
            w = lw.get(k)
            if w is not None and w not in deps:
                deps[w] = "waw"
            for x in rd.get(k, ()):
                if x not in deps:
                    deps[x] = "war"
        need = []
        for p, kind in deps.items():
            P = self.ops[p]
            if P["dma"] is None and dma is None and P["eng"] == eng:
                if eng == "pe":
                    continue
            need.append(p)
            P["signal"] = True
        op = dict(eng=eng, fn=fn, deps=need, dma=dma, ninc=ninc, signal=False)
        self.ops.append(op)
        for k in wk:
            lw[k] = idx
            rd[k] = []
        for k in rk:
            rd.setdefault(k, []).append(idx)
        return idx

    def emit(self, nc, block, stack):
        ENG = ["pe", "act", "dve", "pool", "sp"]
        esem = {e: stack.enter_context(nc.semaphore("S_" + e)) for e in ENG}
        slotsem = {}
        cnt = {e: 0 for e in ENG}
        scnt = {}
        const_total = 16 * sum(o["ninc"] for o in self.ops if o["dma"] == "const")
        for o in self.ops:
            if o["dma"] is None:
                if o["signal"]:
                    cnt[o["eng"]] += 1
                    o["token"] = (esem[o["eng"]], cnt[o["eng"]])
            else:
                s = o["dma"]
                if s not in slotsem:
                    slotsem[s] = stack.enter_context(nc.semaphore("D_" + s))
                    scnt[s] = 0
                scnt[s] += 16 * o["ninc"]
                o["token"] = (slotsem[s], const_total if s == "const" else scnt[s])
                o["sem"] = slotsem[s]
        per = {e: [o for o in self.ops if o["eng"] == e] for e in ENG}
        ops = self.ops

        def runner(e):
            def f(engobj):
                waited = {}
                for o in per[e]:
                    need = {}
                    for p in o["deps"]:
                        sem, val = ops[p]["token"]
                        if need.get(sem, (None, 0))[1] < val:
                            need[sem] = (sem, val)
                    for sem, val in need.values():
                        if waited.get(sem, 0) < val:
                            engobj.wait_ge(sem, val)
                            waited[sem] = val
                    if o["fn"] is None:
                        continue
                    if o["dma"] is not None:
                        o["fn"](engobj, o["sem"])
                    else:
                        inst = o["fn"](engobj)
                        if o["signal"]:
                            inst.then_inc(esem[e], 1)
            return f
        block.tensor(runner("pe"))
        block.scalar(runner("act"))
        block.vector(runner("dve"))
        block.gpsimd(runner("pool"))
        block.sync(runner("sp"))


def build_program(nlayers=DEPTH, PBT=512, final_norm=True, max_tasks=None):
    nc = bass.Bass("TRN2", target_bir_lowering=False)
    NE = 2
    NO = 2

    def din(name, shape):
        return nc.dram_tensor(name, list(shape), F32, kind="ExternalInput").ap()

    def dout(name, shape):
        return nc.dram_tensor(name, list(shape), F32, kind="ExternalOutput").ap()

    x_p = din("x_p", [SEQ, D])
    x_s = din("x_s", [NS * DSQ, D])
    sa = din("sa", [NE, NS, 30, 512])
    sb = din("sb", [NE, NS, 2, 512])
    c17 = din("c17", [1 + NS, D])
    w_in_ab = din("w_in_ab", [NE, D, 2560])
    conv_a_w = din("conv_a_w", [NE, 31, 512])
    conv_a_b = din("conv_a_b", [NE, 512])
    ln_a_g = din("ln_a_g", [NE, 512])
    ln_a_b = din("ln_a_b", [NE, 512])
    conv_b_w = din("conv_b_w", [NE, 3, 512])
    w_out_ab = din("w_out_ab", [NE, D, D])
    w_in_c = din("w_in_c", [NO, D, 2048])
    b_in_c = din("b_in_c", [NO, 2048])
    ln_v_g = din("ln_v_g", [NO, D])
    ln_v_b = din("ln_v_b", [NO, D])
    w_s = din("w_s", [NO, 8, 128, 128])
    b_s = din("b_s", [NO, 8, 128])
    w_out_c = din("w_out_c", [NO, D, D])
    w_ada = din("w_ada", [DEPTH, D, 6 * D])
    b_ada = din("b_ada", [DEPTH, 6 * D])
    norm_g = din("norm_g", [DEPTH, 2, D])
    w_ff1 = din("w_ff1", [DEPTH, D, 4 * D])
    w_ff2 = din("w_ff2", [DEPTH, 4 * D, D])
    final_g = din("final_g", [D])

    y_p = dout("y_p", [SEQ, D])
    y_s = dout("y_s", [NS * DSQ, D])
    ca_p = dout("ca_p", [NE, 30, 512])
    ca_s = dout("ca_s", [NE, NS, 30, 512])
    cb_p = dout("cb_p", [NE, 2, 512])
    cb_s = dout("cb_s", [NE, NS, 2, 512])
    cv_p = dout("cv_p", [NO, 128, D])
    cv_s = dout("cv_s", [NO, NS * DSQ, D])

    NPB = SEQ // PBT
    TB = PBT + NS * DSQ
    NSEGP = PBT // 512

    stack = ExitStack()
    ARENA = 206 * 1024
    arena = stack.enter_context(nc.sbuf_tensor("arena", [128, ARENA], U8))
    pst = [stack.enter_context(nc.psum_tensor("ps%d" % i, [128, 512], F32)) for i in range(8)]
    PS = [PBank(pst[i], i) for i in range(8)]
    block = stack.enter_context(nc.Block())
    P = Prog()

    astate = {"top": 0}

    def alloc(shape, dt):
        off = (astate["top"] + GRAN - 1) // GRAN * GRAN
        b = Buf(arena, off, shape, dt)
        astate["top"] = off + b.nbytes
        assert astate["top"] <= ARENA, ("arena overflow", astate["top"])
        return b

    def mark():
        return astate["top"]

    def release(m):
        astate["top"] = m

    psn = {"i": 0}

    def psum():
        b = PS[psn["i"] % 5]
        psn["i"] += 1
        return b
    PSN = {"p": PS[6], "s": PS[7]}
    PSW = PS[5]

    def keysof(*vs):
        out = []
        for v in vs:
            if isinstance(v, View):
                out.append(v.keys)
        return out

    def apof(v):
        return v.ap if isinstance(v, View) else v

    def ACT(out, in_, func, bias=0.0, scale=1.0):
        o, i, b, s = out.ap, in_.ap, apof(bias), apof(scale)
        P.add("act", lambda e: e.activation(out=o, in_=i, func=func, bias=b, scale=s),
              reads=keysof(in_, bias, scale), writes=keysof(out))

    def TT(eng, out, in0, in1, op):
        o, a, b = out.ap, in0.ap, in1.ap
        P.add(eng, lambda e: e.tensor_tensor(out=o, in0=a, in1=b, op=op),
              reads=keysof(in0, in1), writes=keysof(out))

    def TS(eng, out, in0, s1, op0, s2=None, op1=None):
        o, a, x1, x2 = out.ap, in0.ap, apof(s1), apof(s2)
        if op1 is None:
            fn = lambda e: e.tensor_scalar(out=o, in0=a, scalar1=x1, scalar2=None, op0=op0)
        else:
            fn = lambda e: e.tensor_scalar(out=o, in0=a, scalar1=x1, scalar2=x2, op0=op0, op1=op1)
        P.add(eng, fn, reads=keysof(in0, s1, s2), writes=keysof(out))

    def STT(out, in0, scalar, in1, op0, op1):
        o, a, s, b = out.ap, in0.ap, apof(scalar), in1.ap
        P.add("dve", lambda e: e.scalar_tensor_tensor(out=o, in0=a, scalar=s, in1=b, op0=op0, op1=op1),
              reads=keysof(in0, scalar, in1), writes=keysof(out))

    def COPY(eng, out, in_):
        o, i = out.ap, in_.ap
        if eng == "act":
            P.add("act", lambda e: e.copy(out=o, in_=i), reads=keysof(in_), writes=keysof(out))
        else:
            P.add(eng, lambda e: e.tensor_copy(out=o, in_=i), reads=keysof(in_), writes=keysof(out))

    def MEMSET(eng, out, val):
        o = out.ap
        P.add(eng, lambda e: e.memset(o, val), writes=keysof(out))

    def RECIP(out, in_):
        o, i = out.ap, in_.ap
        P.add("dve", lambda e: e.reciprocal(out=o, in_=i), reads=keysof(in_), writes=keysof(out))

    def MM(out, pairs, transpose=False):
        o = out.ap
        pl = [(a.ap, b.ap) for a, b in pairs]
        n = len(pl)

        def fn(e):
            inst = None
            for i, (a, b) in enumerate(pl):
                inst = e.matmul(o, a, b, start=(i == 0), stop=(i == n - 1))
            return inst
        rk = []
        for a, b in pairs:
            rk += [a.keys, b.keys]
        P.add("pe", fn, reads=rk, writes=keysof(out))

    def MM1(out, lhsT, rhs, start, stop):
        o, a, b = out.ap, lhsT.ap, rhs.ap
        P.add("pe", lambda e: e.matmul(o, a, b, start=start, stop=stop),
              reads=[lhsT.keys, rhs.keys], writes=keysof(out))

    fresh = set()

    def MMh(out, pairs, fkey):
        if fkey in fresh:
            fresh.discard(fkey)
            n_ = len(pairs)
            for i_, (a_, b_) in enumerate(pairs):
                MM1(out, a_, b_, start=(i_ == 0), stop=(i_ == n_ - 1))
        else:
            MM(out, pairs)

    def TR(out, in_, ident_v):
        o, i, idn = out.ap, in_.ap, ident_v.ap
        P.add("pe", lambda e: e.transpose(o, i, idn), reads=keysof(in_, ident_v), writes=keysof(out))

    outkeys = []
    nout = {"i": 0}

    def DMA(eng, slot, pairs, reads=(), writes=(), is_out=False):
        pl = list(pairs)

        def fn(e, sem):
            for d, s in pl:
                e.dma_start(out=d, in_=s).then_inc(sem, 16)
        w = list(writes)
        if is_out:
            k = ("out", nout["i"])
            nout["i"] += 1
            outkeys.append(k)
            w.append(k)
        P.add(eng, fn, reads=list(reads), writes=w, dma=slot, ninc=len(pl))

    ident = alloc((128,), F32)
    identb = alloc((128,), BF16)
    onesM = alloc((128,), BF16)
    onesA = alloc((128,), BF16)
    colT = alloc((NRT * 128,), F32)
    mods = alloc((DEPTH * 48, 1 + NS), F32)
    cT = alloc((KC, 1 + NS), BF16)
    cwb = alloc((NE * 4, 31), BF16)
    wsT = [alloc((8, 128), BF16) for _ in range(NO)]
    wsTs = [alloc((8, 128), BF16) for _ in range(NO)]
    carry_a = [alloc((4, 30), BF16) for _ in range(NE)]
    carry_b = [alloc((4, 2), F32) for _ in range(NE)]
    xT = alloc((KC, TB), F32)
    h = alloc((KC, TB), BF16)
    mo = alloc((KC, TB), BF16)
    NB = 4
    ring8 = []
    ring32 = []
    for i in range(NB):
        b8 = alloc((8, 512), BF16)
        ring8.append(b8)
        ring32.append(Buf(arena, b8.off, (32, 128), BF16))

    sqs = [alloc((512,), BF16) for _ in range(6)]
    sqc = {"i": 0}
    pending_sq = []

    def sq_accum(sg, k):
        o, n = sg["off"], sg["n"]
        slot = sqs[sqc["i"] % 6]
        sqc["i"] += 1
        ACT(slot[:, 0:n], xT[:, k, o:o + n], AF.Square)
        pending_sq.append((PSN[sg["kind"]][:, 0:n], slot[:, 0:n], k))

    def flush_sq(keep=0):
        while len(pending_sq) > keep:
            pv, sv, k = pending_sq.pop(0)
            MM1(pv, onesM[:, :], sv, start=(k == 0), stop=(k == KC - 1))

    dummy = alloc((8,), F32)
    zpad = alloc((512,), BF16)
    MEMSET("pool", zpad[:, :], 0.0)

    def warm(nmm):
        o, a, b = PSW[:, :].ap, onesM[:, :].ap, zpad[:, :].ap

        def fn(e):
            inst = None
            for _ in range(nmm):
                inst = e.matmul(o, a, b, start=True, stop=True)
            return inst
        P.add("pe", fn, reads=[onesM[:, :].keys, zpad[:, :].keys], writes=[PSW[:, :].keys])

    def preload_ln():
        ACT(dummy[:, 0:1], ident[:, 0:1], AF.Ln, bias=1.0)

    def col(r):
        return colT[:, r:r + 1]

    MEMSET("pool", ident[:, :], 1.0)
    iap = ident[:, :]
    P.add("pool", lambda e: e.affine_select(out=iap.ap, in_=iap.ap, pattern=[[-1, 128]],
                                            compare_op=ALU.is_equal, fill=0.0, base=0, channel_multiplier=1),
          reads=[iap.keys], writes=[iap.keys])
    COPY("pool", identb[:, :], ident[:, :])
    MEMSET("pool", onesM[:, :], 1.0 / 1024.0)
    MEMSET("pool", onesA[:, :], 1.0 / 512.0)
    for e_ in range(NE):
        MEMSET("pool", carry_a[e_][:, :, :], 0.0)
        MEMSET("pool", carry_b[e_][:, :, :], 0.0)

    m0 = mark()
    rows = alloc((NRT, 128), F32)
    c_sb = alloc((D,), F32)
    srcs = [
        (R_NG, norm_g.rearrange("l w (k p) -> (l w k) p", p=128)),
        (R_FG, final_g.rearrange("(k p) -> k p", p=128)),
        (R_BADA, b_ada.rearrange("l (c p) -> (l c) p", p=128)),
        (R_CAW, conv_a_w.rearrange("e t (j p) -> (e t j) p", p=128)),
        (R_CAB, conv_a_b.rearrange("e (j p) -> (e j) p", p=128)),
        (R_LAG, ln_a_g.rearrange("e (j p) -> (e j) p", p=128)),
        (R_LAB, ln_a_b.rearrange("e (j p) -> (e j) p", p=128)),
        (R_CBW, conv_b_w.rearrange("e t (j p) -> (e t j) p", p=128)),
        (R_BU, b_in_c.rearrange("o (k p) -> o k p", p=128)[:, 0:8, :].rearrange("o k p -> (o k) p")
         if False else None),
    ]
    pairs = []
    wk = []
    MEMSET("pool", rows[:, :, :], 0.0)
    for base, src in srcs:
        if src is None:
            continue
        n = src.shape[0]
        r = 0
        while r < n:
            g = base + r
            t, pp = g // 128, g % 128
            m = min(n - r, 128 - pp)
            dv = rows[pp:pp + m, t, :]
            pairs.append((dv.ap, src[r:r + m, :]))
            wk.append(dv.keys)
            r += m
    for o_ in range(NO):
        g = R_BU + o_ * 8
        t, pp = g // 128, g % 128
        dv = rows[pp:pp + 8, t, :]
        pairs.append((dv.ap, b_in_c[o_, 0:1024].rearrange("(k p) -> k p", p=128)))
        wk.append(dv.keys)
    DMA("sp", "rows", pairs, writes=wk)
    cv = c_sb[0:1 + NS, :]
    DMA("sp", "cload", [(cv.ap, c17[:, :])], writes=[cv.keys])
    for t in range(NRT):
        pb = psum()
        TR(pb[:, 0:128], rows[:, t, :], ident[:, :])
        COPY("dve", colT[:, t * 128:(t + 1) * 128], pb[:, 0:128])
    for e_ in range(NE):
        for j in range(4):
            a0 = R_CAW + e_ * 124 + j
            src = View(colT.ap[:, a0:a0 + 121:4], colT[:, a0:a0 + 121].keys)
            COPY("dve", cwb[:, e_ * 4 + j, :], src)
    ACT(cv, cv, AF.Silu)
    pb = psum()
    for k in range(KC):
        TR(pb[:, k * 17:(k + 1) * 17], c_sb[0:1 + NS, k * 128:(k + 1) * 128], ident[0:1 + NS, 0:1 + NS])
    COPY("dve", cT[:, :, :], pb[:, 0:KC * 17].r("p (a b) -> p a b", b=17))
    release(m0)

    m0 = mark()
    wsn = alloc((8, 128), F32)
    for o_ in range(NO):
        for samp in (False, True):
            wv = wsn[:, :, :]
            if not samp:
                DMA("sp", "wsn", [(wv.ap, w_s[o_].rearrange("h t s -> t h s"))], writes=[wv.keys])
            else:
                MEMSET("pool", wv, 0.0)
                prs = []
                wks = []
                for n_ in range(NS):
                    dv = wsn[8 * n_:8 * n_ + 8, :, 8 * n_:8 * n_ + 8]
                    prs.append((dv.ap, w_s[o_, :, 0:8, 0:8].rearrange("h t s -> t h s")))
                    wks.append(dv.keys)
                DMA("sp", "wsn", prs, writes=wks)
            P.add("pool", lambda e, a=wv.ap: e.affine_select(out=a, in_=a, pattern=[[0, 8], [-1, 128]],
                                                             compare_op=ALU.is_ge, fill=0.0, base=0,
                                                             channel_multiplier=1),
                  reads=[wv.keys], writes=[wv.keys])
            dst = wsTs[o_] if samp else wsT[o_]
            for hh in range(2):
                pb = psum()
                for q in range(4):
                    TR(pb[:, q * 128:(q + 1) * 128], wsn[:, hh * 4 + q, :], ident[:, :])
                COPY("dve", dst[:, hh * 4:hh * 4 + 4, :], pb[:, :].r("p (a b) -> p a b", b=128))
    release(m0)

    blocks = []
    for b in range(NPB):
        segs = []
        for s in range(NSEGP):
            segs.append(dict(kind="p", off=s * 512, n=512, tok0=b * PBT + s * 512,
                             last=(b == NPB - 1 and s == NSEGP - 1)))
        if b == 0:
            segs.append(dict(kind="s", off=PBT, n=NS * DSQ, tok0=0, last=False))
        blocks.append(dict(b=b, segs=segs))

    tasks = []

    def wdma_cols(W2d, colchunks, nk):
        Wv = W2d.rearrange("(k p) n -> p k n", p=128)

        def f(slot):
            rb = ring8[slot] if nk == 8 else ring32[slot]
            prs = []
            wks = []
            i = 0
            while i < len(colchunks):
                j = i
                while j + 1 < len(colchunks) and colchunks[j + 1] == colchunks[j] + 1:
                    j += 1
                dv = rb[:, :, i * 128:(j + 1) * 128]
                prs.append((dv.ap, Wv[:, :, colchunks[i] * 128:(colchunks[j] + 1) * 128]))
                wks.append(dv.keys)
                i = j + 1
            DMA("pool", "ring%d" % slot, prs, writes=wks)
        return f

    def mods_idx(l, m, k):
        return (l * 6 + m) * 8 + k

    def mod_scalar(l, m, k):
        return mods[:, mods_idx(l, m, k), 0:1]

    def mod_bc(l, m, k):
        return mods[:, mods_idx(l, m, k), 1:1 + NS].un(2).bc([128, NS, DSQ])

    def ada_tasks(l):
        out = []
        for g in range(12):
            def run(slot, g=g):
                wt = ring8[slot]
                pb = psum()
                for oc in range(4):
                    MM(pb[:, oc * 17:(oc + 1) * 17],
                       [(wt[:, k, oc * 128:(oc + 1) * 128], cT[:, k, :]) for k in range(KC)])
                base = l * 48 + g * 4
                bb = colT[:, R_BADA + base:R_BADA + base + 4].un(2).bc([128, 4, 17])
                TT("dve", mods[:, base:base + 4, :], pb[:, 0:68].r("p (a b) -> p a b", b=17), bb, ALU.add)
                if g == 11:
                    for w_, m in ((0, 1), (1, 4)):
                        i0 = mods_idx(l, m, 0)
                        sc = mods[:, i0:i0 + 8, :]
                        TS("dve", sc, sc, 1.0, ALU.add)
                        gb = colT[:, R_NG + l * 16 + w_ * 8:R_NG + l * 16 + w_ * 8 + 8].un(2).bc([128, 8, 17])
                        TT("dve", sc, sc, gb, ALU.mult)
            out.append((wdma_cols(w_ada[l], [g * 4 + i for i in range(4)], 8), run))
        return out

    def norm_mod(blk, l, w_):
        m_sh, m_sc = (0, 1) if w_ == 0 else (3, 4)
        mk = mark()
        rs = alloc((TB,), F32)
        tmp = [alloc((512,), F32) for _ in range(3)]
        ti = 0
        flush_sq()
        warm(WARM_N)
        for sg in blk["segs"]:
            fresh.add(sg["off"])
        for sg in blk["segs"]:
            o, n = sg["off"], sg["n"]
            pb = PSN[sg["kind"]]
            ACT(rs[:, o:o + n], pb[:, 0:n], AF.Ln, bias=EPS)
            ACT(rs[:, o:o + n], rs[:, o:o + n], AF.Exp, scale=-0.5)
            for k in range(KC):
                t = tmp[ti % 3]
                ti += 1
                if sg["kind"] == "p":
                    STT(t[:, 0:n], xT[:, k, o:o + n], mod_scalar(l, m_sc, k), rs[:, o:o + n], ALU.mult, ALU.mult)
                    ACT(h[:, k, o:o + n], t[:, 0:n], AF.Identity, bias=mod_scalar(l, m_sh, k))
                else:
                    t3 = t[:, 0:n].r("p (a b) -> p a b", b=DSQ)
                    TT("dve", t[:, 0:n], xT[:, k, o:o + n], rs[:, o:o + n], ALU.mult)
                    TT("dve", t3, t3, mod_bc(l, m_sc, k), ALU.mult)
                    TT("dve", h[:, k, o:o + n].r("p (a b) -> p a b", b=DSQ), t3, mod_bc(l, m_sh, k), ALU.add)
        release(mk)

    def residual(sg, k, pbv, l, m_g, tmpb):
        o, n = sg["off"], sg["n"]
        if sg["kind"] == "p":
            STT(xT[:, k, o:o + n], pbv, mod_scalar(l, m_g, k), xT[:, k, o:o + n], ALU.mult, ALU.add)
        else:
            t3 = tmpb[:, 0:n].r("p (a b) -> p a b", b=DSQ)
            TT("dve", t3, pbv.r("p (a b) -> p a b", b=DSQ), mod_bc(l, m_g, k), ALU.mult)
            TT("dve", xT[:, k, o:o + n], xT[:, k, o:o + n], tmpb[:, 0:n], ALU.add)
        sq_accum(sg, k)

    def proj_run(blk, wt, nk, noc, rhs, evac, from_h=False):
        for oc in range(noc):
            for sg in blk["segs"]:
                pb = psum()
                n = sg["n"]
                prs_ = [(wt[:, k, oc * 128:(oc + 1) * 128], rhs(k, sg)) for k in range(nk)]
                if from_h:
                    MMh(pb[:, 0:n], prs_, sg["off"])
                else:
                    MM(pb[:, 0:n], prs_)
                flush_sq()
                evac(oc, sg, pb[:, 0:n])

    def even_mixer_tasks(blk, l, st):
        e_ = l // 2
        b = blk["b"]
        lastblk = (b == NPB - 1)
        T = []
        W = w_in_ab[e_]

        def pre(slot_unused=None):
            st["mk"] = mark()
            st["aext"] = alloc((4, 30 + PBT), BF16)
            st["asx"] = alloc((4, NS, 38), BF16)
            st["af32"] = alloc((4, 128), F32)
            st["sig"] = [alloc((512,), F32) for _ in range(2)]
            st["acc"] = alloc((4, TB), F32)
            st["accb"] = alloc((4, TB), BF16)
            st["sqb"] = alloc((4, TB), BF16)
            st["bxe"] = alloc((4, 2 + PBT), F32)
            st["bxs"] = alloc((4, NS, 10), F32)
            st["cb"] = alloc((4, TB), F32)
            st["dg"] = alloc((2, 31, 128), BF16)
            st["mean"] = alloc((TB,), F32)
            st["var"] = alloc((TB,), F32)
            st["rstd"] = alloc((TB,), F32)
            st["t1"] = [alloc((512,), F32) for _ in range(2)]
            st["sto"] = alloc((512,), F32)
            st["ctr"] = 0
            aext, bxe, asx, bxs = st["aext"], st["bxe"], st["asx"], st["bxs"]
            COPY("pool", aext[:, :, 0:30], carry_a[e_][:, :, :])
            COPY("pool", bxe[:, :, 0:2], carry_b[e_][:, :, :])
            if b == 0:
                hs = alloc((4, 512), F32)
                hb = alloc((512,), F32)
                prs, wks = [], []
                for t in range(4):
                    dv = hs[0:120, t, :]
                    prs.append((dv.ap, sa[e_, 4 * t:4 * t + 4].rearrange("n r c -> (n r) c")))
                    wks.append(dv.keys)
                dvb = hb[0:32, :]
                prs.append((dvb.ap, sb[e_].rearrange("n r c -> (n r) c")))
                wks.append(dvb.keys)
                DMA("sp", "hist", prs, writes=wks)
                for t in range(4):
                    pb = psum()
                    for j in range(4):
                        TR(pb[:, j * 120:(j + 1) * 120], hs[0:120, t, j * 128:(j + 1) * 128], ident[0:120, 0:120])
                    for j in range(4):
                        COPY("act", asx[:, j, 4 * t:4 * t + 4, 0:30],
                             pb[:, j * 120:(j + 1) * 120].r("p (a b) -> p a b", b=30))
                pb = psum()
                for j in range(4):
                    TR(pb[:, j * 32:(j + 1) * 32], hb[0:32, j * 128:(j + 1) * 128], ident[0:32, 0:32])
                for j in range(4):
                    COPY("act", bxs[:, j, :, 0:2], pb[:, j * 32:(j + 1) * 32].r("p (a b) -> p a b", b=2))
                prs2, rks = [], []
                for t in range(4):
                    for q in range(4):
                        sv2 = hs[30 * q + 8:30 * q + 30, t, :]
                        prs2.append((ca_s[e_, 4 * t + q, 0:22, :], sv2.ap))
                        rks.append(sv2.keys)
                DMA("sp", "histo", prs2, reads=rks, is_out=True)

        for g in range(2):
            def runA(slot, g=g):
                if g == 0:
                    pre()
                wt = ring8[slot]
                aext, asx, af32 = st["aext"], st["asx"], st["af32"]
                dg = st["dg"]
                for jj in range(2):
                    j = 2 * g + jj
                    for sg in blk["segs"]:
                        o, n = sg["off"], sg["n"]
                        pv, pg = psum(), psum()
                        MMh(pv[:, 0:n], [(wt[:, k, (2 * jj) * 128:(2 * jj + 1) * 128], h[:, k, o:o + n]) for k in range(KC)], o)
                        MM(pg[:, 0:n], [(wt[:, k, (2 * jj + 1) * 128:(2 * jj + 2) * 128], h[:, k, o:o + n]) for k in range(KC)])
                        sgb = st["sig"][st["ctr"] % 2]
                        st["ctr"] += 1
                        ACT(sgb[:, 0:n], pg[:, 0:n], AF.Sigmoid)
                        if sg["kind"] == "p":
                            TT("dve", aext[:, j, 30 + o:30 + o + n], pv[:, 0:n], sgb[:, 0:n], ALU.mult)
                            if sg["last"]:
                                TT("dve", af32[:, j, 0:30], pv[:, n - 30:n], sgb[:, n - 30:n], ALU.mult)
                        else:
                            TT("dve", asx[:, j, :, 30:38], pv[:, 0:n].r("p (a b) -> p a b", b=DSQ),
                               sgb[:, 0:n].r("p (a b) -> p a b", b=DSQ), ALU.mult)
                            TT("dve", af32[:, j, 0:n], pv[:, 0:n], sgb[:, 0:n], ALU.mult)
                    TT("dve", dg[:, jj, :, :], identb[:, :].un(1).bc([128, 31, 128]),
                       cwb[:, e_ * 4 + j, :].un(2).bc([128, 31, 128]), ALU.mult)

                    def conv(j=j, jj=jj):
                        cbias = col(R_CAB + e_ * 4 + j)
                        for sg in blk["segs"]:
                            o, n = sg["off"], sg["n"]
                            pb = psum()
                            if sg["kind"] == "p":
                                MM(pb[:, 0:n], [(dg[:, jj, tap, :], aext[:, j, o + tap:o + tap + n]) for tap in range(31)])
                            else:
                                MM(pb[:, 0:n], [(dg[:, jj, tap, :], asx[:, j, :, tap:tap + DSQ]) for tap in range(31)])
                            TS("dve", st["acc"][:, j, o:o + n], pb[:, 0:n], cbias, ALU.add)
                            COPY("act", st["accb"][:, j, o:o + n], st["acc"][:, j, o:o + n])
                            ACT(st["sqb"][:, j, o:o + n], st["acc"][:, j, o:o + n], AF.Square)
                    if st.get("pend") is not None:
                        st["pend"]()
                    st["pend"] = conv
                if g == 1:
                    st["pend"]()
                    st["pend"] = None
                if g == 1:
                    for sg in blk["segs"]:
                        o, n = sg["off"], sg["n"]
                        pm, pe2 = psum(), psum()
                        MM(pm[:, 0:n], [(onesA[:, :], st["accb"][:, j2, o:o + n]) for j2 in range(4)])
                        MM(pe2[:, 0:n], [(onesA[:, :], st["sqb"][:, j2, o:o + n]) for j2 in range(4)])
                        mean, var, rstd = st["mean"][:, o:o + n], st["var"][:, o:o + n], st["rstd"][:, o:o + n]
                        COPY("act", mean, pm[:, 0:n])
                        STT(var, mean, -1.0, mean, ALU.mult, ALU.mult)
                        TT("dve", var, pe2[:, 0:n], var, ALU.add)
                        ACT(rstd, var, AF.Ln, bias=EPS)
                        ACT(rstd, rstd, AF.Exp, scale=-0.5)
                        for j2 in range(4):
                            t1 = st["t1"][j2 % 2]
                            TT("dve", t1[:, 0:n], st["acc"][:, j2, o:o + n], mean, ALU.subtract)
                            TT("dve", t1[:, 0:n], t1[:, 0:n], rstd, ALU.mult)
                            ACT(mo[:, j2, o:o + n], t1[:, 0:n], AF.Silu,
                                bias=col(R_LAB + e_ * 4 + j2), scale=col(R_LAG + e_ * 4 + j2))
                    preload_ln()
                    COPY("pool", carry_a[e_][:, :, :], aext[:, :, PBT:PBT + 30])
                    if lastblk:
                        pb = psum()
                        for j2 in range(4):
                            TR(pb[0:30, j2 * 128:(j2 + 1) * 128], af32[:, j2, 0:30], ident[:, :])
                        sv = st["sto"][0:30, :]
                        COPY("dve", sv, pb[0:30, :])
                        DMA("sp", "sto", [(ca_p[e_], sv.ap)], reads=[sv.keys], is_out=True)
                    if b == 0:
                        pb = psum()
                        for j2 in range(4):
                            TR(pb[:, j2 * 128:(j2 + 1) * 128], af32[:, j2, 0:128], ident[:, :])
                        sv = st["sto"][:, :]
                        COPY("dve", sv, pb[:, :])
                        DMA("sp", "sto", [(ca_s[e_, n_, 22:30, :], st["sto"][8 * n_:8 * n_ + 8, :].ap) for n_ in range(NS)],
                            reads=[sv.keys], is_out=True)
            T.append((wdma_cols(W, [2 * g, 4 + 2 * g, 2 * g + 1, 4 + 2 * g + 1], 8), runA))
        for g in range(2):
            def runB(slot, g=g):
                wt = ring8[slot]
                bxe, bxs, cb = st["bxe"], st["bxs"], st["cb"]
                for jj in range(2):
                    j = 2 * g + jj
                    for sg in blk["segs"]:
                        o, n = sg["off"], sg["n"]
                        px, pc = psum(), psum()
                        MM(px[:, 0:n], [(wt[:, k, (2 * jj) * 128:(2 * jj + 1) * 128], h[:, k, o:o + n]) for k in range(KC)])
                        MM(pc[:, 0:n], [(wt[:, k, (2 * jj + 1) * 128:(2 * jj + 2) * 128], h[:, k, o:o + n]) for k in range(KC)])
                        sgb = st["sig"][st["ctr"] % 2]
                        st["ctr"] += 1
                        COPY("act", sgb[:, 0:n], px[:, 0:n])
                        w0 = col(R_CBW + e_ * 12 + 0 * 4 + j)
                        w1 = col(R_CBW + e_ * 12 + 1 * 4 + j)
                        w2 = col(R_CBW + e_ * 12 + 2 * 4 + j)
                        if sg["kind"] == "p":
                            TT("dve", bxe[:, j, 2 + o:2 + o + n], pc[:, 0:n], sgb[:, 0:n], ALU.mult)
                            c_ = cb[:, j, o:o + n]
                            TS("dve", c_, bxe[:, j, o:o + n], w0, ALU.mult)
                            STT(c_, bxe[:, j, o + 1:o + 1 + n], w1, c_, ALU.mult, ALU.add)
                            STT(c_, bxe[:, j, o + 2:o + 2 + n], w2, c_, ALU.mult, ALU.add)
                        else:
                            TT("dve", bxs[:, j, :, 2:10], pc[:, 0:n].r("p (a b) -> p a b", b=DSQ),
                               sgb[:, 0:n].r("p (a b) -> p a b", b=DSQ), ALU.mult)
                            c_ = cb[:, j, o:o + n].r("p (a b) -> p a b", b=DSQ)
                            TS("dve", c_, bxs[:, j, :, 0:8], w0, ALU.mult)
                            STT(c_, bxs[:, j, :, 1:9], w1, c_, ALU.mult, ALU.add)
                            STT(c_, bxs[:, j, :, 2:10], w2, c_, ALU.mult, ALU.add)
                if g == 1:
                    COPY("pool", carry_b[e_][:, :, :], bxe[:, :, PBT:PBT + 2])
                    if lastblk:
                        t1 = st["t1"][0]
                        COPY("dve", t1[:, 0:8].r("p (a b) -> p a b", b=2), bxe[:, :, PBT:PBT + 2])
                        pb = psum()
                        for j2 in range(4):
                            TR(pb[0:2, j2 * 128:(j2 + 1) * 128], t1[:, 2 * j2:2 * j2 + 2], ident[:, :])
                        sv = st["sto"][0:2, :]
                        COPY("dve", sv, pb[0:2, :])
                        DMA("sp", "sto", [(cb_p[e_], sv.ap)], reads=[sv.keys], is_out=True)
                    if b == 0:
                        t1 = st["t1"][1]
                        pb = psum()
                        for j2 in range(4):
                            COPY("dve", t1[:, 32 * j2:32 * j2 + 32].r("p (a b) -> p a b", b=2), bxs[:, j2, :, 8:10])
                            TR(pb[0:32, j2 * 128:(j2 + 1) * 128], t1[:, 32 * j2:32 * j2 + 32], ident[:, :])
                        sv = st["sto"][0:32, :]
                        COPY("dve", sv, pb[0:32, :])
                        DMA("sp", "sto", [(cb_s[e_].rearrange("n r c -> (n r) c"), sv.ap)], reads=[sv.keys], is_out=True)
            T.append((wdma_cols(W, [8 + 2 * g, 16 + 2 * g, 8 + 2 * g + 1, 16 + 2 * g + 1], 8), runB))

        def runBB(slot):
            wt = ring8[slot]

            def ev(oc, sg, pv):
                o, n = sg["off"], sg["n"]
                TT("dve", mo[:, 4 + oc, o:o + n], pv, st["cb"][:, oc, o:o + n], ALU.mult)
            proj_run(blk, wt, KC, 4, lambda k, sg: h[:, k, sg["off"]:sg["off"] + sg["n"]], ev)
        T.append((wdma_cols(W, [12, 13, 14, 15], 8), runBB))
        for g in range(2):
            def runO(slot, g=g):
                wt = ring8[slot]
                tmpb = st["t1"][0]

                def ev(oc, sg, pv):
                    residual(sg, g * 4 + oc, pv, l, 2, tmpb)
                proj_run(blk, wt, KC, 4, lambda k, sg: mo[:, k, sg["off"]:sg["off"] + sg["n"]], ev)
                if g == 1:
                    release(st["mk"])
            T.append((wdma_cols(w_out_ab[e_], [4 * g + i for i in range(4)], 8), runO))
        return T

    def odd_mixer_tasks(blk, l, st):
        o_ = l // 2
        b = blk["b"]
        lastblk = (b == NPB - 1)
        T = []
        W = w_in_c[o_]
        ntile = sum(sg["n"] for sg in blk["segs"]) // 128

        def pre():
            st["mk"] = mark()
            st["bvb"] = alloc((D,), F32)
            st["lng"] = alloc((D,), F32)
            st["lnb"] = alloc((D,), F32)
            st["bsb"] = alloc((8, 128), F32)
            st["vraw"] = [alloc((D,), F32) for _ in range(ntile)]
            st["vnb"] = alloc((ntile, D), BF16)
            st["stat"] = alloc((ntile, 16), F32)
            st["ubuf"] = alloc((8, TB), F32)
            st["tmp"] = [alloc((512,), F32) for _ in range(2)]
            st["ctr"] = 0
            prs = [(st["bvb"][:, :].ap, b_in_c[o_:o_ + 1, 1024:2048].partition_broadcast(128)),
                   (st["lng"][:, :].ap, ln_v_g[o_:o_ + 1, :].partition_broadcast(128)),
                   (st["lnb"][:, :].ap, ln_v_b[o_:o_ + 1, :].partition_broadcast(128)),
                   (st["bsb"][:, :, :].r("p a b -> p (a b)").ap,
                    b_s[o_:o_ + 1].rearrange("o h t -> o (h t)").partition_broadcast(128))]
            DMA("sp", "oddc", prs, writes=[st["bvb"][:, :].keys, st["lng"][:, :].keys,
                                           st["lnb"][:, :].keys, st["bsb"][:, :, :].keys])

        def tile_info(ti):
            c = 0
            for sg in blk["segs"]:
                if ti * 128 < c + sg["n"]:
                    return sg, ti * 128
                c += sg["n"]
            raise ValueError

        for g in range(2):
            def runV(slot, g=g):
                if g == 0:
                    pre()
                wt = ring8[slot]
                for ti in range(ntile):
                    pb = psum()
                    o = ti * 128
                    MMh(pb[:, :], [(h[:, k, o:o + 128], wt[:, k, :]) for k in range(KC)], o)
                    vr = st["vraw"][ti]
                    TT("dve", vr[:, g * 512:(g + 1) * 512], pb[:, :], st["bvb"][:, g * 512:(g + 1) * 512], ALU.add)
                    ACT(vr[:, g * 512:(g + 1) * 512], vr[:, g * 512:(g + 1) * 512], AF.Gelu_apprx_tanh)
                    if g == 1:
                        sta = st["stat"]
                        for hh in range(2):
                            sv_, vv_ = sta[:, ti, hh * 6:hh * 6 + 6], vr[:, hh * 512:(hh + 1) * 512]
                            P.add("dve", lambda e, a=sv_.ap, c=vv_.ap: e.bn_stats(out=a, in_=c),
                                  reads=[vv_.keys], writes=[sv_.keys])
                        mv = sta[:, ti, 12:14]
                        s12 = sta[:, ti, 0:12]
                        P.add("dve", lambda e, a=mv.ap, c=s12.ap: e.bn_aggr(out=a, in_=c),
                              reads=[s12.keys], writes=[mv.keys])
                        rstd = sta[:, ti, 14:15]
                        ACT(rstd, sta[:, ti, 13:14], AF.Sqrt, bias=EPS)
                        RECIP(rstd, rstd)
                        TS("dve", vr[:, :], vr[:, :], sta[:, ti, 12:13], ALU.subtract, rstd, ALU.mult)
                        TT("dve", vr[:, :], vr[:, :], st["lng"][:, :], ALU.mult)
                        TT("dve", vr[:, :], vr[:, :], st["lnb"][:, :], ALU.add)
                        COPY("act", st["vnb"][:, ti, :], vr[:, :])
                        sg, _ = tile_info(ti)
                        if sg["kind"] == "s":
                            DMA("sp", "cvs", [(cv_s[o_], vr[:, :].ap)], reads=[vr[:, :].keys], is_out=True)
                        elif lastblk and ti == ntile - 1:
                            DMA("sp", "cvp", [(cv_p[o_], vr[:, :].ap)], reads=[vr[:, :].keys], is_out=True)
            T.append((wdma_cols(W, [8 + 4 * g + i for i in range(4)], 8), runV))
        for g in range(2):
            def runU(slot, g=g):
                wt = ring8[slot]
                for oc in range(4):
                    j = 4 * g + oc
                    for sg in blk["segs"]:
                        o, n = sg["off"], sg["n"]
                        pu = psum()
                        MM(pu[:, 0:n], [(wt[:, k, oc * 128:(oc + 1) * 128], h[:, k, o:o + n]) for k in range(KC)])
                        ACT(st["ubuf"][:, j, o:o + n], pu[:, 0:n], AF.Gelu_apprx_tanh, bias=col(R_BU + o_ * 8 + j))
                if g == 1:
                    preload_ln()
            T.append((wdma_cols(W, [4 * g + i for i in range(4)], 8), runU))

        def gate():
            for j in range(8):
                for sg in blk["segs"]:
                    o, n = sg["off"], sg["n"]
                    pg = psum()
                    for cc in range(n // 128):
                        ti = (o + cc * 128) // 128
                        rhs = wsTs[o_][:, j, :] if sg["kind"] == "s" else wsT[o_][:, j, :]
                        MM(pg[:, cc * 128:(cc + 1) * 128], [(st["vnb"][:, ti, j * 128:(j + 1) * 128], rhs)])
                    tb = st["tmp"][st["ctr"] % 2]
                    st["ctr"] += 1
                    if sg["kind"] == "p":
                        nb4 = n // 128
                        bsv = st["bsb"][:, j, :].un(1).bc([128, nb4, 128])
                        TT("dve", tb[:, 0:n].r("p (a b) -> p a b", b=128), pg[:, 0:n].r("p (a b) -> p a b", b=128),
                           bsv, ALU.add)
                    else:
                        bsv = st["bsb"][:, j, 0:DSQ].un(1).bc([128, NS, DSQ])
                        TT("dve", tb[:, 0:n].r("p (a b) -> p a b", b=DSQ), pg[:, 0:n].r("p (a b) -> p a b", b=DSQ),
                           bsv, ALU.add)
                    TT("dve", mo[:, j, o:o + n], tb[:, 0:n], st["ubuf"][:, j, o:o + n], ALU.mult)
        T.append((None, lambda slot: gate()))
        for g in range(2):
            def runO(slot, g=g):
                wt = ring8[slot]
                tmpb = st["tmp"][0]

                def ev(oc, sg, pv):
                    residual(sg, g * 4 + oc, pv, l, 2, tmpb)
                proj_run(blk, wt, KC, 4, lambda k, sg: mo[:, k, sg["off"]:sg["off"] + sg["n"]], ev)
                if g == 1:
                    release(st["mk"])
            T.append((wdma_cols(w_out_c[o_], [4 * g + i for i in range(4)], 8), runO))
        return T

    def ffn_tasks(blk, l, st):
        T = []

        def pre():
            st["mk"] = mark()
            st["f"] = alloc((32, TB), BF16)
            st["rl"] = [alloc((512,), F32) for _ in range(3)]
            st["ctr"] = 0
        for g in range(8):
            def run1(slot, g=g):
                if g == 0:
                    pre()
                wt = ring8[slot]

                def ev(oc, sg, pv):
                    o, n = sg["off"], sg["n"]
                    r = st["rl"][st["ctr"] % 3]
                    st["ctr"] += 1
                    ACT(r[:, 0:n], pv, AF.Relu)
                    TT("dve", st["f"][:, g * 4 + oc, o:o + n], r[:, 0:n], r[:, 0:n], ALU.mult)
                proj_run(blk, wt, KC, 4, lambda k, sg: h[:, k, sg["off"]:sg["off"] + sg["n"]], ev, from_h=True)
                if g == 7:
                    preload_ln()
            T.append((wdma_cols(w_ff1[l], [4 * g + i for i in range(4)], 8), run1))
        for g in range(8):
            def run2(slot, g=g):
                wt = ring32[slot]
                tmpb = st["rl"][0]

                def ev(oc, sg, pv):
                    residual(sg, g, pv, l, 5, tmpb)
                proj_run(blk, wt, 32, 1, lambda k, sg: st["f"][:, k, sg["off"]:sg["off"] + sg["n"]], ev)
                if g == 7:
                    release(st["mk"])
            T.append((wdma_cols(w_ff2[l], [g], 32), run2))
        return T

    def load_block(blk):
        mk = mark()
        stg = [alloc((D,), F32) for _ in range(2)]
        i = 0
        for sg in blk["segs"]:
            for tt in range(sg["n"] // 128):
                s = stg[i % 2]
                sv = s[:, :]
                if sg["kind"] == "p":
                    src = x_p[sg["tok0"] + tt * 128:sg["tok0"] + (tt + 1) * 128, :]
                else:
                    src = x_s[:, :]
                DMA("sp", "xs%d" % (i % 2), [(sv.ap, src)], writes=[sv.keys])
                o = sg["off"] + tt * 128
                for hh in range(2):
                    pb = psum()
                    for q in range(4):
                        k = hh * 4 + q
                        TR(pb[:, q * 128:(q + 1) * 128], s[:, k * 128:(k + 1) * 128], ident[:, :])
                    COPY("act", xT[:, hh * 4:hh * 4 + 4, o:o + 128], pb[:, :].r("p (a b) -> p a b", b=128))
                i += 1
        for sg in blk["segs"]:
            for k in range(KC):
                sq_accum(sg, k)
                flush_sq(keep=2)
        release(mk)

    def store_block(blk):
        mk = mark()
        stg = [alloc((D,), F32) for _ in range(2)]
        rs = alloc((TB,), F32)
        i = 0
        flush_sq()
        for sg in blk["segs"]:
            o, n = sg["off"], sg["n"]
            if final_norm:
                pb = PSN[sg["kind"]]
                ACT(rs[:, o:o + n], pb[:, 0:n], AF.Ln, bias=EPS)
                ACT(rs[:, o:o + n], rs[:, o:o + n], AF.Exp, scale=-0.5)
                for k in range(KC):
                    STT(xT[:, k, o:o + n], xT[:, k, o:o + n], col(R_FG + k), rs[:, o:o + n], ALU.mult, ALU.mult)
            for tt in range(n // 128):
                s = stg[i % 2]
                oo = o + tt * 128
                for hh in range(2):
                    pb = psum()
                    for q in range(4):
                        k = hh * 4 + q
                        TR(pb[:, q * 128:(q + 1) * 128], xT[:, k, oo:oo + 128], ident[:, :])
                    COPY("act" if hh == 0 else "dve", s[:, hh * 512:(hh + 1) * 512], pb[:, :])
                if sg["kind"] == "p":
                    dst = y_p[sg["tok0"] + tt * 128:sg["tok0"] + (tt + 1) * 128, :]
                else:
                    dst = y_s[:, :]
                DMA("sp", "ys%d" % (i % 2), [(dst, s[:, :].ap)], reads=[s[:, :].keys], is_out=True)
                i += 1
        release(mk)

    def T_plain(fn):
        return (None, lambda slot: fn())

    if nlayers > 0:
        tasks += ada_tasks(0)
    for blk in blocks:
        tasks.append(T_plain(lambda blk=blk: load_block(blk)))
        for l in range(nlayers):
            st = {}
            tasks.append(T_plain(lambda blk=blk, l=l: norm_mod(blk, l, 0)))
            if l % 2 == 0:
                tasks += even_mixer_tasks(blk, l, st)
            else:
                tasks += odd_mixer_tasks(blk, l, st)
            tasks.append(T_plain(lambda blk=blk, l=l: norm_mod(blk, l, 1)))
            st2 = {}
            ft = ffn_tasks(blk, l, st2)
            if blk["b"] == 0 and l + 1 < nlayers:
                at = ada_tasks(l + 1)
                merged = []
                for i_, t_ in enumerate(ft):
                    merged.append(t_)
                    if i_ < len(at):
                        merged.append(at[i_])
                ft = merged
            tasks += ft
        tasks.append(T_plain(lambda blk=blk: store_block(blk)))

    if max_tasks is not None:
        tasks = tasks[:max_tasks]
    widx = [i for i, t in enumerate(tasks) if t[0] is not None]
    slot_of = {ti: n % NB for n, ti in enumerate(widx)}
    issued = 0
    LOOK = NB - 1
    for n, ti in enumerate(widx[:LOOK]):
        tasks[ti][0](slot_of[ti])
        issued += 1
    wpos = 0
    for i, (dfn, run) in enumerate(tasks):
        if dfn is not None:
            if issued < len(widx):
                tj = widx[issued]
                tasks[tj][0](slot_of[tj])
                issued += 1
            run(slot_of[i])
            wpos += 1
        else:
            run(None)

    P.add("sp", None, reads=outkeys)
    P.emit(nc, block, stack)
    stack.close()
    return nc


_CACHE = {}


def _get_prog(**kw):
    key = tuple(sorted(kw.items()))
    if key not in _CACHE:
        _CACHE[key] = build_program(**kw)
    return _CACHE[key]


def make_in_maps(inputs):
    f = lambda a: np.ascontiguousarray(np.asarray(a, dtype=np.float32))
    shared = {k: f(inputs[k]) for k in
              ["w_in_ab", "conv_a_w", "conv_a_b", "ln_a_g", "ln_a_b", "conv_b_w", "w_out_ab", "w_in_c", "b_in_c",
               "ln_v_g", "ln_v_b", "w_s", "b_s", "w_out_c", "w_ada", "b_ada", "norm_g", "w_ff1", "w_ff2", "final_g"]}
    xp, xs = f(inputs["x_prompt"]), f(inputs["x_sample"])
    sa, sb = f(inputs["state_conv_a"]), f(inputs["state_conv_b"])
    cp, cs = f(inputs["c_prompt"]), f(inputs["c_sample"])
    maps = []
    for c in range(NCORES):
        m = dict(shared)
        m["x_p"] = xp[c]
        m["x_s"] = np.ascontiguousarray(xs[NS * c:NS * (c + 1)].reshape(NS * DSQ, D))
        m["sa"] = np.ascontiguousarray(sa[:, NS * c:NS * (c + 1)])
        m["sb"] = np.ascontiguousarray(sb[:, NS * c:NS * (c + 1)])
        m["c17"] = np.ascontiguousarray(np.concatenate([cp[c:c + 1], cs[NS * c:NS * (c + 1)]], axis=0))
        maps.append(m)
    return maps


def gather(results):
    r = results
    y_prompt = np.stack([r[c]["y_p"] for c in range(NCORES)], axis=0)
    y_sample = np.concatenate([r[c]["y_s"].reshape(NS, DSQ, D) for c in range(NCORES)], axis=0)
    ca_p = np.stack([r[c]["ca_p"] for c in range(NCORES)], axis=1)
    ca_s = np.concatenate([r[c]["ca_s"] for c in range(NCORES)], axis=1)
    cb_p = np.stack([r[c]["cb_p"] for c in range(NCORES)], axis=1)
    cb_s = np.concatenate([r[c]["cb_s"] for c in range(NCORES)], axis=1)
    cv_p = np.stack([r[c]["cv_p"] for c in range(NCORES)], axis=1)
    cv_s = np.concatenate([r[c]["cv_s"].reshape(2, NS, DSQ, D) for c in range(NCORES)], axis=1)
    return tuple(np.ascontiguousarray(a, dtype=np.float32) for a in
                 (y_prompt, y_sample, ca_p, ca_s, cb_p, cb_s, cv_p, cv_s))


def kernel(**inputs):
    nc = _get_prog()
    res = run_bass_kernel_spmd(nc, make_in_maps(inputs), core_ids=list(range(NCORES)))
    return gather(res.results)
```

```python
import numpy as np
from contextlib import ExitStack
import concourse.bass as bass
import concourse.mybir as mybir
from concourse.bass_utils import run_bass_kernel_spmd

F32 = mybir.dt.float32
BF16 = mybir.dt.bfloat16
U8 = mybir.dt.uint8
AF = mybir.ActivationFunctionType
ALU = mybir.AluOpType

D = 1024
KC = 8
SEQ = 2048
NS = 16
DSQ = 8
DEPTH = 4
EPS = 1e-6
GRAN = 256
WARM_N = 14
NCORES = 8

R_NG = 0
R_FG = 64
R_BADA = 72
R_CAW = 264
R_CAB = 512
R_LAG = 520
R_LAB = 528
R_CBW = 536
R_BU = 560
R_TOT = 576
NRT = 5


class View:
    __slots__ = ("ap", "keys")

    def __init__(self, ap, keys):
        self.ap = ap
        self.keys = keys

    def r(self, pat, **kw):
        return View(self.ap.rearrange(pat, **kw), self.keys)

    def bc(self, shape):
        return View(self.ap.to_broadcast(list(shape)), self.keys)

    def un(self, ax):
        return View(self.ap.unsqueeze(ax), self.keys)


class Buf:
    def __init__(self, arena, off, shape, dt):
        self.off = off
        self.shape = tuple(shape)
        self.dt = dt
        self.es = 4 if dt == F32 else 2
        n = 1
        for s in shape:
            n *= s
        self.nbytes = n * self.es
        flat = arena[:, off:off + self.nbytes].bitcast(dt)
        if len(shape) == 1:
            self.ap = flat
        elif len(shape) == 2:
            self.ap = flat.rearrange("p (a b) -> p a b", a=shape[0])
        elif len(shape) == 3:
            self.ap = flat.rearrange("p (a b c) -> p a b c", a=shape[0], b=shape[1])
        else:
            raise ValueError
        st = []
        acc = 1
        for s in reversed(self.shape):
            st.append(acc)
            acc *= s
        self.strides = tuple(reversed(st))

    def __getitem__(self, idx):
        if not isinstance(idx, tuple):
            idx = (idx,)
        idx = list(idx) + [slice(None)] * (1 + len(self.shape) - len(idx))
        ap = self.ap[tuple(idx)]
        rng = []
        for d, ix in enumerate(idx[1:]):
            if isinstance(ix, int):
                rng.append((ix, ix + 1))
            else:
                a = 0 if ix.start is None else ix.start
                b = self.shape[d] if ix.stop is None else ix.stop
                assert ix.step in (None, 1)
                rng.append((a, b))
        outer = rng[:-1]
        n_outer = 1
        for a, b in outer:
            n_outer *= (b - a)
        keys = set()
        la, lb = rng[-1]
        if n_outer <= 128:
            def rec(d, base):
                if d == len(outer):
                    lo = self.off + (base + la) * self.es
                    hi = self.off + (base + lb) * self.es
                    for g in range(lo // GRAN, (hi - 1) // GRAN + 1):
                        keys.add(g)
                    return
                for i in range(outer[d][0], outer[d][1]):
                    rec(d + 1, base + i * self.strides[d])
            rec(0, 0)
        else:
            lo = self.off + sum(r[0] * s for r, s in zip(rng, self.strides)) * self.es
            hi = self.off + (sum((r[1] - 1) * s for r, s in zip(rng, self.strides)) + 1) * self.es
            for g in range(lo // GRAN, (hi - 1) // GRAN + 1):
                keys.add(g)
        return View(ap, frozenset(keys))


class PBank:
    def __init__(self, t, i):
        self.t = t
        self.i = i
        self.keys = frozenset([("ps", i)])

    def __getitem__(self, idx):
        return View(self.t[idx], self.keys)


class Prog:
    def __init__(self):
        self.ops = []
        self.last_w = {}
        self.readers = {}
        self.slots = {}

    def add(self, eng, fn, reads=(), writes=(), dma=None, ninc=1):
        idx = len(self.ops)
        deps = {}
        rk = set()
        for r in reads:
            rk |= set(r) if isinstance(r, (set, frozenset)) else {r}
        wk = set()
        for w in writes:
            wk |= set(w) if isinstance(w, (set, frozenset)) else {w}
        wk |= {k for k in rk if isinstance(k, tuple) and k[0] == "ps"}
        lw = self.last_w
        rd = self.readers
        for k in rk:
            w = lw.get(k)
            if w is not None:
                deps[w] = "raw"
        for k in wk:
            w = lw.get(k)
            if w is not None and w not in deps:
                deps[w] = "waw"
            for x in rd.get(k, ()):
                if x not in deps:
                    deps[x] = "war"
        need = []
        for p, kind in deps.items():
            P = self.ops[p]
            if P["dma"] is None and dma is None and P["eng"] == eng:
                if eng == "pe":
                    continue
            need.append(p)
            P["signal"] = True
        op = dict(eng=eng, fn=fn, deps=need, dma=dma, ninc=ninc, signal=False)
        self.ops.append(op)
        for k in wk:
            lw[k] = idx
            rd[k] = []
        for k in rk:
            rd.setdefault(k, []).append(idx)
        return idx

    def emit(self, nc, block, stack):
        ENG = ["pe", "act", "dve", "pool", "sp"]
        esem = {e: stack.enter_context(nc.semaphore("S_" + e)) for e in ENG}
        slotsem = {}
        cnt = {e: 0 for e in ENG}
        scnt = {}
        const_total = 16 * sum(o["ninc"] for o in self.ops if o["dma"] == "const")
        for o in self.ops:
            if o["dma"] is None:
                if o["signal"]:
                    cnt[o["eng"]] += 1
                    o["token"] = (esem[o["eng"]], cnt[o["eng"]])
            else:
                s = o["dma"]
                if s not in slotsem:
                    slotsem[s] = stack.enter_context(nc.semaphore("D_" + s))
                    scnt[s] = 0
                scnt[s] += 16 * o["ninc"]
                o["token"] = (slotsem[s], const_total if s == "const" else scnt[s])
                o["sem"] = slotsem[s]
        per = {e: [o for o in self.ops if o["eng"] == e] for e in ENG}
        ops = self.ops

        def runner(e):
            def f(engobj):
                waited = {}
                for o in per[e]:
                    need = {}
                    for p in o["deps"]:
                        sem, val = ops[p]["token"]
                        if need.get(sem, (None, 0))[1] < val:
                            need[sem] = (sem, val)
                    for sem, val in need.values():
                        if waited.get(sem, 0) < val:
                            engobj.wait_ge(sem, val)
                            waited[sem] = val
                    if o["fn"] is None:
                        continue
                    if o["dma"] is not None:
                        o["fn"](engobj, o["sem"])
                    else:
                        inst = o["fn"](engobj)
                        if o["signal"]:
                            inst.then_inc(esem[e], 1)
            return f
        block.tensor(runner("pe"))
        block.scalar(runner("act"))
        block.vector(runner("dve"))
        block.gpsimd(runner("pool"))
        block.sync(runner("sp"))


def build_program(nlayers=DEPTH, PBT=512, final_norm=True, max_tasks=None):
    nc = bass.Bass("TRN2", target_bir_lowering=False)
    NE = 2
    NO = 2

    def din(name, shape):
        return nc.dram_tensor(name, list(shape), F32, kind="ExternalInput").ap()

    def dout(name, shape):
        return nc.dram_tensor(name, list(shape), F32, kind="ExternalOutput").ap()

    x_p = din("x_p", [SEQ, D])
    x_s = din("x_s", [NS * DSQ, D])
    sa = din("sa", [NE, NS, 30, 512])
    sb = din("sb", [NE, NS, 2, 512])
    c17 = din("c17", [1 + NS, D])
    w_in_ab = din("w_in_ab", [NE, D, 2560])
    conv_a_w = din("conv_a_w", [NE, 31, 512])
    conv_a_b = din("conv_a_b", [NE, 512])
    ln_a_g = din("ln_a_g", [NE, 512])
    ln_a_b = din("ln_a_b", [NE, 512])
    conv_b_w = din("conv_b_w", [NE, 3, 512])
    w_out_ab = din("w_out_ab", [NE, D, D])
    w_in_c = din("w_in_c", [NO, D, 2048])
    b_in_c = din("b_in_c", [NO, 2048])
    ln_v_g = din("ln_v_g", [NO, D])
    ln_v_b = din("ln_v_b", [NO, D])
    w_s = din("w_s", [NO, 8, 128, 128])
    b_s = din("b_s", [NO, 8, 128])
    w_out_c = din("w_out_c", [NO, D, D])
    w_ada = din("w_ada", [DEPTH, D, 6 * D])
    b_ada = din("b_ada", [DEPTH, 6 * D])
    norm_g = din("norm_g", [DEPTH, 2, D])
    w_ff1 = din("w_ff1", [DEPTH, D, 4 * D])
    w_ff2 = din("w_ff2", [DEPTH, 4 * D, D])
    final_g = din("final_g", [D])

    y_p = dout("y_p", [SEQ, D])
    y_s = dout("y_s", [NS * DSQ, D])
    ca_p = dout("ca_p", [NE, 30, 512])
    ca_s = dout("ca_s", [NE, NS, 30, 512])
    cb_p = dout("cb_p", [NE, 2, 512])
    cb_s = dout("cb_s", [NE, NS, 2, 512])
    cv_p = dout("cv_p", [NO, 128, D])
    cv_s = dout("cv_s", [NO, NS * DSQ, D])

    NPB = SEQ // PBT
    TB = PBT + NS * DSQ
    NSEGP = PBT // 512

    stack = ExitStack()
    ARENA = 206 * 1024
    arena = stack.enter_context(nc.sbuf_tensor("arena", [128, ARENA], U8))
    pst = [stack.enter_context(nc.psum_tensor("ps%d" % i, [128, 512], F32)) for i in range(8)]
    PS = [PBank(pst[i], i) for i in range(8)]
    block = stack.enter_context(nc.Block())
    P = Prog()

    astate = {"top": 0}

    def alloc(shape, dt):
        off = (astate["top"] + GRAN - 1) // GRAN * GRAN
        b = Buf(arena, off, shape, dt)
        astate["top"] = off + b.nbytes
        assert astate["top"] <= ARENA, ("arena overflow", astate["top"])
        return b

    def mark():
        return astate["top"]

    def release(m):
        astate["top"] = m

    psn = {"i": 0}

    def psum():
        b = PS[psn["i"] % 5]
        psn["i"] += 1
        return b
    PSN = {"p": PS[6], "s": PS[7]}
    PSW = PS[5]

    def keysof(*vs):
        out = []
        for v in vs:
            if isinstance(v, View):
                out.append(v.keys)
        return out

    def apof(v):
        return v.ap if isinstance(v, View) else v

    def ACT(out, in_, func, bias=0.0, scale=1.0):
        o, i, b, s = out.ap, in_.ap, apof(bias), apof(scale)
        P.add("act", lambda e: e.activation(out=o, in_=i, func=func, bias=b, scale=s),
              reads=keysof(in_, bias, scale), writes=keysof(out))

    def TT(eng, out, in0, in1, op):
        o, a, b = out.ap, in0.ap, in1.ap
        P.add(eng, lambda e: e.tensor_tensor(out=o, in0=a, in1=b, op=op),
              reads=keysof(in0, in1), writes=keysof(out))

    def TS(eng, out, in0, s1, op0, s2=None, op1=None):
        o, a, x1, x2 = out.ap, in0.ap, apof(s1), apof(s2)
        if op1 is None:
            fn = lambda e: e.tensor_scalar(out=o, in0=a, scalar1=x1, scalar2=None, op0=op0)
        else:
            fn = lambda e: e.tensor_scalar(out=o, in0=a, scalar1=x1, scalar2=x2, op0=op0, op1=op1)
        P.add(eng, fn, reads=keysof(in0, s1, s2), writes=keysof(out))

    def STT(out, in0, scalar, in1, op0, op1):
        o, a, s, b = out.ap, in0.ap, apof(scalar), in1.ap
        P.add("dve", lambda e: e.scalar_tensor_tensor(out=o, in0=a, scalar=s, in1=b, op0=op0, op1=op1),
              reads=keysof(in0, scalar, in1), writes=keysof(out))

    def COPY(eng, out, in_):
        o, i = out.ap, in_.ap
        if eng == "act":
            P.add("act", lambda e: e.copy(out=o, in_=i), reads=keysof(in_), writes=keysof(out))
        else:
            P.add(eng, lambda e: e.tensor_copy(out=o, in_=i), reads=keysof(in_), writes=keysof(out))

    def MEMSET(eng, out, val):
        o = out.ap
        P.add(eng, lambda e: e.memset(o, val), writes=keysof(out))

    def RECIP(out, in_):
        o, i = out.ap, in_.ap
        P.add("dve", lambda e: e.reciprocal(out=o, in_=i), reads=keysof(in_), writes=keysof(out))

    def MM(out, pairs, transpose=False):
        o = out.ap
        pl = [(a.ap, b.ap) for a, b in pairs]
        n = len(pl)

        def fn(e):
            inst = None
            for i, (a, b) in enumerate(pl):
                inst = e.matmul(o, a, b, start=(i == 0), stop=(i == n - 1))
            return inst
        rk = []
        for a, b in pairs:
            rk += [a.keys, b.keys]
        P.add("pe", fn, reads=rk, writes=keysof(out))

    def MM1(out, lhsT, rhs, start, stop):
        o, a, b = out.ap, lhsT.ap, rhs.ap
        P.add("pe", lambda e: e.matmul(o, a, b, start=start, stop=stop),
              reads=[lhsT.keys, rhs.keys], writes=keysof(out))

    fresh = set()

    def MMh(out, pairs, fkey):
        if fkey in fresh:
            fresh.discard(fkey)
            n_ = len(pairs)
            for i_, (a_, b_) in enumerate(pairs):
                MM1(out, a_, b_, start=(i_ == 0), stop=(i_ == n_ - 1))
        else:
            MM(out, pairs)

    def TR(out, in_, ident_v):
        o, i, idn = out.ap, in_.ap, ident_v.ap
        P.add("pe", lambda e: e.transpose(o, i, idn), reads=keysof(in_, ident_v), writes=keysof(out))

    outkeys = []
    nout = {"i": 0}

    def DMA(eng, slot, pairs, reads=(), writes=(), is_out=False):
        pl = list(pairs)

        def fn(e, sem):
            for d, s in pl:
                e.dma_start(out=d, in_=s).then_inc(sem, 16)
        w = list(writes)
        if is_out:
            k = ("out", nout["i"])
            nout["i"] += 1
            outkeys.append(k)
            w.append(k)
        P.add(eng, fn, reads=list(reads), writes=w, dma=slot, ninc=len(pl))

    ident = alloc((128,), F32)
    identb = alloc((128,), BF16)
    onesM = alloc((128,), BF16)
    onesA = alloc((128,), BF16)
    colT = alloc((NRT * 128,), F32)
    mods = alloc((DEPTH * 48, 1 + NS), F32)
    cT = alloc((KC, 1 + NS), BF16)
    cwb = alloc((NE * 4, 31), BF16)
    wsT = [alloc((8, 128), BF16) for _ in range(NO)]
    wsTs = [alloc((8, 128), BF16) for _ in range(NO)]
    carry_a = [alloc((4, 30), BF16) for _ in range(NE)]
    carry_b = [alloc((4, 2), F32) for _ in range(NE)]
    xT = alloc((KC, TB), F32)
    h = alloc((KC, TB), BF16)
    mo = alloc((KC, TB), BF16)
    NB = 4
    ring8 = []
    ring32 = []
    for i in range(NB):
        b8 = alloc((8, 512), BF16)
        ring8.append(b8)
        ring32.append(Buf(arena, b8.off, (32, 128), BF16))

    sqs = [alloc((512,), BF16) for _ in range(6)]
    sqc = {"i": 0}
    pending_sq = []

    def sq_accum(sg, k):
        o, n = sg["off"], sg["n"]
        slot = sqs[sqc["i"] % 6]
        sqc["i"] += 1
        ACT(slot[:, 0:n], xT[:, k, o:o + n], AF.Square)
        pending_sq.append((PSN[sg["kind"]][:, 0:n], slot[:, 0:n], k))

    def flush_sq(keep=0):
        while len(pending_sq) > keep:
            pv, sv, k = pending_sq.pop(0)
            MM1(pv, onesM[:, :], sv, start=(k == 0), stop=(k == KC - 1))

    dummy = alloc((8,), F32)
    zpad = alloc((512,), BF16)
    MEMSET("pool", zpad[:, :], 0.0)

    def warm(nmm):
        o, a, b = PSW[:, :].ap, onesM[:, :].ap, zpad[:, :].ap

        def fn(e):
            inst = None
            for _ in range(nmm):
                inst = e.matmul(o, a, b, start=True, stop=True)
            return inst
        P.add("pe", fn, reads=[onesM[:, :].keys, zpad[:, :].keys], writes=[PSW[:, :].keys])

    def preload_ln():
        ACT(dummy[:, 0:1], ident[:, 0:1], AF.Ln, bias=1.0)

    OD_OFF = ARENA - 16 * 1024
    od_bvb = Buf(arena, OD_OFF, (D,), F32)
    od_lng = Buf(arena, OD_OFF + 4096, (D,), F32)
    od_lnb = Buf(arena, OD_OFF + 8192, (D,), F32)
    od_bsb = Buf(arena, OD_OFF + 12288, (8, 128), F32)

    def load_oddc(o_):
        prs = [(od_bvb[:, :].ap, b_in_c[o_:o_ + 1, 1024:2048].partition_broadcast(128)),
               (od_lng[:, :].ap, ln_v_g[o_:o_ + 1, :].partition_broadcast(128)),
               (od_lnb[:, :].ap, ln_v_b[o_:o_ + 1, :].partition_broadcast(128)),
               (od_bsb[:, :, :].r("p a b -> p (a b)").ap,
                b_s[o_:o_ + 1].rearrange("o h t -> o (h t)").partition_broadcast(128))]
        DMA("sp", "oddc", prs, writes=[od_bvb[:, :].keys, od_lng[:, :].keys, od_lnb[:, :].keys, od_bsb[:, :, :].keys])

    def col(r):
        return colT[:, r:r + 1]

    MEMSET("pool", ident[:, :], 1.0)
    iap = ident[:, :]
    P.add("pool", lambda e: e.affine_select(out=iap.ap, in_=iap.ap, pattern=[[-1, 128]],
                                            compare_op=ALU.is_equal, fill=0.0, base=0, channel_multiplier=1),
          reads=[iap.keys], writes=[iap.keys])
    COPY("pool", identb[:, :], ident[:, :])
    MEMSET("pool", onesM[:, :], 1.0 / 1024.0)
    MEMSET("pool", onesA[:, :], 1.0 / 512.0)
    for e_ in range(NE):
        MEMSET("pool", carry_a[e_][:, :, :], 0.0)
        MEMSET("pool", carry_b[e_][:, :, :], 0.0)

    m0 = mark()
    rows = alloc((NRT, 128), F32)
    c_sb = alloc((D,), F32)
    srcs = [
        (R_NG, norm_g.rearrange("l w (k p) -> (l w k) p", p=128)),
        (R_FG, final_g.rearrange("(k p) -> k p", p=128)),
        (R_BADA, b_ada.rearrange("l (c p) -> (l c) p", p=128)),
        (R_CAW, conv_a_w.rearrange("e t (j p) -> (e t j) p", p=128)),
        (R_CAB, conv_a_b.rearrange("e (j p) -> (e j) p", p=128)),
        (R_LAG, ln_a_g.rearrange("e (j p) -> (e j) p", p=128)),
        (R_LAB, ln_a_b.rearrange("e (j p) -> (e j) p", p=128)),
        (R_CBW, conv_b_w.rearrange("e t (j p) -> (e t j) p", p=128)),
        (R_BU, b_in_c.rearrange("o (k p) -> o k p", p=128)[:, 0:8, :].rearrange("o k p -> (o k) p")
         if False else None),
    ]
    pairs = []
    wk = []
    MEMSET("pool", rows[:, :, :], 0.0)
    for base, src in srcs:
        if src is None:
            continue
        n = src.shape[0]
        r = 0
        while r < n:
            g = base + r
            t, pp = g // 128, g % 128
            m = min(n - r, 128 - pp)
            dv = rows[pp:pp + m, t, :]
            pairs.append((dv.ap, src[r:r + m, :]))
            wk.append(dv.keys)
            r += m
    for o_ in range(NO):
        g = R_BU + o_ * 8
        t, pp = g // 128, g % 128
        dv = rows[pp:pp + 8, t, :]
        pairs.append((dv.ap, b_in_c[o_, 0:1024].rearrange("(k p) -> k p", p=128)))
        wk.append(dv.keys)
    DMA("sp", "rows", pairs, writes=wk)
    cv = c_sb[0:1 + NS, :]
    DMA("sp", "cload", [(cv.ap, c17[:, :])], writes=[cv.keys])
    for t in range(NRT):
        pb = psum()
        TR(pb[:, 0:128], rows[:, t, :], ident[:, :])
        COPY("dve", colT[:, t * 128:(t + 1) * 128], pb[:, 0:128])
    for e_ in range(NE):
        for j in range(4):
            a0 = R_CAW + e_ * 124 + j
            src = View(colT.ap[:, a0:a0 + 121:4], colT[:, a0:a0 + 121].keys)
            COPY("dve", cwb[:, e_ * 4 + j, :], src)
    ACT(cv, cv, AF.Silu)
    pb = psum()
    for k in range(KC):
        TR(pb[:, k * 17:(k + 1) * 17], c_sb[0:1 + NS, k * 128:(k + 1) * 128], ident[0:1 + NS, 0:1 + NS])
    COPY("dve", cT[:, :, :], pb[:, 0:KC * 17].r("p (a b) -> p a b", b=17))
    release(m0)

    m0 = mark()
    wsn = alloc((8, 128), F32)
    for o_ in range(NO):
        for samp in (False, True):
            wv = wsn[:, :, :]
            if not samp:
                DMA("sp", "wsn", [(wv.ap, w_s[o_].rearrange("h t s -> t h s"))], writes=[wv.keys])
            else:
                MEMSET("pool", wv, 0.0)
                prs = []
                wks = []
                for n_ in range(NS):
                    dv = wsn[8 * n_:8 * n_ + 8, :, 8 * n_:8 * n_ + 8]
                    prs.append((dv.ap, w_s[o_, :, 0:8, 0:8].rearrange("h t s -> t h s")))
                    wks.append(dv.keys)
                DMA("sp", "wsn", prs, writes=wks)
            P.add("pool", lambda e, a=wv.ap: e.affine_select(out=a, in_=a, pattern=[[0, 8], [-1, 128]],
                                                             compare_op=ALU.is_ge, fill=0.0, base=0,
                                                             channel_multiplier=1),
                  reads=[wv.keys], writes=[wv.keys])
            dst = wsTs[o_] if samp else wsT[o_]
            for hh in range(2):
                pb = psum()
                for q in range(4):
                    TR(pb[:, q * 128:(q + 1) * 128], wsn[:, hh * 4 + q, :], ident[:, :])
                COPY("dve", dst[:, hh * 4:hh * 4 + 4, :], pb[:, :].r("p (a b) -> p a b", b=128))
    release(m0)

    blocks = []
    for b in range(NPB):
        segs = []
        for s in range(NSEGP):
            segs.append(dict(kind="p", off=s * 512, n=512, tok0=b * PBT + s * 512,
                             last=(b == NPB - 1 and s == NSEGP - 1)))
        if b == 0:
            segs.append(dict(kind="s", off=PBT, n=NS * DSQ, tok0=0, last=False))
        blocks.append(dict(b=b, segs=segs))

    tasks = []

    def wdma_cols(W2d, colchunks, nk):
        Wv = W2d.rearrange("(k p) n -> p k n", p=128)

        def f(slot):
            rb = ring8[slot] if nk == 8 else ring32[slot]
            prs = []
            wks = []
            i = 0
            while i < len(colchunks):
                j = i
                while j + 1 < len(colchunks) and colchunks[j + 1] == colchunks[j] + 1:
                    j += 1
                dv = rb[:, :, i * 128:(j + 1) * 128]
                prs.append((dv.ap, Wv[:, :, colchunks[i] * 128:(colchunks[j] + 1) * 128]))
                wks.append(dv.keys)
                i = j + 1
            DMA("pool", "ring%d" % slot, prs, writes=wks)
        return f

    def mods_idx(l, m, k):
        return (l * 6 + m) * 8 + k

    def mod_scalar(l, m, k):
        return mods[:, mods_idx(l, m, k), 0:1]

    def mod_bc(l, m, k):
        return mods[:, mods_idx(l, m, k), 1:1 + NS].un(2).bc([128, NS, DSQ])

    def ada_tasks(l):
        out = []
        for g in range(12):
            def run(slot, g=g):
                wt = ring8[slot]
                pb = psum()
                for oc in range(4):
                    MM(pb[:, oc * 17:(oc + 1) * 17],
                       [(wt[:, k, oc * 128:(oc + 1) * 128], cT[:, k, :]) for k in range(KC)])
                base = l * 48 + g * 4
                bb = colT[:, R_BADA + base:R_BADA + base + 4].un(2).bc([128, 4, 17])
                TT("dve", mods[:, base:base + 4, :], pb[:, 0:68].r("p (a b) -> p a b", b=17), bb, ALU.add)
                if g == 11:
                    for w_, m in ((0, 1), (1, 4)):
                        i0 = mods_idx(l, m, 0)
                        sc = mods[:, i0:i0 + 8, :]
                        TS("dve", sc, sc, 1.0, ALU.add)
                        gb = colT[:, R_NG + l * 16 + w_ * 8:R_NG + l * 16 + w_ * 8 + 8].un(2).bc([128, 8, 17])
                        TT("dve", sc, sc, gb, ALU.mult)
            out.append((wdma_cols(w_ada[l], [g * 4 + i for i in range(4)], 8), run))
        return out

    def norm_mod(blk, l, w_):
        m_sh, m_sc = (0, 1) if w_ == 0 else (3, 4)
        mk = mark()
        rs = alloc((TB,), F32)
        tmp = [alloc((512,), F32) for _ in range(3)]
        ti = 0
        flush_sq()
        warm(WARM_N)
        for sg in blk["segs"]:
            fresh.add(sg["off"])
        for sg in blk["segs"]:
            o, n = sg["off"], sg["n"]
            pb = PSN[sg["kind"]]
            ACT(rs[:, o:o + n], pb[:, 0:n], AF.Ln, bias=EPS)
            ACT(rs[:, o:o + n], rs[:, o:o + n], AF.Exp, scale=-0.5)
            for k in range(KC):
                t = tmp[ti % 3]
                ti += 1
                if sg["kind"] == "p":
                    STT(t[:, 0:n], xT[:, k, o:o + n], mod_scalar(l, m_sc, k), rs[:, o:o + n], ALU.mult, ALU.mult)
                    ACT(h[:, k, o:o + n], t[:, 0:n], AF.Identity, bias=mod_scalar(l, m_sh, k))
                else:
                    t3 = t[:, 0:n].r("p (a b) -> p a b", b=DSQ)
                    TT("dve", t[:, 0:n], xT[:, k, o:o + n], rs[:, o:o + n], ALU.mult)
                    TT("dve", t3, t3, mod_bc(l, m_sc, k), ALU.mult)
                    TT("dve", h[:, k, o:o + n].r("p (a b) -> p a b", b=DSQ), t3, mod_bc(l, m_sh, k), ALU.add)
        release(mk)

    def residual(sg, k, pbv, l, m_g, tmpb):
        o, n = sg["off"], sg["n"]
        if sg["kind"] == "p":
            STT(xT[:, k, o:o + n], pbv, mod_scalar(l, m_g, k), xT[:, k, o:o + n], ALU.mult, ALU.add)
        else:
            t3 = tmpb[:, 0:n].r("p (a b) -> p a b", b=DSQ)
            TT("dve", t3, pbv.r("p (a b) -> p a b", b=DSQ), mod_bc(l, m_g, k), ALU.mult)
            TT("dve", xT[:, k, o:o + n], xT[:, k, o:o + n], tmpb[:, 0:n], ALU.add)
        sq_accum(sg, k)

    def proj_run(blk, wt, nk, noc, rhs, evac, from_h=False):
        for oc in range(noc):
            for sg in blk["segs"]:
                pb = psum()
                n = sg["n"]
                prs_ = [(wt[:, k, oc * 128:(oc + 1) * 128], rhs(k, sg)) for k in range(nk)]
                if from_h:
                    MMh(pb[:, 0:n], prs_, sg["off"])
                else:
                    MM(pb[:, 0:n], prs_)
                flush_sq()
                evac(oc, sg, pb[:, 0:n])

    def even_mixer_tasks(blk, l, st):
        e_ = l // 2
        b = blk["b"]
        lastblk = (b == NPB - 1)
        T = []
        W = w_in_ab[e_]

        def pre(slot_unused=None):
            st["mk"] = mark()
            st["aext"] = alloc((4, 30 + PBT), BF16)
            st["asx"] = alloc((4, NS, 38), BF16)
            st["af32"] = alloc((4, 128), F32)
            st["sig"] = [alloc((512,), F32) for _ in range(2)]
            st["acc"] = alloc((4, TB), F32)
            st["accb"] = alloc((4, TB), BF16)
            st["sqb"] = alloc((4, TB), BF16)
            st["bxe"] = alloc((4, 2 + PBT), F32)
            st["bxs"] = alloc((4, NS, 10), F32)
            st["cb"] = alloc((4, TB), F32)
            st["dg"] = alloc((2, 31, 128), BF16)
            st["mean"] = alloc((TB,), F32)
            st["var"] = alloc((TB,), F32)
            st["rstd"] = alloc((TB,), F32)
            st["t1"] = [alloc((512,), F32) for _ in range(2)]
            st["sto"] = alloc((512,), F32)
            st["ctr"] = 0
            aext, bxe, asx, bxs = st["aext"], st["bxe"], st["asx"], st["bxs"]
            COPY("pool", aext[:, :, 0:30], carry_a[e_][:, :, :])
            COPY("pool", bxe[:, :, 0:2], carry_b[e_][:, :, :])
            if b == 0:
                hs = alloc((4, 512), F32)
                hb = alloc((512,), F32)
                prs, wks = [], []
                for t in range(4):
                    dv = hs[0:120, t, :]
                    prs.append((dv.ap, sa[e_, 4 * t:4 * t + 4].rearrange("n r c -> (n r) c")))
                    wks.append(dv.keys)
                dvb = hb[0:32, :]
                prs.append((dvb.ap, sb[e_].rearrange("n r c -> (n r) c")))
                wks.append(dvb.keys)
                DMA("sp", "hist", prs, writes=wks)
                for t in range(4):
                    pb = psum()
                    for j in range(4):
                        TR(pb[:, j * 120:(j + 1) * 120], hs[0:120, t, j * 128:(j + 1) * 128], ident[0:120, 0:120])
                    for j in range(4):
                        COPY("act", asx[:, j, 4 * t:4 * t + 4, 0:30],
                             pb[:, j * 120:(j + 1) * 120].r("p (a b) -> p a b", b=30))
                pb = psum()
                for j in range(4):
                    TR(pb[:, j * 32:(j + 1) * 32], hb[0:32, j * 128:(j + 1) * 128], ident[0:32, 0:32])
                for j in range(4):
                    COPY("act", bxs[:, j, :, 0:2], pb[:, j * 32:(j + 1) * 32].r("p (a b) -> p a b", b=2))
                prs2, rks = [], []
                for t in range(4):
                    for q in range(4):
                        sv2 = hs[30 * q + 8:30 * q + 30, t, :]
                        prs2.append((ca_s[e_, 4 * t + q, 0:22, :], sv2.ap))
                        rks.append(sv2.keys)
                DMA("sp", "histo", prs2, reads=rks, is_out=True)

        for g in range(2):
            def runA(slot, g=g):
                if g == 0:
                    pre()
                wt = ring8[slot]
                aext, asx, af32 = st["aext"], st["asx"], st["af32"]
                dg = st["dg"]
                for jj in range(2):
                    j = 2 * g + jj
                    for sg in blk["segs"]:
                        o, n = sg["off"], sg["n"]
                        pv, pg = psum(), psum()
                        MMh(pv[:, 0:n], [(wt[:, k, (2 * jj) * 128:(2 * jj + 1) * 128], h[:, k, o:o + n]) for k in range(KC)], o)
                        MM(pg[:, 0:n], [(wt[:, k, (2 * jj + 1) * 128:(2 * jj + 2) * 128], h[:, k, o:o + n]) for k in range(KC)])
                        sgb = st["sig"][st["ctr"] % 2]
                        st["ctr"] += 1
                        ACT(sgb[:, 0:n], pg[:, 0:n], AF.Sigmoid)
                        if sg["kind"] == "p":
                            TT("dve", aext[:, j, 30 + o:30 + o + n], pv[:, 0:n], sgb[:, 0:n], ALU.mult)
                            if sg["last"]:
                                TT("dve", af32[:, j, 0:30], pv[:, n - 30:n], sgb[:, n - 30:n], ALU.mult)
                        else:
                            TT("dve", asx[:, j, :, 30:38], pv[:, 0:n].r("p (a b) -> p a b", b=DSQ),
                               sgb[:, 0:n].r("p (a b) -> p a b", b=DSQ), ALU.mult)
                            TT("dve", af32[:, j, 0:n], pv[:, 0:n], sgb[:, 0:n], ALU.mult)
                    TT("dve", dg[:, jj, :, :], identb[:, :].un(1).bc([128, 31, 128]),
                       cwb[:, e_ * 4 + j, :].un(2).bc([128, 31, 128]), ALU.mult)

                    def conv(j=j, jj=jj):
                        cbias = col(R_CAB + e_ * 4 + j)
                        for sg in blk["segs"]:
                            o, n = sg["off"], sg["n"]
                            pb = psum()
                            if sg["kind"] == "p":
                                MM(pb[:, 0:n], [(dg[:, jj, tap, :], aext[:, j, o + tap:o + tap + n]) for tap in range(31)])
                            else:
                                MM(pb[:, 0:n], [(dg[:, jj, tap, :], asx[:, j, :, tap:tap + DSQ]) for tap in range(31)])
                            TS("dve", st["acc"][:, j, o:o + n], pb[:, 0:n], cbias, ALU.add)
                            COPY("act", st["accb"][:, j, o:o + n], st["acc"][:, j, o:o + n])
                            ACT(st["sqb"][:, j, o:o + n], st["acc"][:, j, o:o + n], AF.Square)
                    if st.get("pend") is not None:
                        st["pend"]()
                    st["pend"] = conv
                if g == 1:
                    st["pend"]()
                    st["pend"] = None
                if g == 1:
                    for sg in blk["segs"]:
                        o, n = sg["off"], sg["n"]
                        pm, pe2 = psum(), psum()
                        MM(pm[:, 0:n], [(onesA[:, :], st["accb"][:, j2, o:o + n]) for j2 in range(4)])
                        MM(pe2[:, 0:n], [(onesA[:, :], st["sqb"][:, j2, o:o + n]) for j2 in range(4)])
                        mean, var, rstd = st["mean"][:, o:o + n], st["var"][:, o:o + n], st["rstd"][:, o:o + n]
                        COPY("act", mean, pm[:, 0:n])
                        STT(var, mean, -1.0, mean, ALU.mult, ALU.mult)
                        TT("dve", var, pe2[:, 0:n], var, ALU.add)
                        ACT(rstd, var, AF.Ln, bias=EPS)
                        ACT(rstd, rstd, AF.Exp, scale=-0.5)
                        for j2 in range(4):
                            t1 = st["t1"][j2 % 2]
                            TT("dve", t1[:, 0:n], st["acc"][:, j2, o:o + n], mean, ALU.subtract)
                            TT("dve", t1[:, 0:n], t1[:, 0:n], rstd, ALU.mult)
                            ACT(mo[:, j2, o:o + n], t1[:, 0:n], AF.Silu,
                                bias=col(R_LAB + e_ * 4 + j2), scale=col(R_LAG + e_ * 4 + j2))
                    preload_ln()
                    COPY("pool", carry_a[e_][:, :, :], aext[:, :, PBT:PBT + 30])
                    if lastblk:
                        pb = psum()
                        for j2 in range(4):
                            TR(pb[0:30, j2 * 128:(j2 + 1) * 128], af32[:, j2, 0:30], ident[:, :])
                        sv = st["sto"][0:30, :]
                        COPY("dve", sv, pb[0:30, :])
                        DMA("sp", "sto", [(ca_p[e_], sv.ap)], reads=[sv.keys], is_out=True)
                    if b == 0:
                        pb = psum()
                        for j2 in range(4):
                            TR(pb[:, j2 * 128:(j2 + 1) * 128], af32[:, j2, 0:128], ident[:, :])
                        sv = st["sto"][:, :]
                        COPY("dve", sv, pb[:, :])
                        DMA("sp", "sto", [(ca_s[e_, n_, 22:30, :], st["sto"][8 * n_:8 * n_ + 8, :].ap) for n_ in range(NS)],
                            reads=[sv.keys], is_out=True)
            T.append((wdma_cols(W, [2 * g, 4 + 2 * g, 2 * g + 1, 4 + 2 * g + 1], 8), runA))
        for g in range(2):
            def runB(slot, g=g):
                wt = ring8[slot]
                bxe, bxs, cb = st["bxe"], st["bxs"], st["cb"]
                for jj in range(2):
                    j = 2 * g + jj
                    for sg in blk["segs"]:
                        o, n = sg["off"], sg["n"]
                        px, pc = psum(), psum()
                        MM(px[:, 0:n], [(wt[:, k, (2 * jj) * 128:(2 * jj + 1) * 128], h[:, k, o:o + n]) for k in range(KC)])
                        MM(pc[:, 0:n], [(wt[:, k, (2 * jj + 1) * 128:(2 * jj + 2) * 128], h[:, k, o:o + n]) for k in range(KC)])
                        sgb = st["sig"][st["ctr"] % 2]
                        st["ctr"] += 1
                        COPY("act", sgb[:, 0:n], px[:, 0:n])
                        w0 = col(R_CBW + e_ * 12 + 0 * 4 + j)
                        w1 = col(R_CBW + e_ * 12 + 1 * 4 + j)
                        w2 = col(R_CBW + e_ * 12 + 2 * 4 + j)
                        if sg["kind"] == "p":
                            TT("dve", bxe[:, j, 2 + o:2 + o + n], pc[:, 0:n], sgb[:, 0:n], ALU.mult)
                            c_ = cb[:, j, o:o + n]
                            TS("dve", c_, bxe[:, j, o:o + n], w0, ALU.mult)
                            STT(c_, bxe[:, j, o + 1:o + 1 + n], w1, c_, ALU.mult, ALU.add)
                            STT(c_, bxe[:, j, o + 2:o + 2 + n], w2, c_, ALU.mult, ALU.add)
                        else:
                            TT("dve", bxs[:, j, :, 2:10], pc[:, 0:n].r("p (a b) -> p a b", b=DSQ),
                               sgb[:, 0:n].r("p (a b) -> p a b", b=DSQ), ALU.mult)
                            c_ = cb[:, j, o:o + n].r("p (a b) -> p a b", b=DSQ)
                            TS("dve", c_, bxs[:, j, :, 0:8], w0, ALU.mult)
                            STT(c_, bxs[:, j, :, 1:9], w1, c_, ALU.mult, ALU.add)
                            STT(c_, bxs[:, j, :, 2:10], w2, c_, ALU.mult, ALU.add)
                if g == 1:
                    COPY("pool", carry_b[e_][:, :, :], bxe[:, :, PBT:PBT + 2])
                    if lastblk:
                        t1 = st["t1"][0]
                        COPY("dve", t1[:, 0:8].r("p (a b) -> p a b", b=2), bxe[:, :, PBT:PBT + 2])
                        pb = psum()
                        for j2 in range(4):
                            TR(pb[0:2, j2 * 128:(j2 + 1) * 128], t1[:, 2 * j2:2 * j2 + 2], ident[:, :])
                        sv = st["sto"][0:2, :]
                        COPY("dve", sv, pb[0:2, :])
                        DMA("sp", "sto", [(cb_p[e_], sv.ap)], reads=[sv.keys], is_out=True)
                    if b == 0:
                        t1 = st["t1"][1]
                        pb = psum()
                        for j2 in range(4):
                            COPY("dve", t1[:, 32 * j2:32 * j2 + 32].r("p (a b) -> p a b", b=2), bxs[:, j2, :, 8:10])
                            TR(pb[0:32, j2 * 128:(j2 + 1) * 128], t1[:, 32 * j2:32 * j2 + 32], ident[:, :])
                        sv = st["sto"][0:32, :]
                        COPY("dve", sv, pb[0:32, :])
                        DMA("sp", "sto", [(cb_s[e_].rearrange("n r c -> (n r) c"), sv.ap)], reads=[sv.keys], is_out=True)
            T.append((wdma_cols(W, [8 + 2 * g, 16 + 2 * g, 8 + 2 * g + 1, 16 + 2 * g + 1], 8), runB))

        def runBB(slot):
            wt = ring8[slot]

            def ev(oc, sg, pv):
                o, n = sg["off"], sg["n"]
                TT("dve", mo[:, 4 + oc, o:o + n], pv, st["cb"][:, oc, o:o + n], ALU.mult)
            proj_run(blk, wt, KC, 4, lambda k, sg: h[:, k, sg["off"]:sg["off"] + sg["n"]], ev)
        T.append((wdma_cols(W, [12, 13, 14, 15], 8), runBB))
        for g in range(2):
            def runO(slot, g=g):
                wt = ring8[slot]
                tmpb = st["t1"][0]

                def ev(oc, sg, pv):
                    residual(sg, g * 4 + oc, pv, l, 2, tmpb)
                proj_run(blk, wt, KC, 4, lambda k, sg: mo[:, k, sg["off"]:sg["off"] + sg["n"]], ev)
                if g == 1:
                    release(st["mk"])
            T.append((wdma_cols(w_out_ab[e_], [4 * g + i for i in range(4)], 8), runO))
        return T

    def odd_mixer_tasks(blk, l, st):
        o_ = l // 2
        b = blk["b"]
        lastblk = (b == NPB - 1)
        T = []
        W = w_in_c[o_]
        ntile = sum(sg["n"] for sg in blk["segs"]) // 128

        def pre():
            st["mk"] = mark()
            st["bvb"], st["lng"], st["lnb"], st["bsb"] = od_bvb, od_lng, od_lnb, od_bsb
            assert astate["top"] + 70 * 1024 < OD_OFF
            st["vraw"] = [alloc((D,), F32) for _ in range(ntile)]
            st["vnb"] = alloc((ntile, D), BF16)
            st["stat"] = alloc((ntile, 16), F32)
            st["ubuf"] = alloc((8, TB), F32)
            st["tmp"] = [alloc((512,), F32) for _ in range(2)]
            st["ctr"] = 0

        def tile_info(ti):
            c = 0
            for sg in blk["segs"]:
                if ti * 128 < c + sg["n"]:
                    return sg, ti * 128
                c += sg["n"]
            raise ValueError

        for g in range(2):
            def runV(slot, g=g):
                if g == 0:
                    pre()
                wt = ring8[slot]
                for ti in range(ntile):
                    pb = psum()
                    o = ti * 128
                    MMh(pb[:, :], [(h[:, k, o:o + 128], wt[:, k, :]) for k in range(KC)], o)
                    vr = st["vraw"][ti]
                    TT("dve", vr[:, g * 512:(g + 1) * 512], pb[:, :], st["bvb"][:, g * 512:(g + 1) * 512], ALU.add)
                    ACT(vr[:, g * 512:(g + 1) * 512], vr[:, g * 512:(g + 1) * 512], AF.Gelu_apprx_tanh)
                if g == 1:
                    sta = st["stat"]
                    for ti in range(ntile):
                        vr = st["vraw"][ti]
                        for hh in range(2):
                            sv_, vv_ = sta[:, ti, hh * 6:hh * 6 + 6], vr[:, hh * 512:(hh + 1) * 512]
                            P.add("dve", lambda e, a=sv_.ap, c=vv_.ap: e.bn_stats(out=a, in_=c),
                                  reads=[vv_.keys], writes=[sv_.keys])
                        mv = sta[:, ti, 12:14]
                        s12 = sta[:, ti, 0:12]
                        P.add("dve", lambda e, a=mv.ap, c=s12.ap: e.bn_aggr(out=a, in_=c),
                              reads=[s12.keys], writes=[mv.keys])
                    ACT(sta[:, :, 14:15], sta[:, :, 13:14], AF.Sqrt, bias=EPS)
                    RECIP(sta[:, :, 14:15], sta[:, :, 14:15])
                    for ti in range(ntile):
                        vr = st["vraw"][ti]
                        STT(vr[:, :], vr[:, :], sta[:, ti, 12:13], st["lng"][:, :], ALU.subtract, ALU.mult)
                        STT(vr[:, :], vr[:, :], sta[:, ti, 14:15], st["lnb"][:, :], ALU.mult, ALU.add)
                        COPY("act", st["vnb"][:, ti, :], vr[:, :])
                        sg, _ = tile_info(ti)
                        if sg["kind"] == "s":
                            DMA("sp", "cvs", [(cv_s[o_], vr[:, :].ap)], reads=[vr[:, :].keys], is_out=True)
                        elif lastblk and ti == ntile - 1:
                            DMA("sp", "cvp", [(cv_p[o_], vr[:, :].ap)], reads=[vr[:, :].keys], is_out=True)
            T.append((wdma_cols(W, [8 + 4 * g + i for i in range(4)], 8), runV))
        for g in range(2):
            def runU(slot, g=g):
                wt = ring8[slot]
                for oc in range(4):
                    j = 4 * g + oc
                    for sg in blk["segs"]:
                        o, n = sg["off"], sg["n"]
                        pu = psum()
                        MM(pu[:, 0:n], [(wt[:, k, oc * 128:(oc + 1) * 128], h[:, k, o:o + n]) for k in range(KC)])
                        ACT(st["ubuf"][:, j, o:o + n], pu[:, 0:n], AF.Gelu_apprx_tanh, bias=col(R_BU + o_ * 8 + j))
                if g == 1:
                    preload_ln()
            T.append((wdma_cols(W, [4 * g + i for i in range(4)], 8), runU))

        def gate():
            for j in range(8):
                for sg in blk["segs"]:
                    o, n = sg["off"], sg["n"]
                    pg = psum()
                    for cc in range(n // 128):
                        ti = (o + cc * 128) // 128
                        rhs = wsTs[o_][:, j, :] if sg["kind"] == "s" else wsT[o_][:, j, :]
                        MM(pg[:, cc * 128:(cc + 1) * 128], [(st["vnb"][:, ti, j * 128:(j + 1) * 128], rhs)])
                    tb = st["tmp"][st["ctr"] % 2]
                    st["ctr"] += 1
                    if sg["kind"] == "p":
                        nb4 = n // 128
                        bsv = st["bsb"][:, j, :].un(1).bc([128, nb4, 128])
                        TT("dve", tb[:, 0:n].r("p (a b) -> p a b", b=128), pg[:, 0:n].r("p (a b) -> p a b", b=128),
                           bsv, ALU.add)
                    else:
                        bsv = st["bsb"][:, j, 0:DSQ].un(1).bc([128, NS, DSQ])
                        TT("dve", tb[:, 0:n].r("p (a b) -> p a b", b=DSQ), pg[:, 0:n].r("p (a b) -> p a b", b=DSQ),
                           bsv, ALU.add)
                    TT("dve", mo[:, j, o:o + n], tb[:, 0:n], st["ubuf"][:, j, o:o + n], ALU.mult)
        T.append((None, lambda slot: gate()))
        for g in range(2):
            def runO(slot, g=g):
                wt = ring8[slot]
                tmpb = st["tmp"][0]

                def ev(oc, sg, pv):
                    residual(sg, g * 4 + oc, pv, l, 2, tmpb)
                proj_run(blk, wt, KC, 4, lambda k, sg: mo[:, k, sg["off"]:sg["off"] + sg["n"]], ev)
                if g == 1:
                    release(st["mk"])
            T.append((wdma_cols(w_out_c[o_], [4 * g + i for i in range(4)], 8), runO))
        return T

    def ffn_tasks(blk, l, st):
        T = []

        def pre():
            st["mk"] = mark()
            st["f"] = alloc((32, TB), BF16)
            st["rl"] = [alloc((512,), F32) for _ in range(3)]
            st["ctr"] = 0
        for g in range(8):
            def run1(slot, g=g):
                if g == 0:
                    pre()
                    if (l + 1) % 2 == 1 and l + 1 < nlayers:
                        load_oddc((l + 1) // 2)
                wt = ring8[slot]

                def ev(oc, sg, pv):
                    o, n = sg["off"], sg["n"]
                    r = st["rl"][st["ctr"] % 3]
                    st["ctr"] += 1
                    ACT(r[:, 0:n], pv, AF.Relu)
                    TT("dve", st["f"][:, g * 4 + oc, o:o + n], r[:, 0:n], r[:, 0:n], ALU.mult)
                proj_run(blk, wt, KC, 4, lambda k, sg: h[:, k, sg["off"]:sg["off"] + sg["n"]], ev, from_h=True)
                if g == 7:
                    preload_ln()
            T.append((wdma_cols(w_ff1[l], [4 * g + i for i in range(4)], 8), run1))
        for g in range(8):
            def run2(slot, g=g):
                wt = ring32[slot]
                tmpb = st["rl"][0]

                def ev(oc, sg, pv):
                    residual(sg, g, pv, l, 5, tmpb)
                proj_run(blk, wt, 32, 1, lambda k, sg: st["f"][:, k, sg["off"]:sg["off"] + sg["n"]], ev)
                if g == 7:
                    release(st["mk"])
            T.append((wdma_cols(w_ff2[l], [g], 32), run2))
        return T

    def load_block(blk):
        mk = mark()
        stg = [alloc((D,), F32) for _ in range(2)]
        i = 0
        for sg in blk["segs"]:
            for tt in range(sg["n"] // 128):
                s = stg[i % 2]
                sv = s[:, :]
                if sg["kind"] == "p":
                    src = x_p[sg["tok0"] + tt * 128:sg["tok0"] + (tt + 1) * 128, :]
                else:
                    src = x_s[:, :]
                DMA("sp", "xs%d" % (i % 2), [(sv.ap, src)], writes=[sv.keys])
                o = sg["off"] + tt * 128
                for hh in range(2):
                    pb = psum()
                    for q in range(4):
                        k = hh * 4 + q
                        TR(pb[:, q * 128:(q + 1) * 128], s[:, k * 128:(k + 1) * 128], ident[:, :])
                    COPY("act", xT[:, hh * 4:hh * 4 + 4, o:o + 128], pb[:, :].r("p (a b) -> p a b", b=128))
                i += 1
        for sg in blk["segs"]:
            for k in range(KC):
                sq_accum(sg, k)
                flush_sq(keep=2)
        release(mk)

    def store_block(blk):
        mk = mark()
        stg = [alloc((D,), F32) for _ in range(2)]
        rs = alloc((TB,), F32)
        i = 0
        flush_sq()
        for sg in blk["segs"]:
            o, n = sg["off"], sg["n"]
            if final_norm:
                pb = PSN[sg["kind"]]
                ACT(rs[:, o:o + n], pb[:, 0:n], AF.Ln, bias=EPS)
                ACT(rs[:, o:o + n], rs[:, o:o + n], AF.Exp, scale=-0.5)
                for k in range(KC):
                    STT(xT[:, k, o:o + n], xT[:, k, o:o + n], col(R_FG + k), rs[:, o:o + n], ALU.mult, ALU.mult)
            for tt in range(n // 128):
                s = stg[i % 2]
                oo = o + tt * 128
                for hh in range(2):
                    pb = psum()
                    for q in range(4):
                        k = hh * 4 + q
                        TR(pb[:, q * 128:(q + 1) * 128], xT[:, k, oo:oo + 128], ident[:, :])
                    COPY("act" if hh == 0 else "dve", s[:, hh * 512:(hh + 1) * 512], pb[:, :])
                if sg["kind"] == "p":
                    dst = y_p[sg["tok0"] + tt * 128:sg["tok0"] + (tt + 1) * 128, :]
                else:
                    dst = y_s[:, :]
                DMA("sp", "ys%d" % (i % 2), [(dst, s[:, :].ap)], reads=[s[:, :].keys], is_out=True)
                i += 1
        release(mk)

    def T_plain(fn):
        return (None, lambda slot: fn())

    if nlayers > 0:
        tasks += ada_tasks(0)
    for blk in blocks:
        tasks.append(T_plain(lambda blk=blk: load_block(blk)))
        for l in range(nlayers):
            st = {}
            tasks.append(T_plain(lambda blk=blk, l=l: norm_mod(blk, l, 0)))
            if l % 2 == 0:
                tasks += even_mixer_tasks(blk, l, st)
            else:
                tasks += odd_mixer_tasks(blk, l, st)
            tasks.append(T_plain(lambda blk=blk, l=l: norm_mod(blk, l, 1)))
            st2 = {}
            ft = ffn_tasks(blk, l, st2)
            if blk["b"] == 0 and l + 1 < nlayers:
                at = ada_tasks(l + 1)
                merged = []
                for i_, t_ in enumerate(ft):
                    merged.append(t_)
                    if i_ < len(at):
                        merged.append(at[i_])
                ft = merged
            tasks += ft
        tasks.append(T_plain(lambda blk=blk: store_block(blk)))

    if max_tasks is not None:
        tasks = tasks[:max_tasks]
    widx = [i for i, t in enumerate(tasks) if t[0] is not None]
    slot_of = {ti: n % NB for n, ti in enumerate(widx)}
    issued = 0
    LOOK = NB - 1
    for n, ti in enumerate(widx[:LOOK]):
        tasks[ti][0](slot_of[ti])
        issued += 1
    wpos = 0
    for i, (dfn, run) in enumerate(tasks):
        if dfn is not None:
            if issued < len(widx):
                tj = widx[issued]
                tasks[tj][0](slot_of[tj])
                issued += 1
            run(slot_of[i])
            wpos += 1
        else:
            run(None)

    P.add("sp", None, reads=outkeys)
    P.emit(nc, block, stack)
    stack.close()
    return nc


_CACHE = {}


def _get_prog(**kw):
    key = tuple(sorted(kw.items()))
    if key not in _CACHE:
        _CACHE[key] = build_program(**kw)
    return _CACHE[key]


def make_in_maps(inputs):
    f = lambda a: np.ascontiguousarray(np.asarray(a, dtype=np.float32))
    shared = {k: f(inputs[k]) for k in
              ["w_in_ab", "conv_a_w", "conv_a_b", "ln_a_g", "ln_a_b", "conv_b_w", "w_out_ab", "w_in_c", "b_in_c",
               "ln_v_g", "ln_v_b", "w_s", "b_s", "w_out_c", "w_ada", "b_ada", "norm_g", "w_ff1", "w_ff2", "final_g"]}
    xp, xs = f(inputs["x_prompt"]), f(inputs["x_sample"])
    sa, sb = f(inputs["state_conv_a"]), f(inputs["state_conv_b"])
    cp, cs = f(inputs["c_prompt"]), f(inputs["c_sample"])
    maps = []
    for c in range(NCORES):
        m = dict(shared)
        m["x_p"] = xp[c]
        m["x_s"] = np.ascontiguousarray(xs[NS * c:NS * (c + 1)].reshape(NS * DSQ, D))
        m["sa"] = np.ascontiguousarray(sa[:, NS * c:NS * (c + 1)])
        m["sb"] = np.ascontiguousarray(sb[:, NS * c:NS * (c + 1)])
        m["c17"] = np.ascontiguousarray(np.concatenate([cp[c:c + 1], cs[NS * c:NS * (c + 1)]], axis=0))
        maps.append(m)
    return maps


def gather(results):
    r = results
    y_prompt = np.stack([r[c]["y_p"] for c in range(NCORES)], axis=0)
    y_sample = np.concatenate([r[c]["y_s"].reshape(NS, DSQ, D) for c in range(NCORES)], axis=0)
    ca_p = np.stack([r[c]["ca_p"] for c in range(NCORES)], axis=1)
    ca_s = np.concatenate([r[c]["ca_s"] for c in range(NCORES)], axis=1)
    cb_p = np.stack([r[c]["cb_p"] for c in range(NCORES)], axis=1)
    cb_s = np.concatenate([r[c]["cb_s"] for c in range(NCORES)], axis=1)
    cv_p = np.stack([r[c]["cv_p"] for c in range(NCORES)], axis=1)
    cv_s = np.concatenate([r[c]["cv_s"].reshape(2, NS, DSQ, D) for c in range(NCORES)], axis=1)
    return tuple(np.ascontiguousarray(a, dtype=np.float32) for a in
                 (y_prompt, y_sample, ca_p, ca_s, cb_p, cb_s, cv_p, cv_s))


def kernel(**inputs):
    nc = _get_prog()
    res = run_bass_kernel_spmd(nc, make_in_maps(inputs), core_ids=list(range(NCORES)))
    return gather(res.results)
```

```python
import numpy as np
from contextlib import ExitStack
import concourse.bass as bass
import concourse.mybir as mybir
from concourse.bass_utils import run_bass_kernel_spmd

F32 = mybir.dt.float32
BF16 = mybir.dt.bfloat16
U8 = mybir.dt.uint8
AF = mybir.ActivationFunctionType
ALU = mybir.AluOpType

D = 1024
KC = 8
SEQ = 2048
NS = 16
DSQ = 8
DEPTH = 4
EPS = 1e-6
GRAN = 256
WARM_N = 14
NCORES = 8

R_NG = 0
R_FG = 64
R_BADA = 72
R_CAW = 264
R_CAB = 512
R_LAG = 520
R_LAB = 528
R_CBW = 536
R_BU = 560
R_TOT = 576
NRT = 5


class View:
    __slots__ = ("ap", "keys")

    def __init__(self, ap, keys):
        self.ap = ap
        self.keys = keys

    def r(self, pat, **kw):
        return View(self.ap.rearrange(pat, **kw), self.keys)

    def bc(self, shape):
        return View(self.ap.to_broadcast(list(shape)), self.keys)

    def un(self, ax):
        return View(self.ap.unsqueeze(ax), self.keys)


class Buf:
    def __init__(self, arena, off, shape, dt):
        self.off = off
        self.shape = tuple(shape)
        self.dt = dt
        self.es = 4 if dt == F32 else 2
        n = 1
        for s in shape:
            n *= s
        self.nbytes = n * self.es
        flat = arena[:, off:off + self.nbytes].bitcast(dt)
        if len(shape) == 1:
            self.ap = flat
        elif len(shape) == 2:
            self.ap = flat.rearrange("p (a b) -> p a b", a=shape[0])
        elif len(shape) == 3:
            self.ap = flat.rearrange("p (a b c) -> p a b c", a=shape[0], b=shape[1])
        else:
            raise ValueError
        st = []
        acc = 1
        for s in reversed(self.shape):
            st.append(acc)
            acc *= s
        self.strides = tuple(reversed(st))

    def __getitem__(self, idx):
        if not isinstance(idx, tuple):
            idx = (idx,)
        idx = list(idx) + [slice(None)] * (1 + len(self.shape) - len(idx))
        ap = self.ap[tuple(idx)]
        rng = []
        for d, ix in enumerate(idx[1:]):
            if isinstance(ix, int):
                rng.append((ix, ix + 1))
            else:
                a = 0 if ix.start is None else ix.start
                b = self.shape[d] if ix.stop is None else ix.stop
                assert ix.step in (None, 1)
                rng.append((a, b))
        outer = rng[:-1]
        n_outer = 1
        for a, b in outer:
            n_outer *= (b - a)
        keys = set()
        la, lb = rng[-1]
        if n_outer <= 128:
            def rec(d, base):
                if d == len(outer):
                    lo = self.off + (base + la) * self.es
                    hi = self.off + (base + lb) * self.es
                    for g in range(lo // GRAN, (hi - 1) // GRAN + 1):
                        keys.add(g)
                    return
                for i in range(outer[d][0], outer[d][1]):
                    rec(d + 1, base + i * self.strides[d])
            rec(0, 0)
        else:
            lo = self.off + sum(r[0] * s for r, s in zip(rng, self.strides)) * self.es
            hi = self.off + (sum((r[1] - 1) * s for r, s in zip(rng, self.strides)) + 1) * self.es
            for g in range(lo // GRAN, (hi - 1) // GRAN + 1):
                keys.add(g)
        return View(ap, frozenset(keys))


class PBank:
    def __init__(self, t, i):
        self.t = t
        self.i = i
        self.keys = frozenset([("ps", i)])

    def __getitem__(self, idx):
        return View(self.t[idx], self.keys)


class Prog:
    def __init__(self):
        self.ops = []
        self.last_w = {}
        self.readers = {}
        self.slots = {}

    def add(self, eng, fn, reads=(), writes=(), dma=None, ninc=1):
        idx = len(self.ops)
        deps = {}
        rk = set()
        for r in reads:
            rk |= set(r) if isinstance(r, (set, frozenset)) else {r}
        wk = set()
        for w in writes:
            wk |= set(w) if isinstance(w, (set, frozenset)) else {w}
        wk |= {k for k in rk if isinstance(k, tuple) and k[0] == "ps"}
        lw = self.last_w
        rd = self.readers
        for k in rk:
            w = lw.get(k)
            if w is not None:
                deps[w] = "raw"
        for k in wk:
            w = lw.get(k)
            if w is not None and w not in deps:
                deps[w] = "waw"
            for x in rd.get(k, ()):
                if x not in deps:
                    deps[x] = "war"
        need = []
        for p, kind in deps.items():
            P = self.ops[p]
            if P["dma"] is None and dma is None and P["eng"] == eng:
                if eng == "pe":
                    continue
            need.append(p)
            P["signal"] = True
        op = dict(eng=eng, fn=fn, deps=need, dma=dma, ninc=ninc, signal=False)
        self.ops.append(op)
        for k in wk:
            lw[k] = idx
            rd[k] = []
        for k in rk:
            rd.setdefault(k, []).append(idx)
        return idx

    def emit(self, nc, block, stack):
        ENG = ["pe", "act", "dve", "pool", "sp"]
        esem = {e: stack.enter_context(nc.semaphore("S_" + e)) for e in ENG}
        slotsem = {}
        cnt = {e: 0 for e in ENG}
        scnt = {}
        const_total = 16 * sum(o["ninc"] for o in self.ops if o["dma"] == "const")
        for o in self.ops:
            if o["dma"] is None:
                if o["signal"]:
                    cnt[o["eng"]] += 1
                    o["token"] = (esem[o["eng"]], cnt[o["eng"]])
            else:
                s = o["dma"]
                if s not in slotsem:
                    slotsem[s] = stack.enter_context(nc.semaphore("D_" + s))
                    scnt[s] = 0
                scnt[s] += 16 * o["ninc"]
                o["token"] = (slotsem[s], const_total if s == "const" else scnt[s])
                o["sem"] = slotsem[s]
        per = {e: [o for o in self.ops if o["eng"] == e] for e in ENG}
        ops = self.ops

        def runner(e):
            def f(engobj):
                waited = {}
                for o in per[e]:
                    need = {}
                    for p in o["deps"]:
                        sem, val = ops[p]["token"]
                        if need.get(sem, (None, 0))[1] < val:
                            need[sem] = (sem, val)
                    for sem, val in need.values():
                        if waited.get(sem, 0) < val:
                            engobj.wait_ge(sem, val)
                            waited[sem] = val
                    if o["fn"] is None:
                        continue
                    if o["dma"] is not None:
                        o["fn"](engobj, o["sem"])
                    else:
                        inst = o["fn"](engobj)
                        if o["signal"]:
                            inst.then_inc(esem[e], 1)
            return f
        block.tensor(runner("pe"))
        block.scalar(runner("act"))
        block.vector(runner("dve"))
        block.gpsimd(runner("pool"))
        block.sync(runner("sp"))


def build_program(nlayers=DEPTH, PBT=512, final_norm=True, max_tasks=None):
    nc = bass.Bass("TRN2", target_bir_lowering=False)
    NE = 2
    NO = 2

    def din(name, shape):
        return nc.dram_tensor(name, list(shape), F32, kind="ExternalInput").ap()

    def dout(name, shape):
        return nc.dram_tensor(name, list(shape), F32, kind="ExternalOutput").ap()

    x_p = din("x_p", [SEQ, D])
    x_s = din("x_s", [NS * DSQ, D])
    sa = din("sa", [NE, NS, 30, 512])
    sb = din("sb", [NE, NS, 2, 512])
    c17 = din("c17", [1 + NS, D])
    w_in_ab = din("w_in_ab", [NE, D, 2560])
    conv_a_w = din("conv_a_w", [NE, 31, 512])
    conv_a_b = din("conv_a_b", [NE, 512])
    ln_a_g = din("ln_a_g", [NE, 512])
    ln_a_b = din("ln_a_b", [NE, 512])
    conv_b_w = din("conv_b_w", [NE, 3, 512])
    w_out_ab = din("w_out_ab", [NE, D, D])
    w_in_c = din("w_in_c", [NO, D, 2048])
    b_in_c = din("b_in_c", [NO, 2048])
    ln_v_g = din("ln_v_g", [NO, D])
    ln_v_b = din("ln_v_b", [NO, D])
    w_s = din("w_s", [NO, 8, 128, 128])
    b_s = din("b_s", [NO, 8, 128])
    w_out_c = din("w_out_c", [NO, D, D])
    w_ada = din("w_ada", [DEPTH, D, 6 * D])
    b_ada = din("b_ada", [DEPTH, 6 * D])
    norm_g = din("norm_g", [DEPTH, 2, D])
    w_ff1 = din("w_ff1", [DEPTH, D, 4 * D])
    w_ff2 = din("w_ff2", [DEPTH, 4 * D, D])
    final_g = din("final_g", [D])

    y_p = dout("y_p", [SEQ, D])
    y_s = dout("y_s", [NS * DSQ, D])
    ca_p = dout("ca_p", [NE, 30, 512])
    ca_s = dout("ca_s", [NE, NS, 30, 512])
    cb_p = dout("cb_p", [NE, 2, 512])
    cb_s = dout("cb_s", [NE, NS, 2, 512])
    cv_p = dout("cv_p", [NO, 128, D])
    cv_s = dout("cv_s", [NO, NS * DSQ, D])

    NPB = SEQ // PBT
    TB = PBT + NS * DSQ
    NSEGP = PBT // 512

    stack = ExitStack()
    ARENA = 206 * 1024
    arena = stack.enter_context(nc.sbuf_tensor("arena", [128, ARENA], U8))
    pst = [stack.enter_context(nc.psum_tensor("ps%d" % i, [128, 512], F32)) for i in range(8)]
    PS = [PBank(pst[i], i) for i in range(8)]
    block = stack.enter_context(nc.Block())
    P = Prog()

    astate = {"top": 0}

    def alloc(shape, dt):
        off = (astate["top"] + GRAN - 1) // GRAN * GRAN
        b = Buf(arena, off, shape, dt)
        astate["top"] = off + b.nbytes
        assert astate["top"] <= ARENA, ("arena overflow", astate["top"])
        return b

    def mark():
        return astate["top"]

    def release(m):
        astate["top"] = m

    psn = {"i": 0}

    def psum():
        b = PS[psn["i"] % 5]
        psn["i"] += 1
        return b
    PSN = {"p": PS[6], "s": PS[7]}
    PSW = PS[5]

    def keysof(*vs):
        out = []
        for v in vs:
            if isinstance(v, View):
                out.append(v.keys)
        return out

    def apof(v):
        return v.ap if isinstance(v, View) else v

    def ACT(out, in_, func, bias=0.0, scale=1.0):
        o, i, b, s = out.ap, in_.ap, apof(bias), apof(scale)
        P.add("act", lambda e: e.activation(out=o, in_=i, func=func, bias=b, scale=s),
              reads=keysof(in_, bias, scale), writes=keysof(out))

    def TT(eng, out, in0, in1, op):
        o, a, b = out.ap, in0.ap, in1.ap
        P.add(eng, lambda e: e.tensor_tensor(out=o, in0=a, in1=b, op=op),
              reads=keysof(in0, in1), writes=keysof(out))

    def TS(eng, out, in0, s1, op0, s2=None, op1=None):
        o, a, x1, x2 = out.ap, in0.ap, apof(s1), apof(s2)
        if op1 is None:
            fn = lambda e: e.tensor_scalar(out=o, in0=a, scalar1=x1, scalar2=None, op0=op0)
        else:
            fn = lambda e: e.tensor_scalar(out=o, in0=a, scalar1=x1, scalar2=x2, op0=op0, op1=op1)
        P.add(eng, fn, reads=keysof(in0, s1, s2), writes=keysof(out))

    def STT(out, in0, scalar, in1, op0, op1):
        o, a, s, b = out.ap, in0.ap, apof(scalar), in1.ap
        P.add("dve", lambda e: e.scalar_tensor_tensor(out=o, in0=a, scalar=s, in1=b, op0=op0, op1=op1),
              reads=keysof(in0, scalar, in1), writes=keysof(out))

    def COPY(eng, out, in_):
        o, i = out.ap, in_.ap
        if eng == "act":
            P.add("act", lambda e: e.copy(out=o, in_=i), reads=keysof(in_), writes=keysof(out))
        else:
            P.add(eng, lambda e: e.tensor_copy(out=o, in_=i), reads=keysof(in_), writes=keysof(out))

    def MEMSET(eng, out, val):
        o = out.ap
        P.add(eng, lambda e: e.memset(o, val), writes=keysof(out))

    def RECIP(out, in_):
        o, i = out.ap, in_.ap
        P.add("dve", lambda e: e.reciprocal(out=o, in_=i), reads=keysof(in_), writes=keysof(out))

    def MM(out, pairs, transpose=False):
        o = out.ap
        pl = [(a.ap, b.ap) for a, b in pairs]
        n = len(pl)

        def fn(e):
            inst = None
            for i, (a, b) in enumerate(pl):
                inst = e.matmul(o, a, b, start=(i == 0), stop=(i == n - 1))
            return inst
        rk = []
        for a, b in pairs:
            rk += [a.keys, b.keys]
        P.add("pe", fn, reads=rk, writes=keysof(out))

    def MM1(out, lhsT, rhs, start, stop):
        o, a, b = out.ap, lhsT.ap, rhs.ap
        P.add("pe", lambda e: e.matmul(o, a, b, start=start, stop=stop),
              reads=[lhsT.keys, rhs.keys], writes=keysof(out))

    fresh = set()

    def MMh(out, pairs, fkey):
        if fkey in fresh:
            fresh.discard(fkey)
            n_ = len(pairs)
            for i_, (a_, b_) in enumerate(pairs):
                MM1(out, a_, b_, start=(i_ == 0), stop=(i_ == n_ - 1))
        else:
            MM(out, pairs)

    def TR(out, in_, ident_v):
        o, i, idn = out.ap, in_.ap, ident_v.ap
        P.add("pe", lambda e: e.transpose(o, i, idn), reads=keysof(in_, ident_v), writes=keysof(out))

    outkeys = []
    nout = {"i": 0}

    def DMA(eng, slot, pairs, reads=(), writes=(), is_out=False):
        pl = list(pairs)

        def fn(e, sem):
            for d, s in pl:
                e.dma_start(out=d, in_=s).then_inc(sem, 16)
        w = list(writes)
        if is_out:
            k = ("out", nout["i"])
            nout["i"] += 1
            outkeys.append(k)
            w.append(k)
        P.add(eng, fn, reads=list(reads), writes=w, dma=slot, ninc=len(pl))

    ident = alloc((128,), F32)
    identb = alloc((128,), BF16)
    onesM = alloc((128,), BF16)
    onesA = alloc((128,), BF16)
    colT = alloc((NRT * 128,), F32)
    mods = alloc((DEPTH * 48, 1 + NS), F32)
    cT = alloc((KC, 1 + NS), BF16)
    cwb = alloc((NE * 4, 31), BF16)
    wsT = [alloc((8, 128), BF16) for _ in range(NO)]
    wsTs = [alloc((8, 128), BF16) for _ in range(NO)]
    carry_a = [alloc((4, 30), BF16) for _ in range(NE)]
    carry_b = [alloc((4, 2), F32) for _ in range(NE)]
    xT = alloc((KC, TB), F32)
    h = alloc((KC, TB), BF16)
    mo = alloc((KC, TB), BF16)
    NB = 4
    ring8 = []
    ring32 = []
    for i in range(NB):
        b8 = alloc((8, 512), BF16)
        ring8.append(b8)
        ring32.append(Buf(arena, b8.off, (32, 128), BF16))

    sqs = [alloc((512,), BF16) for _ in range(6)]
    sqc = {"i": 0}
    pending_sq = []

    def sq_accum(sg, k):
        o, n = sg["off"], sg["n"]
        slot = sqs[sqc["i"] % 6]
        sqc["i"] += 1
        ACT(slot[:, 0:n], xT[:, k, o:o + n], AF.Square)
        pending_sq.append((PSN[sg["kind"]][:, 0:n], slot[:, 0:n], k))

    def flush_sq(keep=0):
        while len(pending_sq) > keep:
            pv, sv, k = pending_sq.pop(0)
            MM1(pv, onesM[:, :], sv, start=(k == 0), stop=(k == KC - 1))

    dummy = alloc((8,), F32)
    zpad = alloc((512,), BF16)
    MEMSET("pool", zpad[:, :], 0.0)

    def warm(nmm):
        o, a, b = PSW[:, :].ap, onesM[:, :].ap, zpad[:, :].ap

        def fn(e):
            inst = None
            for _ in range(nmm):
                inst = e.matmul(o, a, b, start=True, stop=True)
            return inst
        P.add("pe", fn, reads=[onesM[:, :].keys, zpad[:, :].keys], writes=[PSW[:, :].keys])

    def preload_ln():
        ACT(dummy[:, 0:1], ident[:, 0:1], AF.Ln, bias=1.0)

    OD_OFF = ARENA - 16 * 1024
    od_bvb = Buf(arena, OD_OFF, (D,), F32)
    od_lng = Buf(arena, OD_OFF + 4096, (D,), F32)
    od_lnb = Buf(arena, OD_OFF + 8192, (D,), F32)
    od_bsb = Buf(arena, OD_OFF + 12288, (8, 128), F32)

    def load_oddc(o_):
        prs = [(od_bvb[:, :].ap, b_in_c[o_:o_ + 1, 1024:2048].partition_broadcast(128)),
               (od_lng[:, :].ap, ln_v_g[o_:o_ + 1, :].partition_broadcast(128)),
               (od_lnb[:, :].ap, ln_v_b[o_:o_ + 1, :].partition_broadcast(128)),
               (od_bsb[:, :, :].r("p a b -> p (a b)").ap,
                b_s[o_:o_ + 1].rearrange("o h t -> o (h t)").partition_broadcast(128))]
        DMA("sp", "oddc", prs, writes=[od_bvb[:, :].keys, od_lng[:, :].keys, od_lnb[:, :].keys, od_bsb[:, :, :].keys])

    def col(r):
        return colT[:, r:r + 1]

    MEMSET("pool", ident[:, :], 1.0)
    iap = ident[:, :]
    P.add("pool", lambda e: e.affine_select(out=iap.ap, in_=iap.ap, pattern=[[-1, 128]],
                                            compare_op=ALU.is_equal, fill=0.0, base=0, channel_multiplier=1),
          reads=[iap.keys], writes=[iap.keys])
    COPY("pool", identb[:, :], ident[:, :])
    MEMSET("pool", onesM[:, :], 1.0 / 1024.0)
    MEMSET("pool", onesA[:, :], 1.0 / 512.0)
    for e_ in range(NE):
        MEMSET("pool", carry_a[e_][:, :, :], 0.0)
        MEMSET("pool", carry_b[e_][:, :, :], 0.0)

    m0 = mark()
    rows = alloc((NRT, 128), F32)
    c_sb = alloc((D,), F32)
    srcs = [
        (R_NG, norm_g.rearrange("l w (k p) -> (l w k) p", p=128)),
        (R_FG, final_g.rearrange("(k p) -> k p", p=128)),
        (R_BADA, b_ada.rearrange("l (c p) -> (l c) p", p=128)),
        (R_CAW, conv_a_w.rearrange("e t (j p) -> (e t j) p", p=128)),
        (R_CAB, conv_a_b.rearrange("e (j p) -> (e j) p", p=128)),
        (R_LAG, ln_a_g.rearrange("e (j p) -> (e j) p", p=128)),
        (R_LAB, ln_a_b.rearrange("e (j p) -> (e j) p", p=128)),
        (R_CBW, conv_b_w.rearrange("e t (j p) -> (e t j) p", p=128)),
        (R_BU, b_in_c.rearrange("o (k p) -> o k p", p=128)[:, 0:8, :].rearrange("o k p -> (o k) p")
         if False else None),
    ]
    pairs = []
    wk = []
    MEMSET("pool", rows[:, :, :], 0.0)
    for base, src in srcs:
        if src is None:
            continue
        n = src.shape[0]
        r = 0
        while r < n:
            g = base + r
            t, pp = g // 128, g % 128
            m = min(n - r, 128 - pp)
            dv = rows[pp:pp + m, t, :]
            pairs.append((dv.ap, src[r:r + m, :]))
            wk.append(dv.keys)
            r += m
    for o_ in range(NO):
        g = R_BU + o_ * 8
        t, pp = g // 128, g % 128
        dv = rows[pp:pp + 8, t, :]
        pairs.append((dv.ap, b_in_c[o_, 0:1024].rearrange("(k p) -> k p", p=128)))
        wk.append(dv.keys)
    DMA("sp", "rows", pairs, writes=wk)
    cv = c_sb[0:1 + NS, :]
    DMA("sp", "cload", [(cv.ap, c17[:, :])], writes=[cv.keys])
    for t in range(NRT):
        pb = psum()
        TR(pb[:, 0:128], rows[:, t, :], ident[:, :])
        COPY("dve", colT[:, t * 128:(t + 1) * 128], pb[:, 0:128])
    for e_ in range(NE):
        for j in range(4):
            a0 = R_CAW + e_ * 124 + j
            src = View(colT.ap[:, a0:a0 + 121:4], colT[:, a0:a0 + 121].keys)
            COPY("dve", cwb[:, e_ * 4 + j, :], src)
    ACT(cv, cv, AF.Silu)
    pb = psum()
    for k in range(KC):
        TR(pb[:, k * 17:(k + 1) * 17], c_sb[0:1 + NS, k * 128:(k + 1) * 128], ident[0:1 + NS, 0:1 + NS])
    COPY("dve", cT[:, :, :], pb[:, 0:KC * 17].r("p (a b) -> p a b", b=17))
    release(m0)

    m0 = mark()
    wsn = alloc((8, 128), F32)
    for o_ in range(NO):
        for samp in (False, True):
            wv = wsn[:, :, :]
            if not samp:
                DMA("sp", "wsn", [(wv.ap, w_s[o_].rearrange("h t s -> t h s"))], writes=[wv.keys])
            else:
                MEMSET("pool", wv, 0.0)
                prs = []
                wks = []
                for n_ in range(NS):
                    dv = wsn[8 * n_:8 * n_ + 8, :, 8 * n_:8 * n_ + 8]
                    prs.append((dv.ap, w_s[o_, :, 0:8, 0:8].rearrange("h t s -> t h s")))
                    wks.append(dv.keys)
                DMA("sp", "wsn", prs, writes=wks)
            P.add("pool", lambda e, a=wv.ap: e.affine_select(out=a, in_=a, pattern=[[0, 8], [-1, 128]],
                                                             compare_op=ALU.is_ge, fill=0.0, base=0,
                                                             channel_multiplier=1),
                  reads=[wv.keys], writes=[wv.keys])
            dst = wsTs[o_] if samp else wsT[o_]
            for hh in range(2):
                pb = psum()
                for q in range(4):
                    TR(pb[:, q * 128:(q + 1) * 128], wsn[:, hh * 4 + q, :], ident[:, :])
                COPY("dve", dst[:, hh * 4:hh * 4 + 4, :], pb[:, :].r("p (a b) -> p a b", b=128))
    release(m0)

    blocks = []
    for b in range(NPB):
        segs = []
        for s in range(NSEGP):
            segs.append(dict(kind="p", off=s * 512, n=512, tok0=b * PBT + s * 512,
                             last=(b == NPB - 1 and s == NSEGP - 1)))
        if b == 0:
            segs.append(dict(kind="s", off=PBT, n=NS * DSQ, tok0=0, last=False))
        blocks.append(dict(b=b, segs=segs))

    tasks = []

    def wdma_cols(W2d, colchunks, nk):
        Wv = W2d.rearrange("(k p) n -> p k n", p=128)

        def f(slot):
            rb = ring8[slot] if nk == 8 else ring32[slot]
            prs = []
            wks = []
            i = 0
            while i < len(colchunks):
                j = i
                while j + 1 < len(colchunks) and colchunks[j + 1] == colchunks[j] + 1:
                    j += 1
                dv = rb[:, :, i * 128:(j + 1) * 128]
                prs.append((dv.ap, Wv[:, :, colchunks[i] * 128:(colchunks[j] + 1) * 128]))
                wks.append(dv.keys)
                i = j + 1
            DMA("pool", "ring%d" % slot, prs, writes=wks)
        return f

    def mods_idx(l, m, k):
        return (l * 6 + m) * 8 + k

    def mod_scalar(l, m, k):
        return mods[:, mods_idx(l, m, k), 0:1]

    def mod_bc(l, m, k):
        return mods[:, mods_idx(l, m, k), 1:1 + NS].un(2).bc([128, NS, DSQ])

    def ada_tasks(l):
        out = []
        for g in range(12):
            def run(slot, g=g):
                wt = ring8[slot]
                pb = psum()
                for oc in range(4):
                    MM(pb[:, oc * 17:(oc + 1) * 17],
                       [(wt[:, k, oc * 128:(oc + 1) * 128], cT[:, k, :]) for k in range(KC)])
                base = l * 48 + g * 4
                bb = colT[:, R_BADA + base:R_BADA + base + 4].un(2).bc([128, 4, 17])
                TT("dve", mods[:, base:base + 4, :], pb[:, 0:68].r("p (a b) -> p a b", b=17), bb, ALU.add)
                if g == 11:
                    for w_, m in ((0, 1), (1, 4)):
                        i0 = mods_idx(l, m, 0)
                        sc = mods[:, i0:i0 + 8, :]
                        TS("dve", sc, sc, 1.0, ALU.add)
                        gb = colT[:, R_NG + l * 16 + w_ * 8:R_NG + l * 16 + w_ * 8 + 8].un(2).bc([128, 8, 17])
                        TT("dve", sc, sc, gb, ALU.mult)
            out.append((wdma_cols(w_ada[l], [g * 4 + i for i in range(4)], 8), run))
        return out

    def norm_mod(blk, l, w_):
        m_sh, m_sc = (0, 1) if w_ == 0 else (3, 4)
        mk = mark()
        rs = alloc((TB,), F32)
        tmp = [alloc((512,), F32) for _ in range(3)]
        ti = 0
        flush_sq()
        warm(WARM_N)
        for sg in blk["segs"]:
            fresh.add(sg["off"])
        for sg in blk["segs"]:
            o, n = sg["off"], sg["n"]
            pb = PSN[sg["kind"]]
            ACT(rs[:, o:o + n], pb[:, 0:n], AF.Ln, bias=EPS)
            ACT(rs[:, o:o + n], rs[:, o:o + n], AF.Exp, scale=-0.5)
            for k in range(KC):
                t = tmp[ti % 3]
                ti += 1
                if sg["kind"] == "p":
                    STT(t[:, 0:n], xT[:, k, o:o + n], mod_scalar(l, m_sc, k), rs[:, o:o + n], ALU.mult, ALU.mult)
                    ACT(h[:, k, o:o + n], t[:, 0:n], AF.Identity, bias=mod_scalar(l, m_sh, k))
                else:
                    t3 = t[:, 0:n].r("p (a b) -> p a b", b=DSQ)
                    TT("dve", t[:, 0:n], xT[:, k, o:o + n], rs[:, o:o + n], ALU.mult)
                    TT("dve", t3, t3, mod_bc(l, m_sc, k), ALU.mult)
                    TT("dve", h[:, k, o:o + n].r("p (a b) -> p a b", b=DSQ), t3, mod_bc(l, m_sh, k), ALU.add)
        release(mk)

    def residual(sg, k, pbv, l, m_g, tmpb):
        o, n = sg["off"], sg["n"]
        if sg["kind"] == "p":
            STT(xT[:, k, o:o + n], pbv, mod_scalar(l, m_g, k), xT[:, k, o:o + n], ALU.mult, ALU.add)
        else:
            t3 = tmpb[:, 0:n].r("p (a b) -> p a b", b=DSQ)
            TT("dve", t3, pbv.r("p (a b) -> p a b", b=DSQ), mod_bc(l, m_g, k), ALU.mult)
            TT("dve", xT[:, k, o:o + n], xT[:, k, o:o + n], tmpb[:, 0:n], ALU.add)
        sq_accum(sg, k)

    def proj_run(blk, wt, nk, noc, rhs, evac, from_h=False):
        for oc in range(noc):
            for sg in blk["segs"]:
                pb = psum()
                n = sg["n"]
                prs_ = [(wt[:, k, oc * 128:(oc + 1) * 128], rhs(k, sg)) for k in range(nk)]
                if from_h:
                    MMh(pb[:, 0:n], prs_, sg["off"])
                else:
                    MM(pb[:, 0:n], prs_)
                flush_sq()
                evac(oc, sg, pb[:, 0:n])

    def even_mixer_tasks(blk, l, st):
        e_ = l // 2
        b = blk["b"]
        lastblk = (b == NPB - 1)
        T = []
        W = w_in_ab[e_]

        def pre(slot_unused=None):
            st["mk"] = mark()
            st["aext"] = alloc((4, 30 + PBT), BF16)
            st["asx"] = alloc((4, NS, 38), BF16)
            st["af32"] = alloc((4, 128), F32)
            st["sig"] = [alloc((512,), F32) for _ in range(2)]
            st["acc"] = alloc((4, TB), F32)
            st["accb"] = alloc((4, TB), BF16)
            st["sqb"] = alloc((4, TB), BF16)
            st["bxe"] = alloc((4, 2 + PBT), F32)
            st["bxs"] = alloc((4, NS, 10), F32)
            st["cb"] = alloc((4, TB), F32)
            st["dg"] = alloc((2, 31, 128), BF16)
            st["mean"] = alloc((TB,), F32)
            st["var"] = alloc((TB,), F32)
            st["rstd"] = alloc((TB,), F32)
            st["t1"] = [alloc((512,), F32) for _ in range(2)]
            st["sto"] = alloc((512,), F32)
            st["ctr"] = 0
            aext, bxe, asx, bxs = st["aext"], st["bxe"], st["asx"], st["bxs"]
            COPY("pool", aext[:, :, 0:30], carry_a[e_][:, :, :])
            COPY("pool", bxe[:, :, 0:2], carry_b[e_][:, :, :])
            if b == 0:
                hs = alloc((4, 512), F32)
                hb = alloc((512,), F32)
                prs, wks = [], []
                for t in range(4):
                    dv = hs[0:120, t, :]
                    prs.append((dv.ap, sa[e_, 4 * t:4 * t + 4].rearrange("n r c -> (n r) c")))
                    wks.append(dv.keys)
                dvb = hb[0:32, :]
                prs.append((dvb.ap, sb[e_].rearrange("n r c -> (n r) c")))
                wks.append(dvb.keys)
                DMA("sp", "hist", prs, writes=wks)
                for t in range(4):
                    pb = psum()
                    for j in range(4):
                        TR(pb[:, j * 120:(j + 1) * 120], hs[0:120, t, j * 128:(j + 1) * 128], ident[0:120, 0:120])
                    for j in range(4):
                        COPY("act", asx[:, j, 4 * t:4 * t + 4, 0:30],
                             pb[:, j * 120:(j + 1) * 120].r("p (a b) -> p a b", b=30))
                pb = psum()
                for j in range(4):
                    TR(pb[:, j * 32:(j + 1) * 32], hb[0:32, j * 128:(j + 1) * 128], ident[0:32, 0:32])
                for j in range(4):
                    COPY("act", bxs[:, j, :, 0:2], pb[:, j * 32:(j + 1) * 32].r("p (a b) -> p a b", b=2))
                prs2, rks = [], []
                for t in range(4):
                    for q in range(4):
                        sv2 = hs[30 * q + 8:30 * q + 30, t, :]
                        prs2.append((ca_s[e_, 4 * t + q, 0:22, :], sv2.ap))
                        rks.append(sv2.keys)
                DMA("sp", "histo", prs2, reads=rks, is_out=True)

        for g in range(2):
            def runA(slot, g=g):
                if g == 0:
                    pre()
                wt = ring8[slot]
                aext, asx, af32 = st["aext"], st["asx"], st["af32"]
                dg = st["dg"]
                for jj in range(2):
                    j = 2 * g + jj
                    for sg in blk["segs"]:
                        o, n = sg["off"], sg["n"]
                        pv, pg = psum(), psum()
                        MMh(pv[:, 0:n], [(wt[:, k, (2 * jj) * 128:(2 * jj + 1) * 128], h[:, k, o:o + n]) for k in range(KC)], o)
                        MM(pg[:, 0:n], [(wt[:, k, (2 * jj + 1) * 128:(2 * jj + 2) * 128], h[:, k, o:o + n]) for k in range(KC)])
                        sgb = st["sig"][st["ctr"] % 2]
                        st["ctr"] += 1
                        ACT(sgb[:, 0:n], pg[:, 0:n], AF.Sigmoid)
                        if sg["kind"] == "p":
                            TT("dve", aext[:, j, 30 + o:30 + o + n], pv[:, 0:n], sgb[:, 0:n], ALU.mult)
                            if sg["last"]:
                                TT("dve", af32[:, j, 0:30], pv[:, n - 30:n], sgb[:, n - 30:n], ALU.mult)
                        else:
                            TT("dve", asx[:, j, :, 30:38], pv[:, 0:n].r("p (a b) -> p a b", b=DSQ),
                               sgb[:, 0:n].r("p (a b) -> p a b", b=DSQ), ALU.mult)
                            TT("dve", af32[:, j, 0:n], pv[:, 0:n], sgb[:, 0:n], ALU.mult)
                    TT("dve", dg[:, jj, :, :], identb[:, :].un(1).bc([128, 31, 128]),
                       cwb[:, e_ * 4 + j, :].un(2).bc([128, 31, 128]), ALU.mult)

                    def conv(j=j, jj=jj):
                        cbias = col(R_CAB + e_ * 4 + j)
                        for sg in blk["segs"]:
                            o, n = sg["off"], sg["n"]
                            pb = psum()
                            if sg["kind"] == "p":
                                MM(pb[:, 0:n], [(dg[:, jj, tap, :], aext[:, j, o + tap:o + tap + n]) for tap in range(31)])
                            else:
                                MM(pb[:, 0:n], [(dg[:, jj, tap, :], asx[:, j, :, tap:tap + DSQ]) for tap in range(31)])
                            TS("dve", st["acc"][:, j, o:o + n], pb[:, 0:n], cbias, ALU.add)
                            COPY("act", st["accb"][:, j, o:o + n], st["acc"][:, j, o:o + n])
                            ACT(st["sqb"][:, j, o:o + n], st["acc"][:, j, o:o + n], AF.Square)
                    if st.get("pend") is not None:
                        st["pend"]()
                    st["pend"] = conv
                if g == 1:
                    st["pend"]()
                    st["pend"] = None
                if g == 1:
                    for sg in blk["segs"]:
                        o, n = sg["off"], sg["n"]
                        pm, pe2 = psum(), psum()
                        MM(pm[:, 0:n], [(onesA[:, :], st["accb"][:, j2, o:o + n]) for j2 in range(4)])
                        MM(pe2[:, 0:n], [(onesA[:, :], st["sqb"][:, j2, o:o + n]) for j2 in range(4)])
                        mean, var, rstd = st["mean"][:, o:o + n], st["var"][:, o:o + n], st["rstd"][:, o:o + n]
                        COPY("act", mean, pm[:, 0:n])
                        STT(var, mean, -1.0, mean, ALU.mult, ALU.mult)
                        TT("dve", var, pe2[:, 0:n], var, ALU.add)
                        ACT(rstd, var, AF.Ln, bias=EPS)
                        ACT(rstd, rstd, AF.Exp, scale=-0.5)
                        for j2 in range(4):
                            t1 = st["t1"][j2 % 2]
                            TT("dve", t1[:, 0:n], st["acc"][:, j2, o:o + n], mean, ALU.subtract)
                            TT("dve", t1[:, 0:n], t1[:, 0:n], rstd, ALU.mult)
                            ACT(mo[:, j2, o:o + n], t1[:, 0:n], AF.Silu,
                                bias=col(R_LAB + e_ * 4 + j2), scale=col(R_LAG + e_ * 4 + j2))
                    preload_ln()
                    COPY("pool", carry_a[e_][:, :, :], aext[:, :, PBT:PBT + 30])
                    if lastblk:
                        pb = psum()
                        for j2 in range(4):
                            TR(pb[0:30, j2 * 128:(j2 + 1) * 128], af32[:, j2, 0:30], ident[:, :])
                        sv = st["sto"][0:30, :]
                        COPY("dve", sv, pb[0:30, :])
                        DMA("sp", "sto", [(ca_p[e_], sv.ap)], reads=[sv.keys], is_out=True)
                    if b == 0:
                        pb = psum()
                        for j2 in range(4):
                            TR(pb[:, j2 * 128:(j2 + 1) * 128], af32[:, j2, 0:128], ident[:, :])
                        sv = st["sto"][:, :]
                        COPY("dve", sv, pb[:, :])
                        DMA("sp", "sto", [(ca_s[e_, n_, 22:30, :], st["sto"][8 * n_:8 * n_ + 8, :].ap) for n_ in range(NS)],
                            reads=[sv.keys], is_out=True)
            T.append((wdma_cols(W, [2 * g, 4 + 2 * g, 2 * g + 1, 4 + 2 * g + 1], 8), runA))
        for g in range(2):
            def runB(slot, g=g):
                wt = ring8[slot]
                bxe, bxs, cb = st["bxe"], st["bxs"], st["cb"]
                for jj in range(2):
                    j = 2 * g + jj
                    for sg in blk["segs"]:
                        o, n = sg["off"], sg["n"]
                        px, pc = psum(), psum()
                        MM(px[:, 0:n], [(wt[:, k, (2 * jj) * 128:(2 * jj + 1) * 128], h[:, k, o:o + n]) for k in range(KC)])
                        MM(pc[:, 0:n], [(wt[:, k, (2 * jj + 1) * 128:(2 * jj + 2) * 128], h[:, k, o:o + n]) for k in range(KC)])
                        sgb = st["sig"][st["ctr"] % 2]
                        st["ctr"] += 1
                        COPY("act", sgb[:, 0:n], px[:, 0:n])
                        w0 = col(R_CBW + e_ * 12 + 0 * 4 + j)
                        w1 = col(R_CBW + e_ * 12 + 1 * 4 + j)
                        w2 = col(R_CBW + e_ * 12 + 2 * 4 + j)
                        if sg["kind"] == "p":
                            TT("dve", bxe[:, j, 2 + o:2 + o + n], pc[:, 0:n], sgb[:, 0:n], ALU.mult)
                            c_ = cb[:, j, o:o + n]
                            TS("dve", c_, bxe[:, j, o:o + n], w0, ALU.mult)
                            STT(c_, bxe[:, j, o + 1:o + 1 + n], w1, c_, ALU.mult, ALU.add)
                            STT(c_, bxe[:, j, o + 2:o + 2 + n], w2, c_, ALU.mult, ALU.add)
                        else:
                            TT("dve", bxs[:, j, :, 2:10], pc[:, 0:n].r("p (a b) -> p a b", b=DSQ),
                               sgb[:, 0:n].r("p (a b) -> p a b", b=DSQ), ALU.mult)
                            c_ = cb[:, j, o:o + n].r("p (a b) -> p a b", b=DSQ)
                            TS("dve", c_, bxs[:, j, :, 0:8], w0, ALU.mult)
                            STT(c_, bxs[:, j, :, 1:9], w1, c_, ALU.mult, ALU.add)
                            STT(c_, bxs[:, j, :, 2:10], w2, c_, ALU.mult, ALU.add)
                if g == 1:
                    COPY("pool", carry_b[e_][:, :, :], bxe[:, :, PBT:PBT + 2])
                    if lastblk:
                        t1 = st["t1"][0]
                        COPY("dve", t1[:, 0:8].r("p (a b) -> p a b", b=2), bxe[:, :, PBT:PBT + 2])
                        pb = psum()
                        for j2 in range(4):
                            TR(pb[0:2, j2 * 128:(j2 + 1) * 128], t1[:, 2 * j2:2 * j2 + 2], ident[:, :])
                        sv = st["sto"][0:2, :]
                        COPY("dve", sv, pb[0:2, :])
                        DMA("sp", "sto", [(cb_p[e_], sv.ap)], reads=[sv.keys], is_out=True)
                    if b == 0:
                        t1 = st["t1"][1]
                        pb = psum()
                        for j2 in range(4):
                            COPY("dve", t1[:, 32 * j2:32 * j2 + 32].r("p (a b) -> p a b", b=2), bxs[:, j2, :, 8:10])
                            TR(pb[0:32, j2 * 128:(j2 + 1) * 128], t1[:, 32 * j2:32 * j2 + 32], ident[:, :])
                        sv = st["sto"][0:32, :]
                        COPY("dve", sv, pb[0:32, :])
                        DMA("sp", "sto", [(cb_s[e_].rearrange("n r c -> (n r) c"), sv.ap)], reads=[sv.keys], is_out=True)
            T.append((wdma_cols(W, [8 + 2 * g, 16 + 2 * g, 8 + 2 * g + 1, 16 + 2 * g + 1], 8), runB))

        def runBB(slot):
            wt = ring8[slot]

            def ev(oc, sg, pv):
                o, n = sg["off"], sg["n"]
                TT("dve", mo[:, 4 + oc, o:o + n], pv, st["cb"][:, oc, o:o + n], ALU.mult)
            proj_run(blk, wt, KC, 4, lambda k, sg: h[:, k, sg["off"]:sg["off"] + sg["n"]], ev)
        T.append((wdma_cols(W, [12, 13, 14, 15], 8), runBB))
        for g in range(2):
            def runO(slot, g=g):
                wt = ring8[slot]
                tmpb = st["t1"][0]

                def ev(oc, sg, pv):
                    residual(sg, g * 4 + oc, pv, l, 2, tmpb)
                proj_run(blk, wt, KC, 4, lambda k, sg: mo[:, k, sg["off"]:sg["off"] + sg["n"]], ev)
                if g == 1:
                    release(st["mk"])
            T.append((wdma_cols(w_out_ab[e_], [4 * g + i for i in range(4)], 8), runO))
        return T

    def odd_mixer_tasks(blk, l, st):
        o_ = l // 2
        b = blk["b"]
        lastblk = (b == NPB - 1)
        T = []
        W = w_in_c[o_]
        ntile = sum(sg["n"] for sg in blk["segs"]) // 128

        def pre():
            st["mk"] = mark()
            st["bvb"], st["lng"], st["lnb"], st["bsb"] = od_bvb, od_lng, od_lnb, od_bsb
            assert astate["top"] + 70 * 1024 < OD_OFF
            st["vraw"] = [alloc((D,), F32) for _ in range(ntile)]
            st["vnb"] = alloc((ntile, D), BF16)
            st["stat"] = alloc((ntile, 16), F32)
            st["ubuf"] = alloc((8, TB), F32)
            st["tmp"] = [alloc((512,), F32) for _ in range(2)]
            st["ctr"] = 0

        def tile_info(ti):
            c = 0
            for sg in blk["segs"]:
                if ti * 128 < c + sg["n"]:
                    return sg, ti * 128
                c += sg["n"]
            raise ValueError

        for g in range(2):
            def runV(slot, g=g):
                if g == 0:
                    pre()
                wt = ring8[slot]
                for ti in range(ntile):
                    pb = psum()
                    o = ti * 128
                    MMh(pb[:, :], [(h[:, k, o:o + 128], wt[:, k, :]) for k in range(KC)], o)
                    vr = st["vraw"][ti]
                    TT("dve", vr[:, g * 512:(g + 1) * 512], pb[:, :], st["bvb"][:, g * 512:(g + 1) * 512], ALU.add)
                    ACT(vr[:, g * 512:(g + 1) * 512], vr[:, g * 512:(g + 1) * 512], AF.Gelu_apprx_tanh)
                if g == 1:
                    sta = st["stat"]
                    for ti in range(ntile):
                        vr = st["vraw"][ti]
                        for hh in range(2):
                            sv_, vv_ = sta[:, ti, hh * 6:hh * 6 + 6], vr[:, hh * 512:(hh + 1) * 512]
                            P.add("dve", lambda e, a=sv_.ap, c=vv_.ap: e.bn_stats(out=a, in_=c),
                                  reads=[vv_.keys], writes=[sv_.keys])
                        mv = sta[:, ti, 12:14]
                        s12 = sta[:, ti, 0:12]
                        P.add("dve", lambda e, a=mv.ap, c=s12.ap: e.bn_aggr(out=a, in_=c),
                              reads=[s12.keys], writes=[mv.keys])
                    ACT(sta[:, :, 14:15], sta[:, :, 13:14], AF.Sqrt, bias=EPS)
                    RECIP(sta[:, :, 14:15], sta[:, :, 14:15])
                    for ti in range(ntile):
                        vr = st["vraw"][ti]
                        STT(vr[:, :], vr[:, :], sta[:, ti, 12:13], st["lng"][:, :], ALU.subtract, ALU.mult)
                        sg, _ = tile_info(ti)
                        is_out_tile = (sg["kind"] == "s") or (lastblk and ti == ntile - 1)
                        if not is_out_tile:
                            STT(st["vnb"][:, ti, :], vr[:, :], sta[:, ti, 14:15], st["lnb"][:, :], ALU.mult, ALU.add)
                        else:
                            STT(vr[:, :], vr[:, :], sta[:, ti, 14:15], st["lnb"][:, :], ALU.mult, ALU.add)
                            COPY("dve", st["vnb"][:, ti, :], vr[:, :])
                            if sg["kind"] == "s":
                                DMA("sp", "cvs", [(cv_s[o_], vr[:, :].ap)], reads=[vr[:, :].keys], is_out=True)
                            else:
                                DMA("sp", "cvp", [(cv_p[o_], vr[:, :].ap)], reads=[vr[:, :].keys], is_out=True)
            T.append((wdma_cols(W, [8 + 4 * g + i for i in range(4)], 8), runV))
        for g in range(2):
            def runU(slot, g=g):
                wt = ring8[slot]
                for oc in range(4):
                    j = 4 * g + oc
                    for sg in blk["segs"]:
                        o, n = sg["off"], sg["n"]
                        pu = psum()
                        MM(pu[:, 0:n], [(wt[:, k, oc * 128:(oc + 1) * 128], h[:, k, o:o + n]) for k in range(KC)])
                        ACT(st["ubuf"][:, j, o:o + n], pu[:, 0:n], AF.Gelu_apprx_tanh, bias=col(R_BU + o_ * 8 + j))
                if g == 1:
                    preload_ln()
            T.append((wdma_cols(W, [4 * g + i for i in range(4)], 8), runU))

        def gate():
            for j in range(8):
                for sg in blk["segs"]:
                    o, n = sg["off"], sg["n"]
                    pg = psum()
                    for cc in range(n // 128):
                        ti = (o + cc * 128) // 128
                        rhs = wsTs[o_][:, j, :] if sg["kind"] == "s" else wsT[o_][:, j, :]
                        MM(pg[:, cc * 128:(cc + 1) * 128], [(st["vnb"][:, ti, j * 128:(j + 1) * 128], rhs)])
                    tb = st["tmp"][st["ctr"] % 2]
                    st["ctr"] += 1
                    if sg["kind"] == "p":
                        nb4 = n // 128
                        bsv = st["bsb"][:, j, :].un(1).bc([128, nb4, 128])
                        TT("dve", tb[:, 0:n].r("p (a b) -> p a b", b=128), pg[:, 0:n].r("p (a b) -> p a b", b=128),
                           bsv, ALU.add)
                    else:
                        bsv = st["bsb"][:, j, 0:DSQ].un(1).bc([128, NS, DSQ])
                        TT("dve", tb[:, 0:n].r("p (a b) -> p a b", b=DSQ), pg[:, 0:n].r("p (a b) -> p a b", b=DSQ),
                           bsv, ALU.add)
                    TT("dve", mo[:, j, o:o + n], tb[:, 0:n], st["ubuf"][:, j, o:o + n], ALU.mult)
        T.append((None, lambda slot: gate()))
        for g in range(2):
            def runO(slot, g=g):
                wt = ring8[slot]
                tmpb = st["tmp"][0]

                def ev(oc, sg, pv):
                    residual(sg, g * 4 + oc, pv, l, 2, tmpb)
                proj_run(blk, wt, KC, 4, lambda k, sg: mo[:, k, sg["off"]:sg["off"] + sg["n"]], ev)
                if g == 1:
                    release(st["mk"])
            T.append((wdma_cols(w_out_c[o_], [4 * g + i for i in range(4)], 8), runO))
        return T

    def ffn_tasks(blk, l, st):
        T = []

        def pre():
            st["mk"] = mark()
            st["f"] = alloc((32, TB), BF16)
            st["rl"] = [alloc((512,), F32) for _ in range(3)]
            st["ctr"] = 0
        for g in range(8):
            def run1(slot, g=g):
                if g == 0:
                    pre()
                    if (l + 1) % 2 == 1 and l + 1 < nlayers:
                        load_oddc((l + 1) // 2)
                wt = ring8[slot]

                def ev(oc, sg, pv):
                    o, n = sg["off"], sg["n"]
                    r = st["rl"][st["ctr"] % 3]
                    st["ctr"] += 1
                    ACT(r[:, 0:n], pv, AF.Relu)
                    TT("dve", st["f"][:, g * 4 + oc, o:o + n], r[:, 0:n], r[:, 0:n], ALU.mult)
                proj_run(blk, wt, KC, 4, lambda k, sg: h[:, k, sg["off"]:sg["off"] + sg["n"]], ev, from_h=True)
                if g == 7:
                    preload_ln()
            T.append((wdma_cols(w_ff1[l], [4 * g + i for i in range(4)], 8), run1))
        for g in range(8):
            def run2(slot, g=g):
                wt = ring32[slot]
                tmpb = st["rl"][0]

                def ev(oc, sg, pv):
                    residual(sg, g, pv, l, 5, tmpb)
                proj_run(blk, wt, 32, 1, lambda k, sg: st["f"][:, k, sg["off"]:sg["off"] + sg["n"]], ev)
                if g == 7:
                    release(st["mk"])
            T.append((wdma_cols(w_ff2[l], [g], 32), run2))
        return T

    def load_block(blk):
        mk = mark()
        stg = [alloc((D,), F32) for _ in range(2)]
        i = 0
        for sg in blk["segs"]:
            for tt in range(sg["n"] // 128):
                s = stg[i % 2]
                sv = s[:, :]
                if sg["kind"] == "p":
                    src = x_p[sg["tok0"] + tt * 128:sg["tok0"] + (tt + 1) * 128, :]
                else:
                    src = x_s[:, :]
                DMA("sp", "xs%d" % (i % 2), [(sv.ap, src)], writes=[sv.keys])
                o = sg["off"] + tt * 128
                for hh in range(2):
                    pb = psum()
                    for q in range(4):
                        k = hh * 4 + q
                        TR(pb[:, q * 128:(q + 1) * 128], s[:, k * 128:(k + 1) * 128], ident[:, :])
                    COPY("act", xT[:, hh * 4:hh * 4 + 4, o:o + 128], pb[:, :].r("p (a b) -> p a b", b=128))
                i += 1
        for sg in blk["segs"]:
            for k in range(KC):
                sq_accum(sg, k)
                flush_sq(keep=2)
        release(mk)

    def store_block(blk):
        mk = mark()
        stg = [alloc((D,), F32) for _ in range(2)]
        rs = alloc((TB,), F32)
        i = 0
        flush_sq()
        for sg in blk["segs"]:
            o, n = sg["off"], sg["n"]
            if final_norm:
                pb = PSN[sg["kind"]]
                ACT(rs[:, o:o + n], pb[:, 0:n], AF.Ln, bias=EPS)
                ACT(rs[:, o:o + n], rs[:, o:o + n], AF.Exp, scale=-0.5)
                for k in range(KC):
                    STT(xT[:, k, o:o + n], xT[:, k, o:o + n], col(R_FG + k), rs[:, o:o + n], ALU.mult, ALU.mult)
            for tt in range(n // 128):
                s = stg[i % 2]
                oo = o + tt * 128
                for hh in range(2):
                    pb = psum()
                    for q in range(4):
                        k = hh * 4 + q
                        TR(pb[:, q * 128:(q + 1) * 128], xT[:, k, oo:oo + 128], ident[:, :])
                    COPY("act" if hh == 0 else "dve", s[:, hh * 512:(hh + 1) * 512], pb[:, :])
                if sg["kind"] == "p":
                    dst = y_p[sg["tok0"] + tt * 128:sg["tok0"] + (tt + 1) * 128, :]
                else:
                    dst = y_s[:, :]
                DMA("sp", "ys%d" % (i % 2), [(dst, s[:, :].ap)], reads=[s[:, :].keys], is_out=True)
                i += 1
        release(mk)

    def T_plain(fn):
        return (None, lambda slot: fn())

    if nlayers > 0:
        tasks += ada_tasks(0)
    for blk in blocks:
        tasks.append(T_plain(lambda blk=blk: load_block(blk)))
        for l in range(nlayers):
            st = {}
            tasks.append(T_plain(lambda blk=blk, l=l: norm_mod(blk, l, 0)))
            if l % 2 == 0:
                tasks += even_mixer_tasks(blk, l, st)
            else:
                tasks += odd_mixer_tasks(blk, l, st)
            tasks.append(T_plain(lambda blk=blk, l=l: norm_mod(blk, l, 1)))
            st2 = {}
            ft = ffn_tasks(blk, l, st2)
            if blk["b"] == 0 and l + 1 < nlayers:
                at = ada_tasks(l + 1)
                merged = []
                for i_, t_ in enumerate(ft):
                    merged.append(t_)
                    if i_ < len(at):
                        merged.append(at[i_])
                ft = merged
            tasks += ft
        tasks.append(T_plain(lambda blk=blk: store_block(blk)))

    if max_tasks is not None:
        tasks = tasks[:max_tasks]
    widx = [i for i, t in enumerate(tasks) if t[0] is not None]
    slot_of = {ti: n % NB for n, ti in enumerate(widx)}
    issued = 0
    LOOK = NB - 1
    for n, ti in enumerate(widx[:LOOK]):
        tasks[ti][0](slot_of[ti])
        issued += 1
    wpos = 0
    for i, (dfn, run) in enumerate(tasks):
        if dfn is not None:
            if issued < len(widx):
                tj = widx[issued]
                tasks[tj][0](slot_of[tj])
                issued += 1
            run(slot_of[i])
            wpos += 1
        else:
            run(None)

    P.add("sp", None, reads=outkeys)
    P.emit(nc, block, stack)
    stack.close()
    return nc


_CACHE = {}


def _get_prog(**kw):
    key = tuple(sorted(kw.items()))
    if key not in _CACHE:
        _CACHE[key] = build_program(**kw)
    return _CACHE[key]


def make_in_maps(inputs):
    f = lambda a: np.ascontiguousarray(np.asarray(a, dtype=np.float32))
    shared = {k: f(inputs[k]) for k in
              ["w_in_ab", "conv_a_w", "conv_a_b", "ln_a_g", "ln_a_b", "conv_b_w", "w_out_ab", "w_in_c", "b_in_c",
               "ln_v_g", "ln_v_b", "w_s", "b_s", "w_out_c", "w_ada", "b_ada", "norm_g", "w_ff1", "w_ff2", "final_g"]}
    xp, xs = f(inputs["x_prompt"]), f(inputs["x_sample"])
    sa, sb = f(inputs["state_conv_a"]), f(inputs["state_conv_b"])
    cp, cs = f(inputs["c_prompt"]), f(inputs["c_sample"])
    maps = []
    for c in range(NCORES):
        m = dict(shared)
        m["x_p"] = xp[c]
        m["x_s"] = np.ascontiguousarray(xs[NS * c:NS * (c + 1)].reshape(NS * DSQ, D))
        m["sa"] = np.ascontiguousarray(sa[:, NS * c:NS * (c + 1)])
        m["sb"] = np.ascontiguousarray(sb[:, NS * c:NS * (c + 1)])
        m["c17"] = np.ascontiguousarray(np.concatenate([cp[c:c + 1], cs[NS * c:NS * (c + 1)]], axis=0))
        maps.append(m)
    return maps


def gather(results):
    r = results
    y_prompt = np.stack([r[c]["y_p"] for c in range(NCORES)], axis=0)
    y_sample = np.concatenate([r[c]["y_s"].reshape(NS, DSQ, D) for c in range(NCORES)], axis=0)
    ca_p = np.stack([r[c]["ca_p"] for c in range(NCORES)], axis=1)
    ca_s = np.concatenate([r[c]["ca_s"] for c in range(NCORES)], axis=1)
    cb_p = np.stack([r[c]["cb_p"] for c in range(NCORES)], axis=1)
    cb_s = np.concatenate([r[c]["cb_s"] for c in range(NCORES)], axis=1)
    cv_p = np.stack([r[c]["cv_p"] for c in range(NCORES)], axis=1)
    cv_s = np.concatenate([r[c]["cv_s"].reshape(2, NS, DSQ, D) for c in range(NCORES)], axis=1)
    return tuple(np.ascontiguousarray(a, dtype=np.float32) for a in
                 (y_prompt, y_sample, ca_p, ca_s, cb_p, cb_s, cv_p, cv_s))


def kernel(**inputs):
    nc = _get_prog()
    res = run_bass_kernel_spmd(nc, make_in_maps(inputs), core_ids=list(range(NCORES)))
    return gather(res.results)
```

```python
import numpy as np
from contextlib import ExitStack
import concourse.bass as bass
import concourse.mybir as mybir
from concourse.bass_utils import run_bass_kernel_spmd

F32 = mybir.dt.float32
BF16 = mybir.dt.bfloat16
U8 = mybir.dt.uint8
AF = mybir.ActivationFunctionType
ALU = mybir.AluOpType

D = 1024
KC = 8
SEQ = 2048
NS = 16
DSQ = 8
DEPTH = 4
EPS = 1e-6
GRAN = 256
WARM_N = 14
NCORES = 8

R_NG = 0
R_FG = 64
R_BADA = 72
R_CAW = 264
R_CAB = 512
R_LAG = 520
R_LAB = 528
R_CBW = 536
R_BU = 560
R_TOT = 576
NRT = 5


class View:
    __slots__ = ("ap", "keys")

    def __init__(self, ap, keys):
        self.ap = ap
        self.keys = keys

    def r(self, pat, **kw):
        return View(self.ap.rearrange(pat, **kw), self.keys)

    def bc(self, shape):
        return View(self.ap.to_broadcast(list(shape)), self.keys)

    def un(self, ax):
        return View(self.ap.unsqueeze(ax), self.keys)


class Buf:
    def __init__(self, arena, off, shape, dt):
        self.off = off
        self.shape = tuple(shape)
        self.dt = dt
        self.es = 4 if dt == F32 else 2
        n = 1
        for s in shape:
            n *= s
        self.nbytes = n * self.es
        flat = arena[:, off:off + self.nbytes].bitcast(dt)
        if len(shape) == 1:
            self.ap = flat
        elif len(shape) == 2:
            self.ap = flat.rearrange("p (a b) -> p a b", a=shape[0])
        elif len(shape) == 3:
            self.ap = flat.rearrange("p (a b c) -> p a b c", a=shape[0], b=shape[1])
        else:
            raise ValueError
        st = []
        acc = 1
        for s in reversed(self.shape):
            st.append(acc)
            acc *= s
        self.strides = tuple(reversed(st))

    def __getitem__(self, idx):
        if not isinstance(idx, tuple):
            idx = (idx,)
        idx = list(idx) + [slice(None)] * (1 + len(self.shape) - len(idx))
        ap = self.ap[tuple(idx)]
        rng = []
        for d, ix in enumerate(idx[1:]):
            if isinstance(ix, int):
                rng.append((ix, ix + 1))
            else:
                a = 0 if ix.start is None else ix.start
                b = self.shape[d] if ix.stop is None else ix.stop
                assert ix.step in (None, 1)
                rng.append((a, b))
        outer = rng[:-1]
        n_outer = 1
        for a, b in outer:
            n_outer *= (b - a)
        keys = set()
        la, lb = rng[-1]
        if n_outer <= 128:
            def rec(d, base):
                if d == len(outer):
                    lo = self.off + (base + la) * self.es
                    hi = self.off + (base + lb) * self.es
                    for g in range(lo // GRAN, (hi - 1) // GRAN + 1):
                        keys.add(g)
                    return
                for i in range(outer[d][0], outer[d][1]):
                    rec(d + 1, base + i * self.strides[d])
            rec(0, 0)
        else:
            lo = self.off + sum(r[0] * s for r, s in zip(rng, self.strides)) * self.es
            hi = self.off + (sum((r[1] - 1) * s for r, s in zip(rng, self.strides)) + 1) * self.es
            for g in range(lo // GRAN, (hi - 1) // GRAN + 1):
                keys.add(g)
        return View(ap, frozenset(keys))


class PBank:
    def __init__(self, t, i):
        self.t = t
        self.i = i
        self.keys = frozenset([("ps", i)])

    def __getitem__(self, idx):
        return View(self.t[idx], self.keys)


class Prog:
    def __init__(self):
        self.ops = []
        self.last_w = {}
        self.readers = {}
        self.slots = {}

    def add(self, eng, fn, reads=(), writes=(), dma=None, ninc=1):
        idx = len(self.ops)
        deps = {}
        rk = set()
        for r in reads:
            rk |= set(r) if isinstance(r, (set, frozenset)) else {r}
        wk = set()
        for w in writes:
            wk |= set(w) if isinstance(w, (set, frozenset)) else {w}
        wk |= {k for k in rk if isinstance(k, tuple) and k[0] == "ps"}
        lw = self.last_w
        rd = self.readers
        for k in rk:
            w = lw.get(k)
            if w is not None:
                deps[w] = "raw"
        for k in wk:
            w = lw.get(k)
            if w is not None and w not in deps:
                deps[w] = "waw"
            for x in rd.get(k, ()):
                if x not in deps:
                    deps[x] = "war"
        need = []
        for p, kind in deps.items():
            P = self.ops[p]
            if P["dma"] is None and dma is None and P["eng"] == eng:
                if eng == "pe":
                    continue
            need.append(p)
            P["signal"] = True
        op = dict(eng=eng, fn=fn, deps=need, dma=dma, ninc=ninc, signal=False)
        self.ops.append(op)
        for k in wk:
            lw[k] = idx
            rd[k] = []
        for k in rk:
            rd.setdefault(k, []).append(idx)
        return idx

    def emit(self, nc, block, stack):
        ENG = ["pe", "act", "dve", "pool", "sp"]
        esem = {e: stack.enter_context(nc.semaphore("S_" + e)) for e in ENG}
        slotsem = {}
        cnt = {e: 0 for e in ENG}
        scnt = {}
        const_total = 16 * sum(o["ninc"] for o in self.ops if o["dma"] == "const")
        for o in self.ops:
            if o["dma"] is None:
                if o["signal"]:
                    cnt[o["eng"]] += 1
                    o["token"] = (esem[o["eng"]], cnt[o["eng"]])
            else:
                s = o["dma"]
                if s not in slotsem:
                    slotsem[s] = stack.enter_context(nc.semaphore("D_" + s))
                    scnt[s] = 0
                scnt[s] += 16 * o["ninc"]
                o["token"] = (slotsem[s], const_total if s == "const" else scnt[s])
                o["sem"] = slotsem[s]
        per = {e: [o for o in self.ops if o["eng"] == e] for e in ENG}
        ops = self.ops

        def runner(e):
            def f(engobj):
                waited = {}
                for o in per[e]:
                    need = {}
                    for p in o["deps"]:
                        sem, val = ops[p]["token"]
                        if need.get(sem, (None, 0))[1] < val:
                            need[sem] = (sem, val)
                    for sem, val in need.values():
                        if waited.get(sem, 0) < val:
                            engobj.wait_ge(sem, val)
                            waited[sem] = val
                    if o["fn"] is None:
                        continue
                    if o["dma"] is not None:
                        o["fn"](engobj, o["sem"])
                    else:
                        inst = o["fn"](engobj)
                        if o["signal"]:
                            inst.then_inc(esem[e], 1)
            return f
        block.tensor(runner("pe"))
        block.scalar(runner("act"))
        block.vector(runner("dve"))
        block.gpsimd(runner("pool"))
        block.sync(runner("sp"))


def build_program(nlayers=DEPTH, PBT=512, final_norm=True, max_tasks=None):
    nc = bass.Bass("TRN2", target_bir_lowering=False)
    NE = 2
    NO = 2

    def din(name, shape):
        return nc.dram_tensor(name, list(shape), F32, kind="ExternalInput").ap()

    def dout(name, shape):
        return nc.dram_tensor(name, list(shape), F32, kind="ExternalOutput").ap()

    x_p = din("x_p", [SEQ, D])
    x_s = din("x_s", [NS * DSQ, D])
    sa = din("sa", [NE, NS, 30, 512])
    sb = din("sb", [NE, NS, 2, 512])
    c17 = din("c17", [1 + NS, D])
    w_in_ab = din("w_in_ab", [NE, D, 2560])
    conv_a_w = din("conv_a_w", [NE, 31, 512])
    conv_a_b = din("conv_a_b", [NE, 512])
    ln_a_g = din("ln_a_g", [NE, 512])
    ln_a_b = din("ln_a_b", [NE, 512])
    conv_b_w = din("conv_b_w", [NE, 3, 512])
    w_out_ab = din("w_out_ab", [NE, D, D])
    w_in_c = din("w_in_c", [NO, D, 2048])
    b_in_c = din("b_in_c", [NO, 2048])
    ln_v_g = din("ln_v_g", [NO, D])
    ln_v_b = din("ln_v_b", [NO, D])
    w_s = din("w_s", [NO, 8, 128, 128])
    b_s = din("b_s", [NO, 8, 128])
    w_out_c = din("w_out_c", [NO, D, D])
    w_ada = din("w_ada", [DEPTH, D, 6 * D])
    b_ada = din("b_ada", [DEPTH, 6 * D])
    norm_g = din("norm_g", [DEPTH, 2, D])
    w_ff1 = din("w_ff1", [DEPTH, D, 4 * D])
    w_ff2 = din("w_ff2", [DEPTH, 4 * D, D])
    final_g = din("final_g", [D])

    y_p = dout("y_p", [SEQ, D])
    y_s = dout("y_s", [NS * DSQ, D])
    ca_p = dout("ca_p", [NE, 30, 512])
    ca_s = dout("ca_s", [NE, NS, 30, 512])
    cb_p = dout("cb_p", [NE, 2, 512])
    cb_s = dout("cb_s", [NE, NS, 2, 512])
    cv_p = dout("cv_p", [NO, 128, D])
    cv_s = dout("cv_s", [NO, NS * DSQ, D])

    NPB = SEQ // PBT
    TB = PBT + NS * DSQ
    NSEGP = PBT // 512

    stack = ExitStack()
    ARENA = 206 * 1024
    arena = stack.enter_context(nc.sbuf_tensor("arena", [128, ARENA], U8))
    pst = [stack.enter_context(nc.psum_tensor("ps%d" % i, [128, 512], F32)) for i in range(8)]
    PS = [PBank(pst[i], i) for i in range(8)]
    block = stack.enter_context(nc.Block())
    P = Prog()

    astate = {"top": 0}

    def alloc(shape, dt):
        off = (astate["top"] + GRAN - 1) // GRAN * GRAN
        b = Buf(arena, off, shape, dt)
        astate["top"] = off + b.nbytes
        assert astate["top"] <= ARENA, ("arena overflow", astate["top"])
        return b

    def mark():
        return astate["top"]

    def release(m):
        astate["top"] = m

    psn = {"i": 0}

    def psum():
        b = PS[psn["i"] % 5]
        psn["i"] += 1
        return b
    PSN = {"p": PS[6], "s": PS[7]}
    PSW = PS[5]

    def keysof(*vs):
        out = []
        for v in vs:
            if isinstance(v, View):
                out.append(v.keys)
        return out

    def apof(v):
        return v.ap if isinstance(v, View) else v

    def ACT(out, in_, func, bias=0.0, scale=1.0):
        o, i, b, s = out.ap, in_.ap, apof(bias), apof(scale)
        P.add("act", lambda e: e.activation(out=o, in_=i, func=func, bias=b, scale=s),
              reads=keysof(in_, bias, scale), writes=keysof(out))

    def TT(eng, out, in0, in1, op):
        o, a, b = out.ap, in0.ap, in1.ap
        P.add(eng, lambda e: e.tensor_tensor(out=o, in0=a, in1=b, op=op),
              reads=keysof(in0, in1), writes=keysof(out))

    def TS(eng, out, in0, s1, op0, s2=None, op1=None):
        o, a, x1, x2 = out.ap, in0.ap, apof(s1), apof(s2)
        if op1 is None:
            fn = lambda e: e.tensor_scalar(out=o, in0=a, scalar1=x1, scalar2=None, op0=op0)
        else:
            fn = lambda e: e.tensor_scalar(out=o, in0=a, scalar1=x1, scalar2=x2, op0=op0, op1=op1)
        P.add(eng, fn, reads=keysof(in0, s1, s2), writes=keysof(out))

    def STT(out, in0, scalar, in1, op0, op1):
        o, a, s, b = out.ap, in0.ap, apof(scalar), in1.ap
        P.add("dve", lambda e: e.scalar_tensor_tensor(out=o, in0=a, scalar=s, in1=b, op0=op0, op1=op1),
              reads=keysof(in0, scalar, in1), writes=keysof(out))

    def COPY(eng, out, in_):
        o, i = out.ap, in_.ap
        if eng == "act":
            P.add("act", lambda e: e.copy(out=o, in_=i), reads=keysof(in_), writes=keysof(out))
        else:
            P.add(eng, lambda e: e.tensor_copy(out=o, in_=i), reads=keysof(in_), writes=keysof(out))

    def MEMSET(eng, out, val):
        o = out.ap
        P.add(eng, lambda e: e.memset(o, val), writes=keysof(out))

    def RECIP(out, in_):
        o, i = out.ap, in_.ap
        P.add("dve", lambda e: e.reciprocal(out=o, in_=i), reads=keysof(in_), writes=keysof(out))

    def MM(out, pairs, transpose=False):
        o = out.ap
        pl = [(a.ap, b.ap) for a, b in pairs]
        n = len(pl)

        def fn(e):
            inst = None
            for i, (a, b) in enumerate(pl):
                inst = e.matmul(o, a, b, start=(i == 0), stop=(i == n - 1))
            return inst
        rk = []
        for a, b in pairs:
            rk += [a.keys, b.keys]
        P.add("pe", fn, reads=rk, writes=keysof(out))

    def MM1(out, lhsT, rhs, start, stop):
        o, a, b = out.ap, lhsT.ap, rhs.ap
        P.add("pe", lambda e: e.matmul(o, a, b, start=start, stop=stop),
              reads=[lhsT.keys, rhs.keys], writes=keysof(out))

    fresh = set()

    def MMh(out, pairs, fkey):
        if fkey in fresh:
            fresh.discard(fkey)
            n_ = len(pairs)
            for i_, (a_, b_) in enumerate(pairs):
                MM1(out, a_, b_, start=(i_ == 0), stop=(i_ == n_ - 1))
        else:
            MM(out, pairs)

    def TR(out, in_, ident_v):
        o, i, idn = out.ap, in_.ap, ident_v.ap
        P.add("pe", lambda e: e.transpose(o, i, idn), reads=keysof(in_, ident_v), writes=keysof(out))

    outkeys = []
    nout = {"i": 0}

    def DMA(eng, slot, pairs, reads=(), writes=(), is_out=False):
        pl = list(pairs)

        def fn(e, sem):
            for d, s in pl:
                e.dma_start(out=d, in_=s).then_inc(sem, 16)
        w = list(writes)
        if is_out:
            k = ("out", nout["i"])
            nout["i"] += 1
            outkeys.append(k)
            w.append(k)
        P.add(eng, fn, reads=list(reads), writes=w, dma=slot, ninc=len(pl))

    ident = alloc((128,), F32)
    identb = alloc((128,), BF16)
    onesM = alloc((128,), BF16)
    onesA = alloc((128,), BF16)
    colT = alloc((NRT * 128,), F32)
    mods = alloc((DEPTH * 48, 1 + NS), F32)
    cT = alloc((KC, 1 + NS), BF16)
    cwb = alloc((NE * 4, 31), BF16)
    wsT = [alloc((8, 128), BF16) for _ in range(NO)]
    wsTs = [alloc((8, 128), BF16) for _ in range(NO)]
    carry_a = [alloc((4, 30), BF16) for _ in range(NE)]
    carry_b = [alloc((4, 2), F32) for _ in range(NE)]
    xT = alloc((KC, TB), F32)
    h = alloc((KC, TB), BF16)
    mo = alloc((KC, TB), BF16)
    NB = 4
    ring8 = []
    ring32 = []
    for i in range(NB):
        b8 = alloc((8, 512), BF16)
        ring8.append(b8)
        ring32.append(Buf(arena, b8.off, (32, 128), BF16))

    sqs = [alloc((512,), BF16) for _ in range(6)]
    sqc = {"i": 0}
    pending_sq = []

    def sq_accum(sg, k):
        o, n = sg["off"], sg["n"]
        slot = sqs[sqc["i"] % 6]
        sqc["i"] += 1
        ACT(slot[:, 0:n], xT[:, k, o:o + n], AF.Square)
        pending_sq.append((PSN[sg["kind"]][:, 0:n], slot[:, 0:n], k))

    def flush_sq(keep=0):
        while len(pending_sq) > keep:
            pv, sv, k = pending_sq.pop(0)
            MM1(pv, onesM[:, :], sv, start=(k == 0), stop=(k == KC - 1))

    dummy = alloc((8,), F32)
    zpad = alloc((512,), BF16)
    MEMSET("pool", zpad[:, :], 0.0)

    def warm(nmm):
        o, a, b = PSW[:, :].ap, onesM[:, :].ap, zpad[:, :].ap

        def fn(e):
            inst = None
            for _ in range(nmm):
                inst = e.matmul(o, a, b, start=True, stop=True)
            return inst
        P.add("pe", fn, reads=[onesM[:, :].keys, zpad[:, :].keys], writes=[PSW[:, :].keys])

    def preload_ln():
        ACT(dummy[:, 0:1], ident[:, 0:1], AF.Ln, bias=1.0)

    OD_OFF = ARENA - 16 * 1024
    od_bvb = Buf(arena, OD_OFF, (D,), F32)
    od_lng = Buf(arena, OD_OFF + 4096, (D,), F32)
    od_lnb = Buf(arena, OD_OFF + 8192, (D,), F32)
    od_bsb = Buf(arena, OD_OFF + 12288, (8, 128), F32)

    def load_oddc(o_):
        prs = [(od_bvb[:, :].ap, b_in_c[o_:o_ + 1, 1024:2048].partition_broadcast(128)),
               (od_lng[:, :].ap, ln_v_g[o_:o_ + 1, :].partition_broadcast(128)),
               (od_lnb[:, :].ap, ln_v_b[o_:o_ + 1, :].partition_broadcast(128)),
               (od_bsb[:, :, :].r("p a b -> p (a b)").ap,
                b_s[o_:o_ + 1].rearrange("o h t -> o (h t)").partition_broadcast(128))]
        DMA("sp", "oddc", prs, writes=[od_bvb[:, :].keys, od_lng[:, :].keys, od_lnb[:, :].keys, od_bsb[:, :, :].keys])

    def col(r):
        return colT[:, r:r + 1]

    MEMSET("pool", ident[:, :], 1.0)
    iap = ident[:, :]
    P.add("pool", lambda e: e.affine_select(out=iap.ap, in_=iap.ap, pattern=[[-1, 128]],
                                            compare_op=ALU.is_equal, fill=0.0, base=0, channel_multiplier=1),
          reads=[iap.keys], writes=[iap.keys])
    COPY("pool", identb[:, :], ident[:, :])
    MEMSET("pool", onesM[:, :], 1.0 / 1024.0)
    MEMSET("pool", onesA[:, :], 1.0 / 512.0)
    for e_ in range(NE):
        MEMSET("pool", carry_a[e_][:, :, :], 0.0)
        MEMSET("pool", carry_b[e_][:, :, :], 0.0)

    m0 = mark()
    rows = alloc((NRT, 128), F32)
    c_sb = alloc((D,), F32)
    srcs = [
        (R_NG, norm_g.rearrange("l w (k p) -> (l w k) p", p=128)),
        (R_FG, final_g.rearrange("(k p) -> k p", p=128)),
        (R_BADA, b_ada.rearrange("l (c p) -> (l c) p", p=128)),
        (R_CAW, conv_a_w.rearrange("e t (j p) -> (e t j) p", p=128)),
        (R_CAB, conv_a_b.rearrange("e (j p) -> (e j) p", p=128)),
        (R_LAG, ln_a_g.rearrange("e (j p) -> (e j) p", p=128)),
        (R_LAB, ln_a_b.rearrange("e (j p) -> (e j) p", p=128)),
        (R_CBW, conv_b_w.rearrange("e t (j p) -> (e t j) p", p=128)),
        (R_BU, b_in_c.rearrange("o (k p) -> o k p", p=128)[:, 0:8, :].rearrange("o k p -> (o k) p")
         if False else None),
    ]
    pairs = []
    wk = []
    MEMSET("pool", rows[:, :, :], 0.0)
    for base, src in srcs:
        if src is None:
            continue
        n = src.shape[0]
        r = 0
        while r < n:
            g = base + r
            t, pp = g // 128, g % 128
            m = min(n - r, 128 - pp)
            dv = rows[pp:pp + m, t, :]
            pairs.append((dv.ap, src[r:r + m, :]))
            wk.append(dv.keys)
            r += m
    for o_ in range(NO):
        g = R_BU + o_ * 8
        t, pp = g // 128, g % 128
        dv = rows[pp:pp + 8, t, :]
        pairs.append((dv.ap, b_in_c[o_, 0:1024].rearrange("(k p) -> k p", p=128)))
        wk.append(dv.keys)
    DMA("sp", "rows", pairs, writes=wk)
    cv = c_sb[0:1 + NS, :]
    DMA("sp", "cload", [(cv.ap, c17[:, :])], writes=[cv.keys])
    for t in range(NRT):
        pb = psum()
        TR(pb[:, 0:128], rows[:, t, :], ident[:, :])
        COPY("dve", colT[:, t * 128:(t + 1) * 128], pb[:, 0:128])
    for e_ in range(NE):
        for j in range(4):
            a0 = R_CAW + e_ * 124 + j
            src = View(colT.ap[:, a0:a0 + 121:4], colT[:, a0:a0 + 121].keys)
            COPY("dve", cwb[:, e_ * 4 + j, :], src)
    ACT(cv, cv, AF.Silu)
    pb = psum()
    for k in range(KC):
        TR(pb[:, k * 17:(k + 1) * 17], c_sb[0:1 + NS, k * 128:(k + 1) * 128], ident[0:1 + NS, 0:1 + NS])
    COPY("dve", cT[:, :, :], pb[:, 0:KC * 17].r("p (a b) -> p a b", b=17))
    release(m0)

    m0 = mark()
    wsn = alloc((8, 128), F32)
    for o_ in range(NO):
        for samp in (False, True):
            wv = wsn[:, :, :]
            if not samp:
                DMA("sp", "wsn", [(wv.ap, w_s[o_].rearrange("h t s -> t h s"))], writes=[wv.keys])
            else:
                MEMSET("pool", wv, 0.0)
                prs = []
                wks = []
                for n_ in range(NS):
                    dv = wsn[8 * n_:8 * n_ + 8, :, 8 * n_:8 * n_ + 8]
                    prs.append((dv.ap, w_s[o_, :, 0:8, 0:8].rearrange("h t s -> t h s")))
                    wks.append(dv.keys)
                DMA("sp", "wsn", prs, writes=wks)
            P.add("pool", lambda e, a=wv.ap: e.affine_select(out=a, in_=a, pattern=[[0, 8], [-1, 128]],
                                                             compare_op=ALU.is_ge, fill=0.0, base=0,
                                                             channel_multiplier=1),
                  reads=[wv.keys], writes=[wv.keys])
            dst = wsTs[o_] if samp else wsT[o_]
            for hh in range(2):
                pb = psum()
                for q in range(4):
                    TR(pb[:, q * 128:(q + 1) * 128], wsn[:, hh * 4 + q, :], ident[:, :])
                COPY("dve", dst[:, hh * 4:hh * 4 + 4, :], pb[:, :].r("p (a b) -> p a b", b=128))
    release(m0)

    blocks = []
    for b in range(NPB):
        segs = []
        for s in range(NSEGP):
            segs.append(dict(kind="p", off=s * 512, n=512, tok0=b * PBT + s * 512,
                             last=(b == NPB - 1 and s == NSEGP - 1)))
        if b == 0:
            segs.append(dict(kind="s", off=PBT, n=NS * DSQ, tok0=0, last=False))
        blocks.append(dict(b=b, segs=segs))

    tasks = []

    def wdma_cols(W2d, colchunks, nk):
        Wv = W2d.rearrange("(k p) n -> p k n", p=128)

        def f(slot):
            rb = ring8[slot] if nk == 8 else ring32[slot]
            prs = []
            wks = []
            i = 0
            while i < len(colchunks):
                j = i
                while j + 1 < len(colchunks) and colchunks[j + 1] == colchunks[j] + 1:
                    j += 1
                dv = rb[:, :, i * 128:(j + 1) * 128]
                prs.append((dv.ap, Wv[:, :, colchunks[i] * 128:(colchunks[j] + 1) * 128]))
                wks.append(dv.keys)
                i = j + 1
            DMA("pool", "ring%d" % slot, prs, writes=wks)
        return f

    def mods_idx(l, m, k):
        return (l * 6 + m) * 8 + k

    def mod_scalar(l, m, k):
        return mods[:, mods_idx(l, m, k), 0:1]

    def mod_bc(l, m, k):
        return mods[:, mods_idx(l, m, k), 1:1 + NS].un(2).bc([128, NS, DSQ])

    def ada_tasks(l):
        out = []
        for g in range(12):
            def run(slot, g=g):
                wt = ring8[slot]
                pb = psum()
                for oc in range(4):
                    MM(pb[:, oc * 17:(oc + 1) * 17],
                       [(wt[:, k, oc * 128:(oc + 1) * 128], cT[:, k, :]) for k in range(KC)])
                base = l * 48 + g * 4
                bb = colT[:, R_BADA + base:R_BADA + base + 4].un(2).bc([128, 4, 17])
                TT("dve", mods[:, base:base + 4, :], pb[:, 0:68].r("p (a b) -> p a b", b=17), bb, ALU.add)
                if g == 11:
                    for w_, m in ((0, 1), (1, 4)):
                        i0 = mods_idx(l, m, 0)
                        sc = mods[:, i0:i0 + 8, :]
                        TS("dve", sc, sc, 1.0, ALU.add)
                        gb = colT[:, R_NG + l * 16 + w_ * 8:R_NG + l * 16 + w_ * 8 + 8].un(2).bc([128, 8, 17])
                        TT("dve", sc, sc, gb, ALU.mult)
            out.append((wdma_cols(w_ada[l], [g * 4 + i for i in range(4)], 8), run))
        return out

    def norm_mod(blk, l, w_):
        m_sh, m_sc = (0, 1) if w_ == 0 else (3, 4)
        mk = mark()
        rs = alloc((TB,), F32)
        tmp = [alloc((512,), F32) for _ in range(3)]
        ti = 0
        flush_sq()
        warm(WARM_N)
        for sg in blk["segs"]:
            fresh.add(sg["off"])
        for sg in blk["segs"]:
            o, n = sg["off"], sg["n"]
            pb = PSN[sg["kind"]]
            ACT(rs[:, o:o + n], pb[:, 0:n], AF.Ln, bias=EPS)
            ACT(rs[:, o:o + n], rs[:, o:o + n], AF.Exp, scale=-0.5)
            for k in range(KC):
                t = tmp[ti % 3]
                ti += 1
                if sg["kind"] == "p":
                    STT(t[:, 0:n], xT[:, k, o:o + n], mod_scalar(l, m_sc, k), rs[:, o:o + n], ALU.mult, ALU.mult)
                    ACT(h[:, k, o:o + n], t[:, 0:n], AF.Identity, bias=mod_scalar(l, m_sh, k))
                else:
                    t3 = t[:, 0:n].r("p (a b) -> p a b", b=DSQ)
                    TT("dve", t[:, 0:n], xT[:, k, o:o + n], rs[:, o:o + n], ALU.mult)
                    TT("dve", t3, t3, mod_bc(l, m_sc, k), ALU.mult)
                    TT("dve", h[:, k, o:o + n].r("p (a b) -> p a b", b=DSQ), t3, mod_bc(l, m_sh, k), ALU.add)
        release(mk)

    def residual(sg, k, pbv, l, m_g, tmpb):
        o, n = sg["off"], sg["n"]
        if sg["kind"] == "p":
            STT(xT[:, k, o:o + n], pbv, mod_scalar(l, m_g, k), xT[:, k, o:o + n], ALU.mult, ALU.add)
        else:
            t3 = tmpb[:, 0:n].r("p (a b) -> p a b", b=DSQ)
            TT("dve", t3, pbv.r("p (a b) -> p a b", b=DSQ), mod_bc(l, m_g, k), ALU.mult)
            TT("dve", xT[:, k, o:o + n], xT[:, k, o:o + n], tmpb[:, 0:n], ALU.add)
        sq_accum(sg, k)

    def proj_run(blk, wt, nk, noc, rhs, evac, from_h=False):
        for oc in range(noc):
            for sg in blk["segs"]:
                pb = psum()
                n = sg["n"]
                prs_ = [(wt[:, k, oc * 128:(oc + 1) * 128], rhs(k, sg)) for k in range(nk)]
                if from_h:
                    MMh(pb[:, 0:n], prs_, sg["off"])
                else:
                    MM(pb[:, 0:n], prs_)
                flush_sq()
                evac(oc, sg, pb[:, 0:n])

    def even_mixer_tasks(blk, l, st):
        e_ = l // 2
        b = blk["b"]
        lastblk = (b == NPB - 1)
        T = []
        W = w_in_ab[e_]

        def pre(slot_unused=None):
            st["mk"] = mark()
            st["aext"] = alloc((4, 30 + PBT), BF16)
            st["asx"] = alloc((4, NS, 38), BF16)
            st["af32"] = alloc((4, 128), F32)
            st["sig"] = [alloc((512,), F32) for _ in range(2)]
            st["acc"] = alloc((4, TB), F32)
            st["accb"] = alloc((4, TB), BF16)
            st["sqb"] = alloc((4, TB), BF16)
            st["bxe"] = alloc((4, 2 + PBT), F32)
            st["bxs"] = alloc((4, NS, 10), F32)
            st["cb"] = alloc((4, TB), F32)
            st["dg"] = alloc((2, 31, 128), BF16)
            st["mean"] = alloc((TB,), F32)
            st["var"] = alloc((TB,), F32)
            st["rstd"] = alloc((TB,), F32)
            st["t1"] = [alloc((512,), F32) for _ in range(2)]
            st["sto"] = alloc((512,), F32)
            st["ctr"] = 0
            aext, bxe, asx, bxs = st["aext"], st["bxe"], st["asx"], st["bxs"]
            COPY("pool", aext[:, :, 0:30], carry_a[e_][:, :, :])
            COPY("pool", bxe[:, :, 0:2], carry_b[e_][:, :, :])
            if b == 0:
                hs = alloc((4, 512), F32)
                hb = alloc((512,), F32)
                prs, wks = [], []
                for t in range(4):
                    dv = hs[0:120, t, :]
                    prs.append((dv.ap, sa[e_, 4 * t:4 * t + 4].rearrange("n r c -> (n r) c")))
                    wks.append(dv.keys)
                dvb = hb[0:32, :]
                prs.append((dvb.ap, sb[e_].rearrange("n r c -> (n r) c")))
                wks.append(dvb.keys)
                DMA("sp", "hist", prs, writes=wks)
                for t in range(4):
                    pb = psum()
                    for j in range(4):
                        TR(pb[:, j * 120:(j + 1) * 120], hs[0:120, t, j * 128:(j + 1) * 128], ident[0:120, 0:120])
                    for j in range(4):
                        COPY("act", asx[:, j, 4 * t:4 * t + 4, 0:30],
                             pb[:, j * 120:(j + 1) * 120].r("p (a b) -> p a b", b=30))
                pb = psum()
                for j in range(4):
                    TR(pb[:, j * 32:(j + 1) * 32], hb[0:32, j * 128:(j + 1) * 128], ident[0:32, 0:32])
                for j in range(4):
                    COPY("act", bxs[:, j, :, 0:2], pb[:, j * 32:(j + 1) * 32].r("p (a b) -> p a b", b=2))
                prs2, rks = [], []
                for t in range(4):
                    for q in range(4):
                        sv2 = hs[30 * q + 8:30 * q + 30, t, :]
                        prs2.append((ca_s[e_, 4 * t + q, 0:22, :], sv2.ap))
                        rks.append(sv2.keys)
                DMA("sp", "histo", prs2, reads=rks, is_out=True)

        for g in range(2):
            def runA(slot, g=g):
                if g == 0:
                    pre()
                wt = ring8[slot]
                aext, asx, af32 = st["aext"], st["asx"], st["af32"]
                dg = st["dg"]
                for jj in range(2):
                    j = 2 * g + jj
                    for sg in blk["segs"]:
                        o, n = sg["off"], sg["n"]
                        pv, pg = psum(), psum()
                        MMh(pv[:, 0:n], [(wt[:, k, (2 * jj) * 128:(2 * jj + 1) * 128], h[:, k, o:o + n]) for k in range(KC)], o)
                        MM(pg[:, 0:n], [(wt[:, k, (2 * jj + 1) * 128:(2 * jj + 2) * 128], h[:, k, o:o + n]) for k in range(KC)])
                        sgb = st["sig"][st["ctr"] % 2]
                        st["ctr"] += 1
                        ACT(sgb[:, 0:n], pg[:, 0:n], AF.Sigmoid)
                        if sg["kind"] == "p":
                            TT("dve", aext[:, j, 30 + o:30 + o + n], pv[:, 0:n], sgb[:, 0:n], ALU.mult)
                            if sg["last"]:
                                TT("dve", af32[:, j, 0:30], pv[:, n - 30:n], sgb[:, n - 30:n], ALU.mult)
                        else:
                            TT("dve", asx[:, j, :, 30:38], pv[:, 0:n].r("p (a b) -> p a b", b=DSQ),
                               sgb[:, 0:n].r("p (a b) -> p a b", b=DSQ), ALU.mult)
                            TT("dve", af32[:, j, 0:n], pv[:, 0:n], sgb[:, 0:n], ALU.mult)
                    TT("dve", dg[:, jj, :, :], identb[:, :].un(1).bc([128, 31, 128]),
                       cwb[:, e_ * 4 + j, :].un(2).bc([128, 31, 128]), ALU.mult)

                    def conv(j=j, jj=jj):
                        cbias = col(R_CAB + e_ * 4 + j)
                        for sg in blk["segs"]:
                            o, n = sg["off"], sg["n"]
                            pb = psum()
                            if sg["kind"] == "p":
                                MM(pb[:, 0:n], [(dg[:, jj, tap, :], aext[:, j, o + tap:o + tap + n]) for tap in range(31)])
                            else:
                                MM(pb[:, 0:n], [(dg[:, jj, tap, :], asx[:, j, :, tap:tap + DSQ]) for tap in range(31)])
                            TS("dve", st["acc"][:, j, o:o + n], pb[:, 0:n], cbias, ALU.add)
                            COPY("act", st["accb"][:, j, o:o + n], st["acc"][:, j, o:o + n])
                            ACT(st["sqb"][:, j, o:o + n], st["acc"][:, j, o:o + n], AF.Square)
                    if st.get("pend") is not None:
                        st["pend"]()
                    st["pend"] = conv
                if g == 1:
                    st["pend"]()
                    st["pend"] = None
                if g == 1:
                    for sg in blk["segs"]:
                        o, n = sg["off"], sg["n"]
                        pm, pe2 = psum(), psum()
                        MM(pm[:, 0:n], [(onesA[:, :], st["accb"][:, j2, o:o + n]) for j2 in range(4)])
                        MM(pe2[:, 0:n], [(onesA[:, :], st["sqb"][:, j2, o:o + n]) for j2 in range(4)])
                        mean, var, rstd = st["mean"][:, o:o + n], st["var"][:, o:o + n], st["rstd"][:, o:o + n]
                        COPY("act", mean, pm[:, 0:n])
                        STT(var, mean, -1.0, mean, ALU.mult, ALU.mult)
                        TT("dve", var, pe2[:, 0:n], var, ALU.add)
                        ACT(rstd, var, AF.Ln, bias=EPS)
                        ACT(rstd, rstd, AF.Exp, scale=-0.5)
                        for j2 in range(4):
                            t1 = st["t1"][j2 % 2]
                            TT("dve", t1[:, 0:n], st["acc"][:, j2, o:o + n], mean, ALU.subtract)
                            TT("dve", t1[:, 0:n], t1[:, 0:n], rstd, ALU.mult)
                            ACT(mo[:, j2, o:o + n], t1[:, 0:n], AF.Silu,
                                bias=col(R_LAB + e_ * 4 + j2), scale=col(R_LAG + e_ * 4 + j2))
                    preload_ln()
                    COPY("pool", carry_a[e_][:, :, :], aext[:, :, PBT:PBT + 30])
                    if lastblk:
                        pb = psum()
                        for j2 in range(4):
                            TR(pb[0:30, j2 * 128:(j2 + 1) * 128], af32[:, j2, 0:30], ident[:, :])
                        sv = st["sto"][0:30, :]
                        COPY("dve", sv, pb[0:30, :])
                        DMA("sp", "sto", [(ca_p[e_], sv.ap)], reads=[sv.keys], is_out=True)
                    if b == 0:
                        pb = psum()
                        for j2 in range(4):
                            TR(pb[:, j2 * 128:(j2 + 1) * 128], af32[:, j2, 0:128], ident[:, :])
                        sv = st["sto"][:, :]
                        COPY("dve", sv, pb[:, :])
                        DMA("sp", "sto", [(ca_s[e_, n_, 22:30, :], st["sto"][8 * n_:8 * n_ + 8, :].ap) for n_ in range(NS)],
                            reads=[sv.keys], is_out=True)
            T.append((wdma_cols(W, [2 * g, 4 + 2 * g, 2 * g + 1, 4 + 2 * g + 1], 8), runA))
        for g in range(2):
            def runB(slot, g=g):
                wt = ring8[slot]
                bxe, bxs, cb = st["bxe"], st["bxs"], st["cb"]
                for jj in range(2):
                    j = 2 * g + jj
                    for sg in blk["segs"]:
                        o, n = sg["off"], sg["n"]
                        px, pc = psum(), psum()
                        MM(px[:, 0:n], [(wt[:, k, (2 * jj) * 128:(2 * jj + 1) * 128], h[:, k, o:o + n]) for k in range(KC)])
                        MM(pc[:, 0:n], [(wt[:, k, (2 * jj + 1) * 128:(2 * jj + 2) * 128], h[:, k, o:o + n]) for k in range(KC)])
                        sgb = st["sig"][st["ctr"] % 2]
                        st["ctr"] += 1
                        COPY("act", sgb[:, 0:n], px[:, 0:n])
                        w0 = col(R_CBW + e_ * 12 + 0 * 4 + j)
                        w1 = col(R_CBW + e_ * 12 + 1 * 4 + j)
                        w2 = col(R_CBW + e_ * 12 + 2 * 4 + j)
                        if sg["kind"] == "p":
                            TT("dve", bxe[:, j, 2 + o:2 + o + n], pc[:, 0:n], sgb[:, 0:n], ALU.mult)
                            c_ = cb[:, j, o:o + n]
                            TS("dve", c_, bxe[:, j, o:o + n], w0, ALU.mult)
                            STT(c_, bxe[:, j, o + 1:o + 1 + n], w1, c_, ALU.mult, ALU.add)
                            STT(c_, bxe[:, j, o + 2:o + 2 + n], w2, c_, ALU.mult, ALU.add)
                        else:
                            TT("dve", bxs[:, j, :, 2:10], pc[:, 0:n].r("p (a b) -> p a b", b=DSQ),
                               sgb[:, 0:n].r("p (a b) -> p a b", b=DSQ), ALU.mult)
                            c_ = cb[:, j, o:o + n].r("p (a b) -> p a b", b=DSQ)
                            TS("dve", c_, bxs[:, j, :, 0:8], w0, ALU.mult)
                            STT(c_, bxs[:, j, :, 1:9], w1, c_, ALU.mult, ALU.add)
                            STT(c_, bxs[:, j, :, 2:10], w2, c_, ALU.mult, ALU.add)
                if g == 1:
                    COPY("pool", carry_b[e_][:, :, :], bxe[:, :, PBT:PBT + 2])
                    if lastblk:
                        t1 = st["t1"][0]
                        COPY("dve", t1[:, 0:8].r("p (a b) -> p a b", b=2), bxe[:, :, PBT:PBT + 2])
                        pb = psum()
                        for j2 in range(4):
                            TR(pb[0:2, j2 * 128:(j2 + 1) * 128], t1[:, 2 * j2:2 * j2 + 2], ident[:, :])
                        sv = st["sto"][0:2, :]
                        COPY("dve", sv, pb[0:2, :])
                        DMA("sp", "sto", [(cb_p[e_], sv.ap)], reads=[sv.keys], is_out=True)
                    if b == 0:
                        t1 = st["t1"][1]
                        pb = psum()
                        for j2 in range(4):
                            COPY("dve", t1[:, 32 * j2:32 * j2 + 32].r("p (a b) -> p a b", b=2), bxs[:, j2, :, 8:10])
                            TR(pb[0:32, j2 * 128:(j2 + 1) * 128], t1[:, 32 * j2:32 * j2 + 32], ident[:, :])
                        sv = st["sto"][0:32, :]
                        COPY("dve", sv, pb[0:32, :])
                        DMA("sp", "sto", [(cb_s[e_].rearrange("n r c -> (n r) c"), sv.ap)], reads=[sv.keys], is_out=True)
            T.append((wdma_cols(W, [8 + 2 * g, 16 + 2 * g, 8 + 2 * g + 1, 16 + 2 * g + 1], 8), runB))

        def runBB(slot):
            wt = ring8[slot]

            def ev(oc, sg, pv):
                o, n = sg["off"], sg["n"]
                TT("dve", mo[:, 4 + oc, o:o + n], pv, st["cb"][:, oc, o:o + n], ALU.mult)
            proj_run(blk, wt, KC, 4, lambda k, sg: h[:, k, sg["off"]:sg["off"] + sg["n"]], ev)
        T.append((wdma_cols(W, [12, 13, 14, 15], 8), runBB))
        for g in range(2):
            def runO(slot, g=g):
                wt = ring8[slot]
                tmpb = st["t1"][0]

                def ev(oc, sg, pv):
                    residual(sg, g * 4 + oc, pv, l, 2, tmpb)
                proj_run(blk, wt, KC, 4, lambda k, sg: mo[:, k, sg["off"]:sg["off"] + sg["n"]], ev)
                if g == 1:
                    release(st["mk"])
            T.append((wdma_cols(w_out_ab[e_], [4 * g + i for i in range(4)], 8), runO))
        return T

    def odd_mixer_tasks(blk, l, st):
        o_ = l // 2
        b = blk["b"]
        lastblk = (b == NPB - 1)
        T = []
        W = w_in_c[o_]
        ntile = sum(sg["n"] for sg in blk["segs"]) // 128

        def pre():
            st["mk"] = mark()
            st["bvb"], st["lng"], st["lnb"], st["bsb"] = od_bvb, od_lng, od_lnb, od_bsb
            assert astate["top"] + 70 * 1024 < OD_OFF
            st["vraw"] = [alloc((D,), F32) for _ in range(ntile)]
            st["vnb"] = alloc((ntile, D), BF16)
            st["stat"] = alloc((ntile, 16), F32)
            st["ubuf"] = alloc((8, TB), F32)
            st["tmp"] = [alloc((512,), F32) for _ in range(2)]
            st["ctr"] = 0

        def tile_info(ti):
            c = 0
            for sg in blk["segs"]:
                if ti * 128 < c + sg["n"]:
                    return sg, ti * 128
                c += sg["n"]
            raise ValueError

        for g in range(2):
            def runV(slot, g=g):
                if g == 0:
                    pre()
                wt = ring8[slot]
                for ti in range(ntile):
                    pb = psum()
                    o = ti * 128
                    MMh(pb[:, :], [(h[:, k, o:o + 128], wt[:, k, :]) for k in range(KC)], o)
                    vr = st["vraw"][ti]
                    TT("dve", vr[:, g * 512:(g + 1) * 512], pb[:, :], st["bvb"][:, g * 512:(g + 1) * 512], ALU.add)
                    ACT(vr[:, g * 512:(g + 1) * 512], vr[:, g * 512:(g + 1) * 512], AF.Gelu_apprx_tanh)
                if g == 1:
                    sta = st["stat"]
                    for ti in range(ntile):
                        vr = st["vraw"][ti]
                        for hh in range(2):
                            sv_, vv_ = sta[:, ti, hh * 6:hh * 6 + 6], vr[:, hh * 512:(hh + 1) * 512]
                            P.add("dve", lambda e, a=sv_.ap, c=vv_.ap: e.bn_stats(out=a, in_=c),
                                  reads=[vv_.keys], writes=[sv_.keys])
                        mv = sta[:, ti, 12:14]
                        s12 = sta[:, ti, 0:12]
                        P.add("dve", lambda e, a=mv.ap, c=s12.ap: e.bn_aggr(out=a, in_=c),
                              reads=[s12.keys], writes=[mv.keys])
                    ACT(sta[:, :, 14:15], sta[:, :, 13:14], AF.Sqrt, bias=EPS)
                    RECIP(sta[:, :, 14:15], sta[:, :, 14:15])
                    for ti in range(ntile):
                        vr = st["vraw"][ti]
                        STT(vr[:, :], vr[:, :], sta[:, ti, 12:13], st["lng"][:, :], ALU.subtract, ALU.mult)
                        sg, _ = tile_info(ti)
                        is_out_tile = (sg["kind"] == "s") or (lastblk and ti == ntile - 1)
                        if not is_out_tile:
                            STT(st["vnb"][:, ti, :], vr[:, :], sta[:, ti, 14:15], st["lnb"][:, :], ALU.mult, ALU.add)
                        else:
                            STT(vr[:, :], vr[:, :], sta[:, ti, 14:15], st["lnb"][:, :], ALU.mult, ALU.add)
                            COPY("dve", st["vnb"][:, ti, :], vr[:, :])
                            if sg["kind"] == "s":
                                DMA("sp", "cvs", [(cv_s[o_], vr[:, :].ap)], reads=[vr[:, :].keys], is_out=True)
                            else:
                                DMA("sp", "cvp", [(cv_p[o_], vr[:, :].ap)], reads=[vr[:, :].keys], is_out=True)
            T.append((wdma_cols(W, [8 + 4 * g + i for i in range(4)], 8), runV))
        for g in range(2):
            def runU(slot, g=g):
                wt = ring8[slot]
                for oc in range(4):
                    j = 4 * g + oc
                    for sg in blk["segs"]:
                        o, n = sg["off"], sg["n"]
                        pu = psum()
                        MM(pu[:, 0:n], [(wt[:, k, oc * 128:(oc + 1) * 128], h[:, k, o:o + n]) for k in range(KC)])
                        ACT(st["ubuf"][:, j, o:o + n], pu[:, 0:n], AF.Gelu_apprx_tanh, bias=col(R_BU + o_ * 8 + j))
                    if g == 1:
                        gate_head(oc)
                        if oc > 0:
                            gate_head(4 + oc - 1)
                if g == 1:
                    gate_head(7)
                    preload_ln()
            T.append((wdma_cols(W, [4 * g + i for i in range(4)], 8), runU))

        def gate_head(j):
            for sg in blk["segs"]:
                o, n = sg["off"], sg["n"]
                pg = psum()
                for cc in range(n // 128):
                    ti = (o + cc * 128) // 128
                    rhs = wsTs[o_][:, j, :] if sg["kind"] == "s" else wsT[o_][:, j, :]
                    MM(pg[:, cc * 128:(cc + 1) * 128], [(st["vnb"][:, ti, j * 128:(j + 1) * 128], rhs)])
                tb = st["tmp"][st["ctr"] % 2]
                st["ctr"] += 1
                if sg["kind"] == "p":
                    nb4 = n // 128
                    bsv = st["bsb"][:, j, :].un(1).bc([128, nb4, 128])
                    TT("dve", tb[:, 0:n].r("p (a b) -> p a b", b=128), pg[:, 0:n].r("p (a b) -> p a b", b=128),
                       bsv, ALU.add)
                else:
                    bsv = st["bsb"][:, j, 0:DSQ].un(1).bc([128, NS, DSQ])
                    TT("dve", tb[:, 0:n].r("p (a b) -> p a b", b=DSQ), pg[:, 0:n].r("p (a b) -> p a b", b=DSQ),
                       bsv, ALU.add)
                TT("dve", mo[:, j, o:o + n], tb[:, 0:n], st["ubuf"][:, j, o:o + n], ALU.mult)
        for g in range(2):
            def runO(slot, g=g):
                wt = ring8[slot]
                tmpb = st["tmp"][0]

                def ev(oc, sg, pv):
                    residual(sg, g * 4 + oc, pv, l, 2, tmpb)
                proj_run(blk, wt, KC, 4, lambda k, sg: mo[:, k, sg["off"]:sg["off"] + sg["n"]], ev)
                if g == 1:
                    release(st["mk"])
            T.append((wdma_cols(w_out_c[o_], [4 * g + i for i in range(4)], 8), runO))
        return T

    def ffn_tasks(blk, l, st):
        T = []

        def pre():
            st["mk"] = mark()
            st["f"] = alloc((32, TB), BF16)
            st["rl"] = [alloc((512,), F32) for _ in range(3)]
            st["ctr"] = 0
        for g in range(8):
            def run1(slot, g=g):
                if g == 0:
                    pre()
                    if (l + 1) % 2 == 1 and l + 1 < nlayers:
                        load_oddc((l + 1) // 2)
                wt = ring8[slot]

                def ev(oc, sg, pv):
                    o, n = sg["off"], sg["n"]
                    r = st["rl"][st["ctr"] % 3]
                    st["ctr"] += 1
                    ACT(r[:, 0:n], pv, AF.Relu)
                    TT("dve", st["f"][:, g * 4 + oc, o:o + n], r[:, 0:n], r[:, 0:n], ALU.mult)
                proj_run(blk, wt, KC, 4, lambda k, sg: h[:, k, sg["off"]:sg["off"] + sg["n"]], ev, from_h=True)
                if g == 7:
                    preload_ln()
            T.append((wdma_cols(w_ff1[l], [4 * g + i for i in range(4)], 8), run1))
        for g in range(8):
            def run2(slot, g=g):
                wt = ring32[slot]
                tmpb = st["rl"][0]

                def ev(oc, sg, pv):
                    residual(sg, g, pv, l, 5, tmpb)
                proj_run(blk, wt, 32, 1, lambda k, sg: st["f"][:, k, sg["off"]:sg["off"] + sg["n"]], ev)
                if g == 7:
                    release(st["mk"])
            T.append((wdma_cols(w_ff2[l], [g], 32), run2))
        return T

    def load_block(blk):
        mk = mark()
        stg = [alloc((D,), F32) for _ in range(2)]
        i = 0
        for sg in blk["segs"]:
            for tt in range(sg["n"] // 128):
                s = stg[i % 2]
                sv = s[:, :]
                if sg["kind"] == "p":
                    src = x_p[sg["tok0"] + tt * 128:sg["tok0"] + (tt + 1) * 128, :]
                else:
                    src = x_s[:, :]
                DMA("sp", "xs%d" % (i % 2), [(sv.ap, src)], writes=[sv.keys])
                o = sg["off"] + tt * 128
                for hh in range(2):
                    pb = psum()
                    for q in range(4):
                        k = hh * 4 + q
                        TR(pb[:, q * 128:(q + 1) * 128], s[:, k * 128:(k + 1) * 128], ident[:, :])
                    COPY("act", xT[:, hh * 4:hh * 4 + 4, o:o + 128], pb[:, :].r("p (a b) -> p a b", b=128))
                i += 1
        for sg in blk["segs"]:
            for k in range(KC):
                sq_accum(sg, k)
                flush_sq(keep=2)
        release(mk)

    def store_block(blk):
        mk = mark()
        stg = [alloc((D,), F32) for _ in range(2)]
        rs = alloc((TB,), F32)
        i = 0
        flush_sq()
        for sg in blk["segs"]:
            o, n = sg["off"], sg["n"]
            if final_norm:
                pb = PSN[sg["kind"]]
                ACT(rs[:, o:o + n], pb[:, 0:n], AF.Ln, bias=EPS)
                ACT(rs[:, o:o + n], rs[:, o:o + n], AF.Exp, scale=-0.5)
                for k in range(KC):
                    STT(xT[:, k, o:o + n], xT[:, k, o:o + n], col(R_FG + k), rs[:, o:o + n], ALU.mult, ALU.mult)
            for tt in range(n // 128):
                s = stg[i % 2]
                oo = o + tt * 128
                for hh in range(2):
                    pb = psum()
                    for q in range(4):
                        k = hh * 4 + q
                        TR(pb[:, q * 128:(q + 1) * 128], xT[:, k, oo:oo + 128], ident[:, :])
                    COPY("act" if hh == 0 else "dve", s[:, hh * 512:(hh + 1) * 512], pb[:, :])
                if sg["kind"] == "p":
                    dst = y_p[sg["tok0"] + tt * 128:sg["tok0"] + (tt + 1) * 128, :]
                else:
                    dst = y_s[:, :]
                DMA("sp", "ys%d" % (i % 2), [(dst, s[:, :].ap)], reads=[s[:, :].keys], is_out=True)
                i += 1
        release(mk)

    def T_plain(fn):
        return (None, lambda slot: fn())

    if nlayers > 0:
        tasks += ada_tasks(0)
    for blk in blocks:
        tasks.append(T_plain(lambda blk=blk: load_block(blk)))
        for l in range(nlayers):
            st = {}
            tasks.append(T_plain(lambda blk=blk, l=l: norm_mod(blk, l, 0)))
            if l % 2 == 0:
                tasks += even_mixer_tasks(blk, l, st)
            else:
                tasks += odd_mixer_tasks(blk, l, st)
            tasks.append(T_plain(lambda blk=blk, l=l: norm_mod(blk, l, 1)))
            st2 = {}
            ft = ffn_tasks(blk, l, st2)
            if blk["b"] == 0 and l + 1 < nlayers:
                at = ada_tasks(l + 1)
                merged = []
                for i_, t_ in enumerate(ft):
                    merged.append(t_)
                    if i_ < len(at):
                        merged.append(at[i_])
                ft = merged
            tasks += ft
        tasks.append(T_plain(lambda blk=blk: store_block(blk)))

    if max_tasks is not None:
        tasks = tasks[:max_tasks]
    widx = [i for i, t in enumerate(tasks) if t[0] is not None]
    slot_of = {ti: n % NB for n, ti in enumerate(widx)}
    issued = 0
    LOOK = NB - 1
    for n, ti in enumerate(widx[:LOOK]):
        tasks[ti][0](slot_of[ti])
        issued += 1
    wpos = 0
    for i, (dfn, run) in enumerate(tasks):
        if dfn is not None:
            if issued < len(widx):
                tj = widx[issued]
                tasks[tj][0](slot_of[tj])
                issued += 1
            run(slot_of[i])
            wpos += 1
        else:
            run(None)

    P.add("sp", None, reads=outkeys)
    P.emit(nc, block, stack)
    stack.close()
    return nc


_CACHE = {}


def _get_prog(**kw):
    key = tuple(sorted(kw.items()))
    if key not in _CACHE:
        _CACHE[key] = build_program(**kw)
    return _CACHE[key]


def make_in_maps(inputs):
    f = lambda a: np.ascontiguousarray(np.asarray(a, dtype=np.float32))
    shared = {k: f(inputs[k]) for k in
              ["w_in_ab", "conv_a_w", "conv_a_b", "ln_a_g", "ln_a_b", "conv_b_w", "w_out_ab", "w_in_c", "b_in_c",
               "ln_v_g", "ln_v_b", "w_s", "b_s", "w_out_c", "w_ada", "b_ada", "norm_g", "w_ff1", "w_ff2", "final_g"]}
    xp, xs = f(inputs["x_prompt"]), f(inputs["x_sample"])
    sa, sb = f(inputs["state_conv_a"]), f(inputs["state_conv_b"])
    cp, cs = f(inputs["c_prompt"]), f(inputs["c_sample"])
    maps = []
    for c in range(NCORES):
        m = dict(shared)
        m["x_p"] = xp[c]
        m["x_s"] = np.ascontiguousarray(xs[NS * c:NS * (c + 1)].reshape(NS * DSQ, D))
        m["sa"] = np.ascontiguousarray(sa[:, NS * c:NS * (c + 1)])
        m["sb"] = np.ascontiguousarray(sb[:, NS * c:NS * (c + 1)])
        m["c17"] = np.ascontiguousarray(np.concatenate([cp[c:c + 1], cs[NS * c:NS * (c + 1)]], axis=0))
        maps.append(m)
    return maps


def gather(results):
    r = results
    y_prompt = np.stack([r[c]["y_p"] for c in range(NCORES)], axis=0)
    y_sample = np.concatenate([r[c]["y_s"].reshape(NS, DSQ, D) for c in range(NCORES)], axis=0)
    ca_p = np.stack([r[c]["ca_p"] for c in range(NCORES)], axis=1)
    ca_s = np.concatenate([r[c]["ca_s"] for c in range(NCORES)], axis=1)
    cb_p = np.stack([r[c]["cb_p"] for c in range(NCORES)], axis=1)
    cb_s = np.concatenate([r[c]["cb_s"] for c in range(NCORES)], axis=1)
    cv_p = np.stack([r[c]["cv_p"] for c in range(NCORES)], axis=1)
    cv_s = np.concatenate([r[c]["cv_s"].reshape(2, NS, DSQ, D) for c in range(NCORES)], axis=1)
    return tuple(np.ascontiguousarray(a, dtype=np.float32) for a in
                 (y_prompt, y_sample, ca_p, ca_s, cb_p, cb_s, cv_p, cv_s))


def kernel(**inputs):
    nc = _get_prog()
    res = run_bass_kernel_spmd(nc, make_in_maps(inputs), core_ids=list(range(NCORES)))
    return gather(res.results)
```
